# Optimizing a Trainium2 kernel written in Bass

```python
import jax
import jax.numpy as jnp
from jax import lax
import numpy as np

D_MODEL = 1024
BATCH = 2
SEQ = 8192
DEPTH = 2
DEC_BATCH = 128
DEC_SEQ = 1
PAST_LEN = 8192
PAGE_SIZE = 128

N_EVEN = (DEPTH + 1) // 2
N_ODD = DEPTH // 2
A_HEADS = 8
A_KV_HEADS = 2
A_HEAD_DIM = 64
A_GROUP = A_HEADS // A_KV_HEADS
A_Q = A_HEADS * A_HEAD_DIM
A_KV = A_KV_HEADS * A_HEAD_DIM
WINDOW = 128
A_BLOCK = 128
B_HEADS = 4
B_KEY_DIM = 128
B_VAL_DIM = 128
B_FDIM = B_HEADS * B_KEY_DIM
B_WIDTH = B_HEADS * B_VAL_DIM
B_CHUNK = 64
EVEN_IN = A_Q + 2 * A_KV + 2 * B_FDIM + 2 * B_WIDTH
EVEN_OUT = A_Q + B_WIDTH
C_HEADS = 8
C_KEY_DIM = 128
C_VAL_DIM = 128
C_QK = C_HEADS * C_KEY_DIM
C_V = C_HEADS * C_VAL_DIM
C_CONV = 4
C_CONV_DIM = 2 * C_QK + C_V
C_CHUNK = 64
ODD_IN = C_CONV_DIM + C_V + 2 * C_HEADS
D_FF = 2816
N_MOD = 9
DN_ALPHA = (2 * DEPTH) ** 0.25
DN_BETA = (8 * DEPTH) ** -0.25
LN_EPS = 1e-5
NORM_EPS = 1e-6

kernel_name = 'hybrid_swa_hgrn2_gdn_step'


def layer_norm(x, g, b):
    xf = x.astype(jnp.float32)
    mu = jnp.mean(xf, axis=-1, keepdims=True)
    var = jnp.mean(jnp.square(xf - mu), axis=-1, keepdims=True)
    y = (xf - mu) * lax.rsqrt(var + LN_EPS)
    return (y * g.astype(jnp.float32) + b.astype(jnp.float32)).astype(x.dtype)


def rms_norm(x, g):
    xf = x.astype(jnp.float32)
    y = xf * lax.rsqrt(jnp.mean(jnp.square(xf), axis=-1, keepdims=True) + NORM_EPS)
    return y * g.astype(jnp.float32)


def l2_normalize(x):
    xf = x.astype(jnp.float32)
    return xf * lax.rsqrt(jnp.sum(jnp.square(xf), axis=-1, keepdims=True) + NORM_EPS)


def pad_time(t, n):
    return jnp.pad(t, [(0, 0), (0, n)] + [(0, 0)] * (t.ndim - 2))


def alibi_slopes():
    return jnp.asarray(2.0 ** (-8.0 * np.arange(1, A_HEADS + 1) / A_HEADS), dtype=jnp.float32)


def swiglu(h, w_up, w_down):
    gate, up = jnp.split(h @ w_up, 2, axis=-1)
    return (jax.nn.silu(gate) * up) @ w_down


def sink_window_attention(q, k, v, q_pos, k_pos, sinks):
    s = jnp.einsum('...qhgd,...khd->...hgqk', q, k).astype(jnp.float32) * (A_HEAD_DIM ** -0.5)
    dist = q_pos[..., :, None] - k_pos[..., None, :]
    valid = (dist >= 0) & (dist < WINDOW) & (k_pos[..., None, :] >= 0)
    dist = dist[..., None, None, :, :].astype(jnp.float32)
    valid = valid[..., None, None, :, :]
    slopes = alibi_slopes().reshape(A_KV_HEADS, A_GROUP, 1, 1)
    s = jnp.where(valid, s - slopes * dist, -jnp.inf)
    sink = sinks.astype(jnp.float32).reshape(A_KV_HEADS, A_GROUP, 1, 1)
    m = jnp.maximum(jnp.max(s, axis=-1, keepdims=True), sink)
    p = jnp.exp(s - m)
    p = p / (jnp.sum(p, axis=-1, keepdims=True) + jnp.exp(sink - m))
    return jnp.einsum('...hgqk,...khd->...qhgd', p.astype(v.dtype), v)


def swa_prompt(q, k, v, sinks):
    Bn, S_ = q.shape[:2]
    nb = S_ // A_BLOCK
    qb = q.reshape(Bn, nb, A_BLOCK, A_KV_HEADS, A_GROUP, A_HEAD_DIM)
    kp = jnp.pad(k, ((0, 0), (A_BLOCK, 0), (0, 0), (0, 0))).reshape(Bn, nb + 1, A_BLOCK, A_KV_HEADS, A_HEAD_DIM)
    vp = jnp.pad(v, ((0, 0), (A_BLOCK, 0), (0, 0), (0, 0))).reshape(Bn, nb + 1, A_BLOCK, A_KV_HEADS, A_HEAD_DIM)
    kb = jnp.concatenate([kp[:, :-1], kp[:, 1:]], axis=2)
    vb = jnp.concatenate([vp[:, :-1], vp[:, 1:]], axis=2)
    pos = jnp.arange(S_).reshape(nb, A_BLOCK)
    k_pos = jnp.concatenate([pos - A_BLOCK, pos], axis=1)
    o = sink_window_attention(qb, kb, vb, pos, k_pos, sinks)
    return o.reshape(Bn, S_, A_Q)


def hgrn2_scan(q, logf, k, v, S0):
    Bn, T, H, DK = q.shape
    DV = v.shape[-1]
    C = min(B_CHUNK, T)
    nc = -(-T // C)
    pad = nc * C - T
    q, logf, k, v = (pad_time(t, pad) for t in (q, logf, k, v))

    def blocks(t):
        return jnp.moveaxis(t.reshape((Bn, nc, C) + t.shape[2:]), 1, 0)

    causal = jnp.tril(jnp.ones((C, C), dtype=bool))[None, :, :, None, None]

    def step(S, xs):
        q_c, lf_c, k_c, v_c = xs
        A = jnp.cumsum(lf_c, axis=1)
        decay = jnp.exp(jnp.where(causal, A[:, :, None] - A[:, None, :], -jnp.inf))
        scores = jnp.einsum('bthk,bshk,btshk->bhts', q_c, k_c, decay)
        o = jnp.einsum('bhts,bshv->bthv', scores, v_c) + jnp.einsum('bthk,bhkv->bthv', q_c * jnp.exp(A), S)
        A_last = A[:, -1]
        S = jnp.exp(A_last)[..., None] * S + jnp.einsum('bshk,bshv->bhkv', k_c * jnp.exp(A_last[:, None] - A), v_c)
        return S, o

    S, o = lax.scan(step, S0, tuple(blocks(t) for t in (q, logf, k, v)))
    o = jnp.moveaxis(o, 0, 1).reshape(Bn, nc * C, H, DV)[:, :T]
    return o, S


def gated_delta_scan(q, k, v, log_alpha, beta, S0):
    Bn, T, H, DK = q.shape
    DV = v.shape[-1]
    C = min(C_CHUNK, T)
    nc = -(-T // C)
    pad = nc * C - T
    q, k, v, log_alpha, beta = (pad_time(t, pad) for t in (q, k, v, log_alpha, beta))

    def blocks(t):
        return jnp.moveaxis(t.reshape((Bn, nc, C, H) + t.shape[3:]), 3, 2)

    q, k, v, log_alpha, beta = (blocks(t) for t in (q, k, v, log_alpha, beta))
    G = jnp.cumsum(log_alpha, axis=-1)
    diff = G[..., :, None] - G[..., None, :]
    causal = jnp.tril(jnp.ones((C, C), dtype=bool))
    strict = jnp.tril(jnp.ones((C, C), dtype=bool), k=-1)
    decay_incl = jnp.exp(jnp.where(causal, diff, -jnp.inf))
    decay_strict = jnp.where(strict, decay_incl, 0.0)
    L = beta[..., :, None] * jnp.einsum('bnhtk,bnhsk->bnhts', k, k) * decay_strict
    M = L + jnp.eye(C, dtype=L.dtype)
    u = lax.linalg.triangular_solve(M, beta[..., None] * v, left_side=True, lower=True, unit_diagonal=True)
    w = lax.linalg.triangular_solve(M, (beta * jnp.exp(G))[..., None] * k, left_side=True, lower=True, unit_diagonal=True)
    qk = jnp.einsum('bnhtk,bnhsk->bnhts', q, k) * decay_incl
    q_dec = q * jnp.exp(G)[..., None]
    k_dec = k * jnp.exp(G[..., -1:] - G)[..., None]
    g_last = jnp.exp(G[..., -1])

    def step(S, xs):
        q_c, qk_c, u_c, w_c, k_c, gl_c = xs
        delta = u_c - jnp.einsum('bhtk,bhkv->bhtv', w_c, S)
        o = jnp.einsum('bhtk,bhkv->bhtv', q_c, S) + jnp.einsum('bhts,bhsv->bhtv', qk_c, delta)
        S = gl_c[..., None, None] * S + jnp.einsum('bhsk,bhsv->bhkv', k_c, delta)
        return S, o

    xs = tuple(jnp.moveaxis(t, 1, 0) for t in (q_dec, qk, u, w, k_dec, g_last))
    S, o = lax.scan(step, S0, xs)
    o = jnp.moveaxis(jnp.moveaxis(o, 0, 1), 2, 3).reshape(Bn, nc * C, H, DV)[:, :T]
    return o, S


def even_mixer(h, cache_k, cache_v, S0, w_in, w_out, sinks, norm_g, lower_bound):
    Bn, T, _ = h.shape
    splits = np.cumsum([A_Q, A_KV, A_KV, B_FDIM, B_FDIM, B_WIDTH]).tolist()
    q_a, k_a, v_a, q_b, f_b, i_b, g_b = jnp.split(h @ w_in, splits, axis=-1)
    q_a = q_a.reshape(Bn, T, A_KV_HEADS, A_GROUP, A_HEAD_DIM)
    k_a = k_a.reshape(Bn, T, A_KV_HEADS, A_HEAD_DIM)
    v_a = v_a.reshape(Bn, T, A_KV_HEADS, A_HEAD_DIM)
    if cache_k is None:
        o_a = swa_prompt(q_a, k_a, v_a, sinks)
        new_k, new_v = k_a[:, -WINDOW:], v_a[:, -WINDOW:]
    else:
        keys = jnp.concatenate([cache_k, k_a], axis=1)
        vals = jnp.concatenate([cache_v, v_a], axis=1)
        q_pos = PAST_LEN + jnp.arange(T)
        k_pos = PAST_LEN - WINDOW + jnp.arange(WINDOW + T)
        o_a = sink_window_attention(q_a, keys, vals, q_pos, k_pos, sinks).reshape(Bn, T, A_Q)
        new_k, new_v = keys[:, -WINDOW:], vals[:, -WINDOW:]
    qb = jax.nn.silu(q_b.astype(jnp.float32)).reshape(Bn, T, B_HEADS, B_KEY_DIM)
    lb = lower_bound.reshape(B_HEADS, B_KEY_DIM)
    f = lb + (1.0 - lb) * jax.nn.sigmoid(f_b.astype(jnp.float32).reshape(Bn, T, B_HEADS, B_KEY_DIM))
    vb = i_b.astype(jnp.float32).reshape(Bn, T, B_HEADS, B_VAL_DIM)
    o_b, S_new = hgrn2_scan(qb, jnp.log(f), 1.0 - f, vb, S0)
    o_b = rms_norm(o_b, norm_g) * jax.nn.silu(g_b.astype(jnp.float32).reshape(Bn, T, B_HEADS, B_VAL_DIM))
    o = jnp.concatenate([o_a, o_b.astype(h.dtype).reshape(Bn, T, B_WIDTH)], axis=-1)
    return o @ w_out, new_k, new_v, S_new


def odd_mixer(h, conv_hist, S0, w_in, w_out, conv_w, a_log, dt_bias, norm_g):
    Bn, T, _ = h.shape
    qkv, gate, a_in, b_in = jnp.split(h @ w_in, [C_CONV_DIM, C_CONV_DIM + C_V, C_CONV_DIM + C_V + C_HEADS], axis=-1)
    full = jnp.concatenate([conv_hist, qkv], axis=1)
    acc = full[:, 0:T] * conv_w[0]
    for j in range(1, C_CONV):
        acc = acc + full[:, j:j + T] * conv_w[j]
    qkv_c = jax.nn.silu(acc)
    new_hist = full[:, T:]
    q, k, v = jnp.split(qkv_c, [C_QK, 2 * C_QK], axis=-1)
    q = l2_normalize(q.reshape(Bn, T, C_HEADS, C_KEY_DIM)) * (C_KEY_DIM ** -0.5)
    k = l2_normalize(k.reshape(Bn, T, C_HEADS, C_KEY_DIM))
    v = v.astype(jnp.float32).reshape(Bn, T, C_HEADS, C_VAL_DIM)
    beta = jax.nn.sigmoid(b_in.astype(jnp.float32))
    log_alpha = -jnp.exp(a_log.astype(jnp.float32)) * jax.nn.softplus(a_in.astype(jnp.float32) + dt_bias.astype(jnp.float32))
    o, S_new = gated_delta_scan(q, k, v, log_alpha, beta, S0)
    o = rms_norm(o, norm_g) * jax.nn.silu(gate.astype(jnp.float32).reshape(Bn, T, C_HEADS, C_VAL_DIM))
    return o.astype(h.dtype).reshape(Bn, T, C_V) @ w_out, new_hist, S_new


def run_trunk(x, c, caches, p):
    Bn = x.shape[0]
    cs = jax.nn.silu(c)
    probs = jax.nn.softmax(p['hgrn_lb_logits'].astype(jnp.float32), axis=0)
    lower = jnp.cumsum(probs, axis=0)[1:] - probs[0]
    ks, vs, hs, gs, cvs = [], [], [], [], []
    for l in range(DEPTH):
        mods = (cs @ p['ada_w'][l] + p['ada_b'][l]).reshape(Bn, N_MOD, 1, D_MODEL)
        sh1, sc1, g1, sh2, sc2, g2, sh3, sc3, g3 = (mods[:, j] for j in range(N_MOD))
        ffn1 = swiglu(x * (1.0 + sc1) + sh1, p['ffn_w_up'][l, 0], p['ffn_w_down'][l, 0])
        x = layer_norm(DN_ALPHA * x + 0.5 * g1 * ffn1, p['ln_g'][l, 0], p['ln_b'][l, 0])
        hm = x * (1.0 + sc2) + sh2
        if l % 2 == 0:
            e = l // 2
            if caches is None:
                ck, cv = None, None
                S0 = jnp.zeros((Bn, B_HEADS, B_KEY_DIM, B_VAL_DIM), jnp.float32)
            else:
                ck, cv = caches[0][e], caches[1][e]
                S0 = caches[2][e].astype(jnp.float32)
            mix, nk, nv, nS = even_mixer(hm, ck, cv, S0, p['even_w_in'][e], p['even_w_out'][e], p['swa_sinks'][e], p['hgrn_norm_g'][e], lower[e])
            ks.append(nk)
            vs.append(nv)
            hs.append(nS)
        else:
            o_idx = l // 2
            if caches is None:
                hist = jnp.zeros((Bn, C_CONV - 1, C_CONV_DIM), x.dtype)
                S0 = jnp.zeros((Bn, C_HEADS, C_KEY_DIM, C_VAL_DIM), jnp.float32)
            else:
                hist = caches[4][o_idx]
                S0 = caches[3][o_idx].astype(jnp.float32)
            mix, nh, nS = odd_mixer(hm, hist, S0, p['odd_w_in'][o_idx], p['odd_w_out'][o_idx], p['gdn_conv_w'][o_idx], p['gdn_a_log'][o_idx], p['gdn_dt_bias'][o_idx], p['gdn_norm_g'][o_idx])
            gs.append(nS)
            cvs.append(nh)
        x = layer_norm(DN_ALPHA * x + g2 * mix, p['ln_g'][l, 1], p['ln_b'][l, 1])
        ffn2 = swiglu(x * (1.0 + sc3) + sh3, p['ffn_w_up'][l, 1], p['ffn_w_down'][l, 1])
        x = layer_norm(DN_ALPHA * x + 0.5 * g3 * ffn2, p['ln_g'][l, 2], p['ln_b'][l, 2])
    dt = x.dtype
    return x, jnp.stack(ks).astype(dt), jnp.stack(vs).astype(dt), jnp.stack(hs).astype(dt), jnp.stack(gs).astype(dt), jnp.stack(cvs).astype(dt)


def setup_inputs(seed: int = 0) -> dict:
    key = jax.random.key(seed)
    keys = iter(jax.random.split(key, 32))

    def nrm(shape, scale):
        return jax.random.normal(next(keys), shape, jnp.float32) * scale

    dt = jnp.exp(jax.random.uniform(next(keys), (N_ODD, C_HEADS), jnp.float32, minval=np.log(1e-3), maxval=np.log(1e-1)))
    return {
        'x_prompt': nrm((BATCH, SEQ, D_MODEL), 1.0),
        'x_sample': nrm((DEC_BATCH, DEC_SEQ, D_MODEL), 1.0),
        'cache_swa_k': nrm((N_EVEN, DEC_BATCH, WINDOW, A_KV_HEADS, A_HEAD_DIM), 1.0),
        'cache_swa_v': nrm((N_EVEN, DEC_BATCH, WINDOW, A_KV_HEADS, A_HEAD_DIM), 1.0),
        'state_hgrn': nrm((N_EVEN, DEC_BATCH, B_HEADS, B_KEY_DIM, B_VAL_DIM), 0.5),
        'state_gdn': nrm((N_ODD, DEC_BATCH, C_HEADS, C_KEY_DIM, C_VAL_DIM), 0.1),
        'state_gdn_conv': nrm((N_ODD, DEC_BATCH, C_CONV - 1, C_CONV_DIM), 1.0),
        'c_prompt': nrm((BATCH, D_MODEL), 1.0),
        'c_sample': nrm((DEC_BATCH, D_MODEL), 1.0),
        'ada_w': nrm((DEPTH, D_MODEL, N_MOD * D_MODEL), D_MODEL ** -0.5),
        'ada_b': nrm((DEPTH, N_MOD * D_MODEL), 0.01),
        'ln_g': 1.0 + nrm((DEPTH, 3, D_MODEL), 0.02),
        'ln_b': nrm((DEPTH, 3, D_MODEL), 0.02),
        'ffn_w_up': nrm((DEPTH, 2, D_MODEL, 2 * D_FF), D_MODEL ** -0.5),
        'ffn_w_down': nrm((DEPTH, 2, D_FF, D_MODEL), D_FF ** -0.5 * DN_BETA),
        'even_w_in': nrm((N_EVEN, D_MODEL, EVEN_IN), D_MODEL ** -0.5),
        'even_w_out': nrm((N_EVEN, EVEN_OUT, D_MODEL), EVEN_OUT ** -0.5 * DN_BETA),
        'swa_sinks': nrm((N_EVEN, A_HEADS), 1.0),
        'hgrn_norm_g': 1.0 + nrm((N_EVEN, B_VAL_DIM), 0.02),
        'hgrn_lb_logits': nrm((N_EVEN + 1, B_FDIM), 0.5),
        'odd_w_in': nrm((N_ODD, D_MODEL, ODD_IN), D_MODEL ** -0.5),
        'odd_w_out': nrm((N_ODD, C_V, D_MODEL), C_V ** -0.5 * DN_BETA),
        'gdn_conv_w': nrm((N_ODD, C_CONV, C_CONV_DIM), C_CONV ** -0.5),
        'gdn_a_log': jnp.log(jax.random.uniform(next(keys), (N_ODD, C_HEADS), jnp.float32, minval=1.0, maxval=16.0)),
        'gdn_dt_bias': dt + jnp.log(-jnp.expm1(-dt)),
        'gdn_norm_g': 1.0 + nrm((N_ODD, C_VAL_DIM), 0.02),
    }


def reference(x_prompt, x_sample, cache_swa_k, cache_swa_v, state_hgrn, state_gdn, state_gdn_conv, c_prompt, c_sample, ada_w, ada_b, ln_g, ln_b, ffn_w_up, ffn_w_down, even_w_in, even_w_out, swa_sinks, hgrn_norm_g, hgrn_lb_logits, odd_w_in, odd_w_out, gdn_conv_w, gdn_a_log, gdn_dt_bias, gdn_norm_g):
    p = dict(ada_w=ada_w, ada_b=ada_b, ln_g=ln_g, ln_b=ln_b, ffn_w_up=ffn_w_up, ffn_w_down=ffn_w_down, even_w_in=even_w_in, even_w_out=even_w_out, swa_sinks=swa_sinks, hgrn_norm_g=hgrn_norm_g, hgrn_lb_logits=hgrn_lb_logits, odd_w_in=odd_w_in, odd_w_out=odd_w_out, gdn_conv_w=gdn_conv_w, gdn_a_log=gdn_a_log, gdn_dt_bias=gdn_dt_bias, gdn_norm_g=gdn_norm_g)
    y_prompt, p_k, p_v, p_hgrn, p_gdn, p_conv = run_trunk(x_prompt, c_prompt, None, p)
    y_sample, s_k, s_v, s_hgrn, s_gdn, s_conv = run_trunk(x_sample, c_sample, (cache_swa_k, cache_swa_v, state_hgrn, state_gdn, state_gdn_conv), p)
    return (y_prompt, y_sample, p_k, p_v, p_hgrn, p_gdn, p_conv, s_k, s_v, s_hgrn, s_gdn, s_conv)
```

```python
import contextlib
import numpy as np
import concourse.bass as bass
import concourse.mybir as mybir
from concourse.bass_utils import run_bass_kernel_spmd

F32 = mybir.dt.float32
BF16 = mybir.dt.bfloat16
AF = mybir.ActivationFunctionType
ALU = mybir.AluOpType
AX = mybir.AxisListType

NCORES = 8
D = 1024
DC = 8
TP = 2048
NS = 16
NCOND = 18
DFF = 2816
FC = 22
TT = 512
DN_ALPHA = 4.0 ** 0.25
LN_EPS = 1e-5
EPOCH = 30000


class Sync:
    def __init__(self, nc, es, n_dma_sems=32):
        self.nc = nc
        self.es = es
        self.eng = {"pe": nc.tensor, "act": nc.scalar, "dve": nc.vector, "pool": nc.gpsimd, "sp": nc.sync}
        self.count = {e: 0 for e in self.eng}
        self.epoch = {e: 0 for e in self.eng}
        self.sems = {}
        for e in self.eng:
            self.sems[(e, 0)] = es.enter_context(nc.semaphore(f"s_{e}_0"))
        self.known = {e: {} for e in self.eng}
        self.dma_sems = [es.enter_context(nc.semaphore(f"dsem{i}")) for i in range(n_dma_sems)]
        self.dma_val = [0] * n_dma_sems
        self.dma_next = 0
        self.cc_sem = es.enter_context(nc.semaphore("ccsem"))
        self.cc_val = 0
        self.res = {}
        self.n_inst = {e: 0 for e in self.eng}

    def _wait(self, e, tok):
        if tok is None:
            return
        sem, val, src = tok
        if src == e and e == "pe":
            return
        k = self.known[e]
        sid = id(sem)
        if k.get(sid, 0) >= val:
            return
        self.eng[e].wait_ge(sem, val)
        k[sid] = val

    def _deps(self, e, reads, writes):
        for key in reads:
            st = self.res.get(key)
            if st is not None:
                self._wait(e, st["w"])
        for key in writes:
            st = self.res.get(key)
            if st is not None:
                self._wait(e, st["w"])
                for t in st["r"]:
                    self._wait(e, t)

    def _update(self, tok, reads, writes):
        for key in reads:
            st = self.res.setdefault(key, {"w": None, "r": []})
            st["r"] = [t for t in st["r"] if not (t[2] is not None and t[2] == tok[2] and t[0] is tok[0])]
            st["r"].append(tok)
        for key in writes:
            self.res[key] = {"w": tok, "r": []}

    def op(self, e, fn, reads=(), writes=()):
        self._deps(e, reads, writes)
        if self.count[e] >= EPOCH:
            self.epoch[e] += 1
            self.count[e] = 0
            self.sems[(e, self.epoch[e])] = self.es.enter_context(self.nc.semaphore(f"s_{e}_{self.epoch[e]}"))
        sem = self.sems[(e, self.epoch[e])]
        fn().then_inc(sem, 1)
        self.count[e] += 1
        self.n_inst[e] += 1
        tok = (sem, self.count[e], e)
        self._update(tok, reads, writes)
        return tok

    def dma(self, q, fns, reads=(), writes=()):
        if callable(fns):
            fns = [fns]
        self._deps(q, reads, writes)
        i = self.dma_next
        self.dma_next = (self.dma_next + 1) % len(self.dma_sems)
        sem = self.dma_sems[i]
        if self.dma_val[i] > 0:
            self._wait(q, (sem, self.dma_val[i], None))
        for fn in fns:
            fn().then_inc(sem, 16)
            self.dma_val[i] += 16
        tok = (sem, self.dma_val[i], None)
        self._update(tok, reads, writes)
        return tok

    def collective(self, fn, reads=(), writes=()):
        self._deps("pool", reads, writes)
        fn().then_inc(self.cc_sem)
        self.cc_val += 1
        tok = (self.cc_sem, self.cc_val, None)
        self._update(tok, reads, writes)
        return tok

    def barrier(self):
        toks = []
        for e in self.eng:
            if self.count[e] > 0:
                toks.append((self.sems[(e, self.epoch[e])], self.count[e], e))
        for i, s in enumerate(self.dma_sems):
            if self.dma_val[i] > 0:
                toks.append((s, self.dma_val[i], None))
        if self.cc_val > 0:
            toks.append((self.cc_sem, self.cc_val, None))
        for e in self.eng:
            for t in toks:
                self._wait(e, t)
        self.res = {}

    def final_wait(self, e="sp"):
        self.barrier()


class Prog:
    def __init__(self, debug=()):
        self.debug = set(debug)
        self.nc = bass.Bass("TRN2", target_bir_lowering=False)
        self.es = contextlib.ExitStack()
        self.inputs = {}
        self.outputs = {}

    def din(self, name, shape, dt=F32):
        t = self.nc.dram_tensor(name, list(shape), dt, kind="ExternalInput").ap()
        self.inputs[name] = t
        return t

    def dout(self, name, shape, dt=F32):
        t = self.nc.dram_tensor(name, list(shape), dt, kind="ExternalOutput").ap()
        self.outputs[name] = t
        return t

    def sb(self, name, shape, dt=F32, stack=None):
        self.uid = getattr(self, "uid", 0) + 1
        return (stack or self.es).enter_context(self.nc.sbuf_tensor(f"{name}_{self.uid}", list(shape), dt))

    def ps(self, name, shape, dt=F32, stack=None):
        return (stack or self.es).enter_context(self.nc.psum_tensor(name, list(shape), dt))


def build_program(debug=()):
    P = Prog(debug)
    nc = P.nc
    with P.es:
        S = Sync(nc, P.es)
        P.S = S
        _build(P, nc, S)
        S.final_wait()
    return P


def _build(P, nc, S):
    dbg = P.debug
    xTp = P.din("xTp", [128, DC, TP])
    xTs = P.din("xTs", [128, DC, NS])
    ccT = P.din("ccT", [128, DC, NCOND])
    ada_r = P.din("ada_r", [2, 18, 128, DC, 512])
    ada_bT = P.din("ada_bT", [128, 2, 72])
    ln_gT = P.din("ln_gT", [128, 48])
    ln_bT = P.din("ln_bT", [128, 48])
    wup_r = P.din("wup_r", [2, 2, 11, 128, DC, 512])
    wdn_r = P.din("wdn_r", [2, 2, 4, 128, FC, 256])
    consts = P.din("consts", [128, 1800])

    yTp = P.dout("yTp", [128, DC, TP])
    yTs = P.dout("yTs", [128, DC, NS])

    xp = P.sb("xp", [128, DC, TP])
    xs = P.sb("xs", [128, DC, NS])
    mods = P.sb("mods", [128, 2, 72, NCOND])
    lng = P.sb("lng", [128, 48])
    lnb = P.sb("lnb", [128, 48])
    cst = P.sb("cst", [128, 1800])
    onesb = P.sb("onesb", [128, 128], BF16)
    adab = P.sb("adab", [128, 2, 72])
    pbank = [P.ps(f"pb{i}", [128, 512]) for i in range(8)]

    S.dma("sp", lambda: nc.sync.dma_start(out=xp[:], in_=xTp[:, :, :]), writes=["xp"])
    S.dma("sp", lambda: nc.sync.dma_start(out=xs[:], in_=xTs[:, :, :]), writes=["xs"])
    S.dma("sp", lambda: nc.sync.dma_start(out=lng[:], in_=ln_gT[:, :]), writes=["lng"])
    S.dma("sp", lambda: nc.sync.dma_start(out=lnb[:], in_=ln_bT[:, :]), writes=["lnb"])
    S.dma("sp", lambda: nc.sync.dma_start(out=cst[:], in_=consts[:, :]), writes=["cst"])
    S.dma("sp", lambda: nc.sync.dma_start(out=adab[:], in_=ada_bT[:, :, :]), writes=["adab"])
    S.op("dve", lambda: nc.vector.memset(onesb[:], 1.0 / 1024.0), writes=["onesb"])

    with contextlib.ExitStack() as st:
        cs = P.sb("cs", [128, DC, NCOND], stack=st)
        csb = P.sb("csb", [128, DC, NCOND], BF16, stack=st)
        abuf = [P.sb(f"abuf{i}", [128, DC, 512], BF16, stack=st) for i in range(3)]
        S.dma("sp", lambda: nc.sync.dma_start(out=cs[:], in_=ccT[:, :, :]), writes=["cs"])
        S.op("act", lambda: nc.scalar.activation(out=csb[:], in_=cs[:], func=AF.Silu), reads=["cs"], writes=["csb"])
        blocks = [(l, j) for l in range(2) for j in range(18)]

        def load(i):
            l, j = blocks[i]
            b = i % 3
            S.dma("pool", lambda: nc.gpsimd.dma_start(out=abuf[b][:], in_=ada_r[l, j]), writes=[f"abuf{b}"])

        for i in range(2):
            load(i)
        for i, (l, j) in enumerate(blocks):
            b = i % 3
            pb = pbank[i % 2]
            for q in range(4):
                for c in range(DC):
                    S.op("pe", lambda q=q, c=c: nc.tensor.matmul(
                        pb[:, q * NCOND:(q + 1) * NCOND], abuf[b][:, c, q * 128:(q + 1) * 128], csb[:, c, :],
                        start=(c == 0), stop=(c == DC - 1)),
                        reads=[f"abuf{b}", "csb"], writes=[f"pb{i % 2}"])
            S.op("dve", lambda: nc.vector.tensor_tensor(
                out=mods[:, l, 4 * j:4 * j + 4, :],
                in0=pb[:, 0:4 * NCOND].rearrange("p (q n) -> p q n", n=NCOND),
                in1=adab[:, l, 4 * j:4 * j + 4].unsqueeze(2).to_broadcast([128, 4, NCOND]),
                op=ALU.add), reads=[f"pb{i % 2}", "adab"], writes=["mods"])
            if i + 2 < len(blocks):
                load(i + 2)
        for l in range(2):
            for j in (1, 4, 7):
                S.op("dve", lambda l=l, j=j: nc.vector.tensor_scalar(
                    out=mods[:, l, 8 * j:8 * j + 8, :], in0=mods[:, l, 8 * j:8 * j + 8, :],
                    scalar1=1.0, scalar2=None, op0=ALU.add), reads=["mods"], writes=["mods"])
            for j in (2, 8):
                S.op("dve", lambda l=l, j=j: nc.vector.tensor_scalar(
                    out=mods[:, l, 8 * j:8 * j + 8, :], in0=mods[:, l, 8 * j:8 * j + 8, :],
                    scalar1=0.5, scalar2=None, op0=ALU.mult), reads=["mods"], writes=["mods"])
        S.barrier()

    if "mods" in dbg:
        o = P.dout("dbg_mods", [128, 2, 72, NCOND])
        S.dma("sp", lambda: nc.sync.dma_start(out=o[:, :, :, :], in_=mods[:]), reads=["mods"])

    tiles = [("p", i * TT, TT) for i in range(TP // TT)] + [("s", 0, NS)]
    if "only_s" in dbg:
        tiles = [("s", 0, NS)]
    if "only_p0" in dbg:
        tiles = [("p", 0, TT)]

    def xview(kind, t0, n):
        return (xp if kind == "p" else xs)[:, :, t0:t0 + n]

    def ffn_sublayer(l, k):
        jsh, jsc, jg = (0, 1, 2) if k == 0 else (6, 7, 8)
        lnidx = (l * 3 + (0 if k == 0 else 2)) * 8
        with contextlib.ExitStack() as st:
            h = P.sb("f_h", [128, DC, TT], BF16, stack=st)
            act = P.sb("f_act", [128, FC, TT], BF16, stack=st)
            sg = [P.sb(f"f_sg{i}", [128, TT], stack=st) for i in range(2)]
            wu = [P.sb(f"f_wu{i}", [128, DC, 512], BF16, stack=st) for i in range(3)]
            wd = [P.sb(f"f_wd{i}", [128, FC, 256], BF16, stack=st) for i in range(2)]
            xb = P.sb("f_xb", [128, DC, TT], BF16, stack=st)
            sq = P.sb("f_sq", [128, DC, TT], BF16, stack=st)
            r = [P.sb(f"f_r{i}", [128, TT], stack=st) for i in range(2)]
            lt = [P.sb(f"f_lt{i}", [128, TT], stack=st) for i in range(4)]
            wl = []
            for ti in range(len(tiles)):
                for j in range(11):
                    wl.append(("u", j))
                for j in range(4):
                    wl.append(("d", j))
            cnt = {"u": 0, "d": 0}
            slot = []
            for kind_, j in wl:
                if kind_ == "u":
                    slot.append(("u", cnt["u"] % 3)); cnt["u"] += 1
                else:
                    slot.append(("d", cnt["d"] % 2)); cnt["d"] += 1
            issued = [0]
            prev_occ = []
            last_of = {}
            for i, sl in enumerate(slot):
                prev_occ.append(last_of.get(sl, -1))
                last_of[sl] = i

            def issue_upto(n, cur):
                while issued[0] < min(n, len(wl)) and prev_occ[issued[0]] < cur:
                    i = issued[0]
                    kind_, j = wl[i]
                    _, b = slot[i]
                    if kind_ == "u":
                        S.dma("pool", lambda: nc.gpsimd.dma_start(out=wu[b][:], in_=wup_r[l, k, j]), writes=[f"wu{b}"])
                    else:
                        S.dma("pool", lambda: nc.gpsimd.dma_start(out=wd[b][:], in_=wdn_r[l, k, j]), writes=[f"wd{b}"])
                    issued[0] += 1

            wi = 0
            issue_upto(2, 0)
            def make_h(kind, t0, n):
                xv = xview(kind, t0, n)
                xk = f"x{kind}{t0}"
                for c in range(DC):
                    if kind == "p":
                        S.op("dve", lambda c=c: nc.vector.tensor_scalar(
                            out=h[:, c, :n], in0=xv[:, c, :], scalar1=mods[:, l, jsc * 8 + c, 0:1],
                            scalar2=mods[:, l, jsh * 8 + c, 0:1], op0=ALU.mult, op1=ALU.add),
                            reads=[xk, "mods"], writes=["h"])
                    else:
                        S.op("dve", lambda c=c: nc.vector.tensor_tensor(
                            out=sg[0][:, :n], in0=xv[:, c, :], in1=mods[:, l, jsc * 8 + c, 1:1 + NS], op=ALU.mult),
                            reads=[xk, "mods"], writes=["sg0"])
                        S.op("dve", lambda c=c: nc.vector.tensor_tensor(
                            out=h[:, c, :n], in0=sg[0][:, :n], in1=mods[:, l, jsh * 8 + c, 1:1 + NS], op=ALU.add),
                            reads=["sg0", "mods"], writes=["h"])

            make_h(*tiles[0])
            pending_ln = [None]
            for ti, (kind, t0, n) in enumerate(tiles):
                xv = xview(kind, t0, n)
                xk = f"x{kind}{t0}"
                if "ffn_int" in dbg and l == 0 and k == 0 and ti == 0:
                    S.barrier()
                    o = P.dout("dbg_h", [128, DC, TT], BF16)
                    S.dma("sp", lambda: nc.sync.dma_start(out=o[:, :, :], in_=h[:]), reads=["h"])
                for j in range(11):
                    _, b = slot[wi]
                    issue_upto(wi + 3, wi)
                    if j == 3 and pending_ln[0] is not None:
                        pending_ln[0]()
                        pending_ln[0] = None
                    for i2 in range(2):
                        f = 2 * j + i2
                        pg = pbank[(f % 2) * 2]
                        pu = pbank[(f % 2) * 2 + 1]
                        for c in range(DC):
                            S.op("pe", lambda c=c: nc.tensor.matmul(
                                pg[:, :n], wu[b][:, c, i2 * 128:(i2 + 1) * 128], h[:, c, :n],
                                start=(c == 0), stop=(c == DC - 1)),
                                reads=[f"wu{b}", "h"], writes=[f"pb{(f % 2) * 2}"])
                        for c in range(DC):
                            S.op("pe", lambda c=c: nc.tensor.matmul(
                                pu[:, :n], wu[b][:, c, 256 + i2 * 128:256 + (i2 + 1) * 128], h[:, c, :n],
                                start=(c == 0), stop=(c == DC - 1)),
                                reads=[f"wu{b}", "h"], writes=[f"pb{(f % 2) * 2 + 1}"])
                        S.op("act", lambda: nc.scalar.activation(out=sg[f % 2][:, :n], in_=pg[:, :n], func=AF.Silu),
                             reads=[f"pb{(f % 2) * 2}"], writes=[f"sg{f % 2}"])
                        S.op("dve", lambda f=f: nc.vector.tensor_tensor(
                            out=act[:, f, :n], in0=sg[f % 2][:, :n], in1=pu[:, :n], op=ALU.mult),
                            reads=[f"sg{f % 2}", f"pb{(f % 2) * 2 + 1}"], writes=[f"act{f}"])
                    wi += 1
                if ti + 1 < len(tiles):
                    make_h(*tiles[ti + 1])
                if "ffn_int" in dbg and l == 0 and k == 0 and ti == 0:
                    S.barrier()
                    o = P.dout("dbg_act", [128, FC, TT], BF16)
                    S.dma("sp", lambda: nc.sync.dma_start(out=o[:, :, :], in_=act[:]), reads=[f"act{f}" for f in range(FC)])
                for j in range(4):
                    _, b = slot[wi]
                    issue_upto(wi + 3, wi)
                    for i2 in range(2):
                        c_o = 2 * j + i2
                        pd = pbank[4 + (c_o % 2)]
                        for f in range(FC):
                            S.op("pe", lambda f=f: nc.tensor.matmul(
                                pd[:, :n], wd[b][:, f, i2 * 128:(i2 + 1) * 128], act[:, f, :n],
                                start=(f == 0), stop=(f == FC - 1)),
                                reads=[f"wd{b}", f"act{f}"], writes=[f"pb{4 + (c_o % 2)}"])
                        rr = r[c_o % 2]
                        if kind == "p":
                            S.op("act", lambda: nc.scalar.activation(
                                out=rr[:, :n], in_=pd[:, :n], func=AF.Copy, scale=mods[:, l, jg * 8 + c_o, 0:1]),
                                reads=[f"pb{4 + (c_o % 2)}", "mods"], writes=[f"r{c_o % 2}"])
                        else:
                            S.op("dve", lambda: nc.vector.tensor_tensor(
                                out=rr[:, :n], in0=pd[:, :n], in1=mods[:, l, jg * 8 + c_o, 1:1 + NS], op=ALU.mult),
                                reads=[f"pb{4 + (c_o % 2)}", "mods"], writes=[f"r{c_o % 2}"])
                        S.op("dve", lambda: nc.vector.scalar_tensor_tensor(
                            out=xv[:, c_o, :], in0=xv[:, c_o, :], scalar=DN_ALPHA, in1=rr[:, :n],
                            op0=ALU.mult, op1=ALU.add),
                            reads=[f"r{c_o % 2}", xk, f"{xk}c{c_o}"], writes=[f"{xk}c{c_o}"])
                        S.op("act", lambda: nc.scalar.copy(out=xb[:, c_o, :n], in_=xv[:, c_o, :]),
                             reads=[f"{xk}c{c_o}"], writes=[f"xb{c_o}"])
                        S.op("act", lambda: nc.scalar.activation(out=sq[:, c_o, :n], in_=xv[:, c_o, :], func=AF.Square),
                             reads=[f"{xk}c{c_o}"], writes=[f"sq{c_o}"])
                    wi += 1
                if "ffn_int" in dbg and l == 0 and k == 0 and ti == 0:
                    S.barrier()
                    o = P.dout("dbg_pre", [128, DC, n])
                    S.dma("sp", lambda: nc.sync.dma_start(out=o[:, :, :], in_=xv), reads=[xk])
                    S.barrier()
                pending_ln[0] = (lambda xk=xk, xv=xv, n=n: layer_norm_tile(xk, xv, n, xb, sq, lt, lnidx))
            pending_ln[0]()
            pending_ln[0] = None

    def layer_norm_tile(xk, xv, n, xb, sq, lt, lnidx):
        pm, pq = pbank[6], pbank[7]
        for c in range(DC):
            S.op("pe", lambda c=c: nc.tensor.matmul(pm[:, :n], onesb[:], xb[:, c, :n], start=(c == 0), stop=(c == DC - 1)),
                 reads=["onesb", f"xb{c}"], writes=["pb6"])
        for c in range(DC):
            S.op("pe", lambda c=c: nc.tensor.matmul(pq[:, :n], onesb[:], sq[:, c, :n], start=(c == 0), stop=(c == DC - 1)),
                 reads=["onesb", f"sq{c}"], writes=["pb7"])
        msq, var, rstd, nmr = lt
        S.op("act", lambda: nc.scalar.activation(out=msq[:, :n], in_=pm[:, :n], func=AF.Square), reads=["pb6"], writes=["lt0"])
        S.op("dve", lambda: nc.vector.tensor_tensor(out=var[:, :n], in0=pq[:, :n], in1=msq[:, :n], op=ALU.subtract),
             reads=["pb7", "lt0"], writes=["lt1"])
        S.op("dve", lambda: nc.vector.tensor_scalar(out=var[:, :n], in0=var[:, :n], scalar1=LN_EPS, scalar2=None, op0=ALU.add),
             reads=["lt1"], writes=["lt1"])
        S.op("act", lambda: nc.scalar.activation(out=var[:, :n], in_=var[:, :n], func=AF.Ln), reads=["lt1"], writes=["lt1"])
        S.op("act", lambda: nc.scalar.activation(out=rstd[:, :n], in_=var[:, :n], func=AF.Exp, scale=-0.5), reads=["lt1"], writes=["lt2"])
        S.op("dve", lambda: nc.vector.scalar_tensor_tensor(
            out=nmr[:, :n], in0=pm[:, :n], scalar=-1.0, in1=rstd[:, :n], op0=ALU.mult, op1=ALU.mult),
            reads=["pb6", "lt2"], writes=["lt3"])
        for c in range(DC):
            S.op("dve", lambda c=c: nc.vector.tensor_tensor(out=xv[:, c, :], in0=xv[:, c, :], in1=rstd[:, :n], op=ALU.mult),
                 reads=["lt2", f"{xk}c{c}"], writes=[f"{xk}c{c}"])
        for c in range(DC):
            S.op("dve", lambda c=c: nc.vector.tensor_tensor(out=xv[:, c, :], in0=xv[:, c, :], in1=nmr[:, :n], op=ALU.add),
                 reads=["lt3", f"{xk}c{c}"], writes=[f"{xk}c{c}"])
        for c in range(DC):
            S.op("act", lambda c=c: nc.scalar.activation(
                out=xv[:, c, :], in_=xv[:, c, :], func=AF.Identity,
                scale=lng[:, lnidx + c:lnidx + c + 1], bias=lnb[:, lnidx + c:lnidx + c + 1]),
                reads=[f"{xk}c{c}", "lng", "lnb"], writes=[f"{xk}c{c}", xk])

    def V(fn, r=(), w=()):
        return S.op("dve", fn, reads=r, writes=w)

    def A(fn, r=(), w=()):
        return S.op("act", fn, reads=r, writes=w)

    def T(fn, r=(), w=()):
        return S.op("pe", fn, reads=r, writes=w)

    ident = cst[:, 0:128]
    mask16T = cst[:, 128:256]
    chunkmask = cst[:, 256:264]
    rmask = cst[:, 264:264 + 512]
    onesf = cst[:, 776:776 + 1024]
    ones128 = P.sb("ones128", [128, 128], BF16)
    V(lambda: nc.vector.memset(ones128[:], 1.0 / 128.0), w=["ones128"])

    win0 = P.din("win0_r", [6, 128, DC, 512])
    wout0 = P.din("wout0_r", [128, 8, D])
    swab = P.din("swa_dist", [128, 2, 256])
    sinkb = P.din("sinks_b", [128, 8])
    lbl = P.din("lb_logits", [128, 2, 4])
    hng = P.din("hgrn_g", [128, 1])
    pcore = P.din("percore", [128, 16])
    XW = 772
    xsrc = nc.dram_tensor("xch_src0", [128, XW], F32)
    xdst = nc.dram_tensor("xch_dst0", [4 * 128, XW], F32)
    o_swak = P.dout("o_swak", [128, 128])
    o_swav = P.dout("o_swav", [128, 128])
    o_hgrn = P.dout("o_hgrn", [128, 4, 128])
    ck_in = P.din("ck_in", [NS, 128, 128])
    cv_in = P.din("cv_in", [NS, 128, 128])
    sh_in = P.din("sh_in", [NS, 4, 128, 128])
    dec_tab = P.din("dec_tab", [128, 64])
    dec_rep = P.din("dec_rep", [NS, 128])
    o_sk = P.dout("o_sk", [NS, 128, 128])
    o_sv = P.dout("o_sv", [NS, 128, 128])
    o_sh = P.dout("o_sh", [NS, 4, 128, 128])

    def mixer_even():
        l = 0
        jsh, jsc, jg = 3, 4, 5
        with contextlib.ExitStack() as st:
            sink_t = P.sb("m_sink", [128, 8], stack=st)
            lb_t = P.sb("m_lb", [128, 2, 4], stack=st)
            lbv = P.sb("m_lbv", [128, 4], stack=st)
            oml = P.sb("m_oml", [128, 4], stack=st)
            hg_t = P.sb("m_hg", [128, 1], stack=st)
            pc_t = P.sb("m_pc", [128, 16], stack=st)
            xs_t = P.sb("m_xs", [128, XW], stack=st)
            S0 = P.sb("m_S0", [128, 4, 128], stack=st)
            o_a = P.sb("m_oa", [128, 4, TP], BF16, stack=st)
            o_b = P.sb("m_ob", [128, 4, TP], BF16, stack=st)
            o_as = P.sb("m_oas", [128, 4, NS], BF16, stack=st)
            o_bs = P.sb("m_obs", [128, 4, NS], BF16, stack=st)
            st_hm = contextlib.ExitStack()
            hm = P.sb("m_hm", [128, DC, TP], BF16, stack=st_hm)
            wbuf = [P.sb(f"m_w{i}", [128, DC, 512], BF16, stack=st_hm) for i in range(2)]
            S.dma("sp", lambda: nc.sync.dma_start(out=sink_t[:], in_=sinkb[:, :]), writes=["sink"])
            S.dma("sp", lambda: nc.sync.dma_start(out=lb_t[:], in_=lbl[:, :, :]), writes=["lb_t"])
            S.dma("sp", lambda: nc.sync.dma_start(out=hg_t[:], in_=hng[:, :]), writes=["hg"])
            S.dma("sp", lambda: nc.sync.dma_start(out=pc_t[:], in_=pcore[:, :]), writes=["pc"])
            V(lambda: nc.vector.tensor_tensor(out=lbv[:], in0=lb_t[:, 1, :], in1=lb_t[:, 0, :], op=ALU.subtract), r=["lb_t"], w=["lbv"])
            A(lambda: nc.scalar.activation(out=lbv[:], in_=lbv[:], func=AF.Sigmoid), r=["lbv"], w=["lbv"])
            V(lambda: nc.vector.tensor_scalar(out=oml[:], in0=lbv[:], scalar1=-1.0, scalar2=1.0, op0=ALU.mult, op1=ALU.add), r=["lbv"], w=["oml"])
            for c in range(DC):
                for tt in range(TP // TT):
                    V(lambda c=c, tt=tt: nc.vector.tensor_scalar(
                        out=hm[:, c, tt * TT:(tt + 1) * TT], in0=xp[:, c, tt * TT:(tt + 1) * TT],
                        scalar1=mods[:, l, jsc * 8 + c, 0:1], scalar2=mods[:, l, jsh * 8 + c, 0:1],
                        op0=ALU.mult, op1=ALU.add), r=["mods", f"xp{tt * TT}"], w=["hm"])

            def load_w(i, b):
                S.dma("pool", lambda: nc.gpsimd.dma_start(out=wbuf[b][:], in_=win0[i]), writes=[f"mw{b}"])

            def proj_fm(pi, n, b, col0, t0):
                for c in range(DC):
                    T(lambda c=c: nc.tensor.matmul(pbank[pi][:, 0:n], wbuf[b][:, c, col0:col0 + 128], hm[:, c, t0:t0 + n],
                                                   start=(c == 0), stop=(c == DC - 1)), r=[f"mw{b}", "hm"], w=[f"pb{pi}"])

            def proj_tm(pi, b, col0, ncols, t0):
                for c in range(DC):
                    T(lambda c=c: nc.tensor.matmul(pbank[pi][:, 0:ncols], hm[:, c, t0:t0 + 128], wbuf[b][:, c, col0:col0 + ncols],
                                                   start=(c == 0), stop=(c == DC - 1)), r=[f"mw{b}", "hm"], w=[f"pb{pi}"])

            load_w(2, 0)
            with contextlib.ExitStack() as st2:
                F_ = P.sb("a_F", [128, TP], stack=st2)
                AA = P.sb("a_AA", [128, TP], stack=st2)
                vtm = P.sb("a_v", [128, 16, 128], stack=st2)
                ktm = [P.sb(f"a_k{i}", [128, 128], stack=st2) for i in range(2)]
                for hh in range(4):
                    b = hh % 2
                    if hh + 1 < 4:
                        load_w(3 + hh, (hh + 1) % 2)
                    for tt in range(4):
                        proj_fm(tt % 2, TT, b, 128, tt * TT)
                        A(lambda tt=tt: nc.scalar.activation(out=F_[:, tt * TT:(tt + 1) * TT], in_=pbank[tt % 2][:, :], func=AF.Sigmoid),
                          r=[f"pb{tt % 2}"], w=[f"aF{tt}"])
                    for blk in range(16):
                        proj_tm(2 + blk % 2, b, 384, 128, blk * 128)
                        V(lambda blk=blk: nc.vector.tensor_copy(out=vtm[:, blk, :], in_=pbank[2 + blk % 2][:, 0:128]),
                          r=[f"pb{2 + blk % 2}"], w=[f"av{blk}"])
                    Fk = [f"aF{tt}" for tt in range(4)]
                    V(lambda: nc.vector.tensor_scalar(out=F_[:], in0=F_[:], scalar1=oml[:, hh:hh + 1], scalar2=lbv[:, hh:hh + 1],
                                                      op0=ALU.mult, op1=ALU.add), r=Fk + ["oml", "lbv"], w=Fk)
                    A(lambda: nc.scalar.activation(out=AA[:], in_=F_[:], func=AF.Ln), r=Fk, w=["aAA"])
                    V(lambda: nc.vector.tensor_scalar(out=F_[:], in0=F_[:], scalar1=-1.0, scalar2=1.0, op0=ALU.mult, op1=ALU.add), r=Fk, w=Fk)
                    V(lambda: nc.vector.tensor_tensor_scan(out=AA[:, 0:1024], data0=onesf, data1=AA[:, 0:1024], initial=0.0,
                                                           op0=ALU.mult, op1=ALU.add), r=["aAA", "cst"], w=["aAA"])
                    V(lambda: nc.vector.tensor_tensor_scan(out=AA[:, 1024:2048], data0=onesf, data1=AA[:, 1024:2048],
                                                           initial=AA[:, 1023:1024], op0=ALU.mult, op1=ALU.add), r=["aAA", "cst"], w=["aAA"])
                    A(lambda: nc.scalar.activation(out=xs_t[:, 768 + hh:769 + hh], in_=AA[:, TP - 1:TP], func=AF.Exp), r=["aAA"], w=["xs_t"])
                    V(lambda: nc.vector.tensor_scalar(out=AA[:, 0:TP - 1], in0=AA[:, 0:TP - 1], scalar1=AA[:, TP - 1:TP], scalar2=-1.0,
                                                      op0=ALU.subtract, op1=ALU.mult), r=["aAA"], w=["aAA"])
                    V(lambda: nc.vector.memset(AA[:, TP - 1:TP], 0.0), r=["aAA"], w=["aAA"])
                    A(lambda: nc.scalar.activation(out=AA[:], in_=AA[:], func=AF.Exp), r=["aAA"], w=["aAA"])
                    V(lambda: nc.vector.tensor_tensor(out=AA[:], in0=AA[:], in1=F_[:], op=ALU.mult), r=["aAA"] + Fk, w=["aAA"])
                    for blk in range(16):
                        pi = 4 + blk % 2
                        T(lambda blk=blk, pi=pi: nc.tensor.transpose(pbank[pi][:, 0:128], AA[:, blk * 128:(blk + 1) * 128], ident),
                          r=["aAA", "cst"], w=[f"pb{pi}"])
                        V(lambda blk=blk, pi=pi: nc.vector.tensor_copy(out=ktm[blk % 2][:], in_=pbank[pi][:, 0:128]),
                          r=[f"pb{pi}"], w=[f"ak{blk % 2}"])
                        T(lambda blk=blk: nc.tensor.matmul(pbank[6][:, 0:128], ktm[blk % 2][:], vtm[:, blk, :],
                                                           start=(blk == 0), stop=(blk == 15)),
                          r=[f"ak{blk % 2}", f"av{blk}"], w=["pb6"])
                    V(lambda: nc.vector.tensor_copy(out=xs_t[:, 256 + hh * 128:256 + (hh + 1) * 128], in_=pbank[6][:, 0:128]),
                      r=["pb6"], w=["xs_t"])
                S.barrier()

            st_swa = contextlib.ExitStack()
            qT = P.sb("m_qT", [128, 4, TP], BF16, stack=st_swa)
            kT = P.sb("m_kT", [128, 17 * 128], BF16, stack=st_swa)
            vT = P.sb("m_vT", [128, 17, 128], BF16, stack=st_swa)
            load_w(0, 0)
            load_w(1, 1)
            for g in range(4):
                for tt in range(4):
                    pi = (g * 4 + tt) % 2
                    proj_fm(pi, TT, 0, g * 128, tt * TT)
                    A(lambda g=g, tt=tt, pi=pi: nc.scalar.mul(out=qT[:, g, tt * TT:(tt + 1) * TT], in_=pbank[pi][:, :], mul=0.125),
                      r=[f"pb{pi}"], w=["qT"])
            for tt in range(4):
                pi = 2 + tt % 2
                proj_fm(pi, TT, 1, 0, tt * TT)
                A(lambda tt=tt, pi=pi: nc.scalar.copy(out=kT[:, 128 + tt * TT:128 + (tt + 1) * TT], in_=pbank[pi][:, :]),
                  r=[f"pb{pi}"], w=["kT"])
            for blk in range(16):
                pi = 4 + blk % 2
                proj_tm(pi, 1, 128, 128, blk * 128)
                V(lambda blk=blk, pi=pi: nc.vector.tensor_copy(out=vT[:, 1 + blk, :], in_=pbank[pi][:, 0:128]), r=[f"pb{pi}"], w=["vT"])
                if blk == 15:
                    V(lambda pi=pi: nc.vector.tensor_copy(out=xs_t[:, 128:256], in_=pbank[pi][:, 0:128]), r=[f"pb{pi}"], w=["xs_t"])
            proj_fm(6, 128, 1, 0, TP - 128)
            V(lambda: nc.vector.tensor_copy(out=xs_t[:, 0:128], in_=pbank[6][:, 0:128]), r=["pb6"], w=["xs_t"])
            proj_tm(7, 1, 0, 128, TP - 128)
            with contextlib.ExitStack() as st2:
                ko = P.sb("x_ko", [128, 128], stack=st2)
                V(lambda: nc.vector.tensor_copy(out=ko[:], in_=pbank[7][:, 0:128]), r=["pb7"], w=["ko"])
                S.dma("sp", lambda: nc.sync.dma_start(out=o_swak[:, :], in_=ko[:]), reads=["ko"])
                S.dma("sp", lambda: nc.sync.dma_start(out=o_swav[:, :], in_=xs_t[:, 128:256]), reads=["xs_t"])
                S.dma("pool", lambda: nc.gpsimd.dma_start(out=xsrc[:, :], in_=xs_t[:]), reads=["xs_t"], writes=["xsrc"])
                S.collective(lambda: nc.gpsimd.collective_compute(
                    "AllGather", ALU.bypass, replica_groups=([[0, 1, 2, 3]] if "half" in dbg else [[0, 1, 2, 3], [4, 5, 6, 7]]),
                    ins=[xsrc.ap().opt()], outs=[xdst.ap().opt()]), reads=["xsrc"], writes=["xdst"])
                xg_t = P.sb("x_xg", [128, 4, XW], stack=st2)
                S.dma("sp", lambda: nc.sync.dma_start(out=xg_t[:], in_=xdst[:, :].rearrange("(r p) n -> p r n", p=128)),
                      reads=["xdst"], writes=["xg"])
                acc = P.sb("x_acc", [128, 256], stack=st2)
                V(lambda: nc.vector.tensor_scalar(out=acc[:], in0=xg_t[:, 0, 0:256], scalar1=pc_t[:, 0:1], scalar2=None, op0=ALU.mult),
                  r=["xg", "pc"], w=["xacc"])
                for r_ in range(1, 4):
                    V(lambda r_=r_: nc.vector.scalar_tensor_tensor(out=acc[:], in0=xg_t[:, r_, 0:256], scalar=pc_t[:, r_:r_ + 1],
                                                                   in1=acc[:], op0=ALU.mult, op1=ALU.add), r=["xg", "pc", "xacc"], w=["xacc"])
                V(lambda: nc.vector.tensor_copy(out=kT[:, 0:128], in_=acc[:, 0:128]), r=["xacc"], w=["kT"])
                V(lambda: nc.vector.tensor_copy(out=vT[:, 0, :], in_=acc[:, 128:256]), r=["xacc"], w=["vT"])
                V(lambda: nc.vector.memset(S0[:], 0.0), w=["S0"])
                coef = P.sb("x_coef", [128, 4], stack=st2)
                tmpS = P.sb("x_tmpS", [128, 128], stack=st2)
                for r_ in range(4):
                    m_r = pc_t[:, 4 + r_:5 + r_]
                    V(lambda r_=r_: nc.vector.tensor_scalar(out=coef[:], in0=xg_t[:, r_, 768:772], scalar1=-1.0, scalar2=m_r,
                                                            op0=ALU.add, op1=ALU.mult), r=["xg", "pc"], w=["xcoef"])
                    V(lambda: nc.vector.tensor_scalar(out=coef[:], in0=coef[:], scalar1=1.0, scalar2=None, op0=ALU.add), r=["xcoef"], w=["xcoef"])
                    for hh in range(4):
                        V(lambda r_=r_, hh=hh: nc.vector.tensor_scalar(out=tmpS[:], in0=xg_t[:, r_, 256 + hh * 128:256 + (hh + 1) * 128],
                                                                       scalar1=m_r, scalar2=None, op0=ALU.mult), r=["xg", "pc"], w=["xtmpS"])
                        V(lambda hh=hh: nc.vector.scalar_tensor_tensor(out=S0[:, hh, :], in0=S0[:, hh, :], scalar=coef[:, hh:hh + 1],
                                                                       in1=tmpS[:], op0=ALU.mult, op1=ALU.add), r=["xtmpS", "xcoef", "S0"], w=["S0"])
                S.barrier()

            with contextlib.ExitStack() as st2:
                bias_t = P.sb("s_bias", [128, 2, 256], stack=st2)
                S.dma("sp", lambda: nc.sync.dma_start(out=bias_t[:], in_=swab[:, :, :]), writes=["sbias"])
                s_sb = [P.sb(f"s_s{i}", [128, 4, 256], stack=st2) for i in range(1)] * 2
                pT = [P.sb(f"s_pT{i}", [128, 8, 128], BF16, stack=st2) for i in range(1)] * 2
                sm = [P.sb(f"s_m{i}", [128, 16], stack=st2) for i in range(2)]
                it = 0
                for blk in range(16):
                    for kvh in range(2):
                        u = it % 2
                        it += 1
                        ss, pp, mm = s_sb[u], pT[u], sm[u]
                        sk, pk, mk = "ss0", "spT0", f"sm{u}"
                        h0 = kvh * 4
                        for g in range(4):
                            pi = g // 2
                            T(lambda g=g, pi=pi: nc.tensor.matmul(
                                pbank[pi][:, (g % 2) * 256:(g % 2 + 1) * 256],
                                qT[64 * kvh:64 * kvh + 64, g, blk * 128:(blk + 1) * 128],
                                kT[64 * kvh:64 * kvh + 64, blk * 128:(blk + 2) * 128], start=True, stop=True),
                              r=["qT", "kT"], w=[f"pb{pi}"])
                        for g in range(4):
                            V(lambda g=g: nc.vector.scalar_tensor_tensor(
                                out=ss[:, g, :], in0=bias_t[:, 1 if blk == 0 else 0, :], scalar=-(2.0 ** (-(h0 + g + 1))),
                                in1=pbank[g // 2][:, (g % 2) * 256:(g % 2 + 1) * 256], op0=ALU.mult, op1=ALU.add),
                              r=[f"pb{g // 2}", "sbias"], w=[sk])
                        V(lambda: nc.vector.tensor_reduce(out=mm[:, 0:4], in_=ss[:], axis=AX.X, op=ALU.max), r=[sk], w=[mk])
                        V(lambda: nc.vector.tensor_tensor(out=mm[:, 0:4], in0=mm[:, 0:4], in1=sink_t[:, h0:h0 + 4], op=ALU.max), r=[mk, "sink"], w=[mk])
                        V(lambda: nc.vector.tensor_tensor(out=mm[:, 8:12], in0=sink_t[:, h0:h0 + 4], in1=mm[:, 0:4], op=ALU.subtract), r=[mk, "sink"], w=[mk])
                        V(lambda: nc.vector.tensor_scalar(out=mm[:, 0:4], in0=mm[:, 0:4], scalar1=-1.0, scalar2=None, op0=ALU.mult), r=[mk], w=[mk])
                        for g in range(4):
                            A(lambda g=g: nc.scalar.activation(out=ss[:, g, :], in_=ss[:, g, :], func=AF.Exp, bias=mm[:, g:g + 1],
                                                               accum_out=mm[:, 4 + g:5 + g]), r=[sk, mk], w=[sk, mk])
                        A(lambda: nc.scalar.activation(out=mm[:, 8:12], in_=mm[:, 8:12], func=AF.Exp), r=[mk], w=[mk])
                        V(lambda: nc.vector.tensor_tensor(out=mm[:, 8:12], in0=mm[:, 8:12], in1=mm[:, 4:8], op=ALU.add), r=[mk], w=[mk])
                        V(lambda: nc.vector.reciprocal(out=mm[:, 8:12], in_=mm[:, 8:12]), r=[mk], w=[mk])
                        V(lambda: nc.vector.tensor_tensor(out=ss[:], in0=ss[:], in1=mm[:, 8:12].unsqueeze(2).to_broadcast([128, 4, 256]), op=ALU.mult),
                          r=[sk, mk], w=[sk])
                        for g in range(4):
                            for half in range(2):
                                pi = 2 + g // 2
                                j = (g % 2) * 2 + half
                                T(lambda g=g, half=half, pi=pi, j=j: nc.tensor.transpose(
                                    pbank[pi][:, j * 128:(j + 1) * 128], ss[:, g, half * 128:(half + 1) * 128], ident),
                                  r=[sk, "cst"], w=[f"pb{pi}"])
                        for pi in range(2):
                            A(lambda pi=pi: nc.scalar.copy(out=pp[:, 4 * pi:4 * pi + 4, :], in_=pbank[2 + pi][:, :].rearrange("p (a k) -> p a k", k=128)),
                              r=[f"pb{2 + pi}"], w=[pk])
                        po = 4 + kvh
                        for g in range(4):
                            for half in range(2):
                                T(lambda g=g, half=half: nc.tensor.matmul(
                                    pbank[po][64 * (g % 2):64 * (g % 2) + 64, (g // 2) * 128:(g // 2 + 1) * 128],
                                    vT[:, blk + half, 64 * kvh:64 * kvh + 64], pp[:, g * 2 + half, :],
                                    start=(half == 0), stop=(half == 1)), r=["vT", pk], w=[f"pb{po}"])
                        A(lambda: nc.scalar.copy(out=o_a[:, 2 * kvh:2 * kvh + 2, blk * 128:(blk + 1) * 128],
                                                 in_=pbank[po][:, 0:256].rearrange("p (a k) -> p a k", k=128)), r=[f"pb{po}"], w=["o_a"])
                S.barrier()

            st_swa.close()
            NTK = 512
            with contextlib.ExitStack() as st2:
                QS = P.sb("b_QS", [128, NTK], stack=st2)
                F_ = P.sb("b_F", [128, NTK], stack=st2)
                AC = P.sb("b_AC", [128, NTK], stack=st2)
                EA = P.sb("b_EA", [128, NTK], stack=st2)
                QD = P.sb("b_QD", [128, NTK], stack=st2)
                KD = P.sb("b_KD", [128, NTK], stack=st2)
                KC = P.sb("b_KC", [128, NTK], stack=st2)
                GS = P.sb("b_GS", [128, NTK], BF16, stack=st2)
                OO = P.sb("b_OO", [128, NTK], stack=st2)
                SQ = P.sb("b_SQ", [128, NTK], BF16, stack=st2)
                DCH = P.sb("b_DCH", [128, NTK // 16], stack=st2)
                vtm = P.sb("b_v", [128, 4, 128], stack=st2)
                kdm = P.sb("b_kdm", [128, 8, 128], stack=st2)
                scm = P.sb("b_scm", [128, 128], stack=st2)
                SR = P.sb("b_SR", [128, 16, 128], stack=st2)
                load_w(2, 0)
                gc = 0
                for hh in range(4):
                    b = hh % 2
                    if hh + 1 < 4:
                        load_w(3 + hh, (hh + 1) % 2)
                    V(lambda hh=hh: nc.vector.tensor_copy(out=SR[:, gc % 16, :], in_=S0[:, hh, :]), r=["S0"], w=[f"SR{gc % 16}"])
                    for qt in range(TP // NTK):
                        t0 = qt * NTK
                        proj_fm(0, NTK, b, 0, t0)
                        A(lambda: nc.scalar.activation(out=QS[:], in_=pbank[0][:, :], func=AF.Silu), r=["pb0"], w=["QS"])
                        proj_fm(1, NTK, b, 128, t0)
                        A(lambda: nc.scalar.activation(out=F_[:], in_=pbank[1][:, :], func=AF.Sigmoid), r=["pb1"], w=["F"])
                        proj_fm(0, NTK, b, 256, t0)
                        A(lambda: nc.scalar.activation(out=GS[:], in_=pbank[0][:, :], func=AF.Silu), r=["pb0"], w=["GS"])
                        for blk in range(4):
                            proj_tm(2 + blk % 2, b, 384, 128, t0 + blk * 128)
                            V(lambda blk=blk: nc.vector.tensor_copy(out=vtm[:, blk, :], in_=pbank[2 + blk % 2][:, 0:128]),
                              r=[f"pb{2 + blk % 2}"], w=[f"bv{blk}"])
                        V(lambda: nc.vector.tensor_scalar(out=F_[:], in0=F_[:], scalar1=oml[:, hh:hh + 1], scalar2=lbv[:, hh:hh + 1],
                                                          op0=ALU.mult, op1=ALU.add), r=["F", "oml", "lbv"], w=["F"])
                        A(lambda: nc.scalar.activation(out=AC[:], in_=F_[:], func=AF.Ln), r=["F"], w=["AC"])
                        V(lambda: nc.vector.tensor_scalar(out=F_[:], in0=F_[:], scalar1=-1.0, scalar2=1.0, op0=ALU.mult, op1=ALU.add), r=["F"], w=["F"])
                        V(lambda: nc.vector.tensor_tensor_scan(out=AC[:], data0=rmask[:, 0:NTK], data1=AC[:], initial=0.0,
                                                               op0=ALU.mult, op1=ALU.add), r=["AC", "cst"], w=["AC"])
                        A(lambda: nc.scalar.activation(out=EA[:], in_=AC[:], func=AF.Exp), r=["AC"], w=["EA"])
                        V(lambda: nc.vector.tensor_tensor(out=QD[:], in0=QS[:], in1=EA[:], op=ALU.mult), r=["QS", "EA"], w=["QD"])
                        A(lambda: nc.scalar.activation(out=EA[:], in_=AC[:], func=AF.Exp, scale=-1.0), r=["AC", "QD"], w=["EA"])
                        V(lambda: nc.vector.tensor_tensor(out=KD[:], in0=F_[:], in1=EA[:], op=ALU.mult), r=["F", "EA"], w=["KD"])
                        ACv = AC[:].rearrange("p (c j) -> p c j", j=16)
                        V(lambda: nc.vector.tensor_tensor(out=KC[:].rearrange("p (c j) -> p c j", j=16),
                                                          in0=ACv[:, :, 15:16].to_broadcast([128, NTK // 16, 16]), in1=ACv, op=ALU.subtract),
                          r=["AC"], w=["KC"])
                        A(lambda: nc.scalar.activation(out=KC[:], in_=KC[:], func=AF.Exp), r=["KC"], w=["KC"])
                        V(lambda: nc.vector.tensor_tensor(out=KC[:], in0=KC[:], in1=F_[:], op=ALU.mult), r=["KC", "F"], w=["KC"])
                        A(lambda: nc.scalar.activation(out=DCH[:].unsqueeze(2), in_=ACv[:, :, 15:16], func=AF.Exp), r=["AC"], w=["DCH"])
                        for blk in range(4):
                            cs_ = slice(blk * 128, (blk + 1) * 128)
                            T(lambda: nc.tensor.transpose(pbank[4][:, 0:128], KC[:, cs_], ident), r=["KC", "cst"], w=["pb4"])
                            for c in range(8):
                                A(lambda c=c: nc.scalar.activation(out=kdm[:, c, :], in_=pbank[4][:, 0:128], func=AF.Copy, scale=chunkmask[:, c:c + 1]),
                                  r=["pb4", "cst"], w=[f"kdm{c}"])
                            T(lambda: nc.tensor.matmul(pbank[5][:, 0:128], KD[:, cs_], QD[:, cs_], start=True, stop=True), r=["KD", "QD"], w=["pb5"])
                            V(lambda: nc.vector.tensor_tensor(out=scm[:], in0=pbank[5][:, 0:128], in1=mask16T, op=ALU.mult), r=["pb5", "cst"], w=["scm"])
                            for c in range(8):
                                T(lambda c=c: nc.tensor.matmul(pbank[6 + c // 4][:, (c % 4) * 128:(c % 4 + 1) * 128], kdm[:, c, :], vtm[:, blk, :],
                                                               start=True, stop=True), r=[f"kdm{c}", f"bv{blk}"], w=[f"pb{6 + c // 4}"])
                            gcs = []
                            for c in range(8):
                                cur, nxt = gc % 16, (gc + 1) % 16
                                gcs.append(cur)
                                V(lambda c=c, cur=cur, nxt=nxt: nc.vector.scalar_tensor_tensor(
                                    out=SR[:, nxt, :], in0=SR[:, cur, :], scalar=DCH[:, blk * 8 + c:blk * 8 + c + 1],
                                    in1=pbank[6 + c // 4][:, (c % 4) * 128:(c % 4 + 1) * 128], op0=ALU.mult, op1=ALU.add),
                                  r=[f"SR{cur}", "DCH", f"pb{6 + c // 4}"], w=[f"SR{nxt}"])
                                gc += 1
                            po = 2 + blk % 2
                            T(lambda: nc.tensor.matmul(pbank[po][:, 0:128], vtm[:, blk, :], scm[:], start=True, stop=False),
                              r=[f"bv{blk}", "scm"], w=[f"pb{po}"])
                            for c in range(8):
                                T(lambda c=c: nc.tensor.matmul(pbank[po][:, 16 * c:16 * c + 16], SR[:, gcs[c], :],
                                                               QD[:, blk * 128 + 16 * c:blk * 128 + 16 * c + 16], start=False, stop=(c == 7)),
                                  r=[f"SR{gcs[c]}", "QD"], w=[f"pb{po}"])
                            A(lambda: nc.scalar.copy(out=OO[:, cs_], in_=pbank[po][:, 0:128]), r=[f"pb{po}"], w=["OO"])
                        A(lambda: nc.scalar.activation(out=SQ[:], in_=OO[:], func=AF.Square), r=["OO"], w=["SQ"])
                        T(lambda: nc.tensor.matmul(pbank[5][:, 0:NTK], ones128[:], SQ[:], start=True, stop=True), r=["SQ", "ones128"], w=["pb5"])
                        V(lambda: nc.vector.tensor_scalar(out=EA[:], in0=pbank[5][:, 0:NTK], scalar1=1e-6, scalar2=None, op0=ALU.add), r=["pb5"], w=["EA"])
                        A(lambda: nc.scalar.activation(out=EA[:], in_=EA[:], func=AF.Ln), r=["EA"], w=["EA"])
                        A(lambda: nc.scalar.activation(out=EA[:], in_=EA[:], func=AF.Exp, scale=-0.5), r=["EA"], w=["EA"])
                        V(lambda: nc.vector.tensor_tensor(out=OO[:], in0=OO[:], in1=EA[:], op=ALU.mult), r=["OO", "EA"], w=["OO"])
                        V(lambda: nc.vector.scalar_tensor_tensor(out=o_b[:, hh, t0:t0 + NTK], in0=OO[:], scalar=hg_t[:, 0:1], in1=GS[:],
                                                                 op0=ALU.mult, op1=ALU.mult), r=["OO", "GS", "hg"], w=["o_b"])
                    S.dma("sp", lambda hh=hh: nc.sync.dma_start(out=o_hgrn[:, hh, :], in_=SR[:, gc % 16, :]), reads=[f"SR{gc % 16}"])
                S.barrier()

            with contextlib.ExitStack() as st2:
                hms = P.sb("q_hms", [128, DC, NS], BF16, stack=st2)
                t16 = P.sb("q_t16", [128, NS], stack=st2)
                for c in range(DC):
                    V(lambda c=c: nc.vector.tensor_tensor(out=t16[:], in0=xs[:, c, :], in1=mods[:, l, jsc * 8 + c, 1:1 + NS], op=ALU.mult),
                      r=["xs0", "mods"], w=["t16"])
                    V(lambda c=c: nc.vector.tensor_tensor(out=hms[:, c, :], in0=t16[:], in1=mods[:, l, jsh * 8 + c, 1:1 + NS], op=ALU.add),
                      r=["t16", "mods"], w=["hms"])
                qtm = P.sb("q_qtm", [NS, 512], stack=st2)
                kvn = P.sb("q_kvn", [NS, 256], stack=st2)
                hq = P.sb("q_hq", [128, 4, 3, NS], stack=st2)
                hi = P.sb("q_hi", [NS, 4, 128], stack=st2)

                def sproj_tm(pi, b, col0, ncols):
                    for c in range(DC):
                        T(lambda c=c: nc.tensor.matmul(pbank[pi][0:NS, 0:ncols], hms[:, c, :], wbuf[b][:, c, col0:col0 + ncols],
                                                       start=(c == 0), stop=(c == DC - 1)), r=[f"mw{b}", "hms"], w=[f"pb{pi}"])

                def sproj_fm(pi, pc0, b, col0):
                    for c in range(DC):
                        T(lambda c=c: nc.tensor.matmul(pbank[pi][:, pc0:pc0 + NS], wbuf[b][:, c, col0:col0 + 128], hms[:, c, :],
                                                       start=(c == 0), stop=(c == DC - 1)), r=[f"mw{b}", "hms"], w=[f"pb{pi}"])

                load_w(0, 0)
                load_w(1, 1)
                sproj_tm(0, 0, 0, 512)
                A(lambda: nc.scalar.mul(out=qtm[:], in_=pbank[0][0:NS, 0:512], mul=0.125), r=["pb0"], w=["qtm"])
                sproj_tm(1, 1, 0, 256)
                V(lambda: nc.vector.tensor_copy(out=kvn[:], in_=pbank[1][0:NS, 0:256]), r=["pb1"], w=["kvn"])
                load_w(2, 0)
                for hh in range(4):
                    b = hh % 2
                    if hh + 1 < 4:
                        load_w(3 + hh, (hh + 1) % 2)
                    for j3 in range(3):
                        sproj_fm(2, j3 * NS, b, j3 * 128)
                    A(lambda hh=hh: nc.scalar.activation(out=hq[:, hh, 0, :], in_=pbank[2][:, 0:NS], func=AF.Silu), r=["pb2"], w=["hq"])
                    A(lambda hh=hh: nc.scalar.activation(out=hq[:, hh, 1, :], in_=pbank[2][:, NS:2 * NS], func=AF.Sigmoid), r=["pb2"], w=["hq"])
                    A(lambda hh=hh: nc.scalar.activation(out=hq[:, hh, 2, :], in_=pbank[2][:, 2 * NS:3 * NS], func=AF.Silu), r=["pb2"], w=["hq"])
                    sproj_tm(3, b, 384, 128)
                    V(lambda hh=hh: nc.vector.tensor_copy(out=hi[:, hh, :], in_=pbank[3][0:NS, 0:128]), r=["pb3"], w=["hi"])
                S.dma("sp", lambda: nc.sync.dma_start(out=o_sk[:, 0:127, :], in_=ck_in[:, 1:128, :]))
                S.dma("sp", lambda: nc.sync.dma_start(out=o_sv[:, 0:127, :], in_=cv_in[:, 1:128, :]))
                S.dma("sp", lambda: nc.sync.dma_start(out=o_sk[:, 127, :], in_=kvn[:, 0:128]), reads=["kvn"])
                S.dma("sp", lambda: nc.sync.dma_start(out=o_sv[:, 127, :], in_=kvn[:, 128:256]), reads=["kvn"])

                with contextlib.ExitStack() as st3:
                    ck = P.sb("d_ck", [128, 16, 128], stack=st3)
                    cv = P.sb("d_cv", [128, 16, 128], stack=st3)
                    dd = P.sb("d_dd", [128, 64], stack=st3)
                    S.dma("sp", lambda: nc.sync.dma_start(out=ck[:], in_=ck_in.rearrange("s (kb j) f -> (s kb) j f", j=16)), writes=["ck"])
                    S.dma("sp", lambda: nc.sync.dma_start(out=cv[:], in_=cv_in.rearrange("s (kb j) f -> (s kb) j f", j=16)), writes=["cv"])
                    S.dma("sp", lambda: nc.sync.dma_start(out=dd[:], in_=dec_tab[:, :]), writes=["dd"])
                    rep = P.sb("d_rep", [NS, 128], stack=st3)
                    S.dma("sp", lambda: nc.sync.dma_start(out=rep[:], in_=dec_rep[:, :]), writes=["rep"])
                    qbc = P.sb("d_qbc", [128, 512], stack=st3)
                    T(lambda: nc.tensor.matmul(pbank[0][:, 0:512], rep[:], qtm[:], start=True, stop=True), r=["rep", "qtm"], w=["pb0"])
                    V(lambda: nc.vector.tensor_copy(out=qbc[:], in_=pbank[0][:, 0:512]), r=["pb0"], w=["qbc"])
                    tmp = P.sb("d_tmp", [128, 16, 64], stack=st3)
                    sc = P.sb("d_sc", [128, 8, 16], stack=st3)
                    op_ = P.sb("d_op", [128, 8, 64], stack=st3)
                    sm = P.sb("d_sm", [128, 32], stack=st3)
                    for hd in range(8):
                        g, kvh = hd // 2, hd % 2
                        V(lambda hd=hd, kvh=kvh: nc.vector.tensor_tensor(
                            out=tmp[:], in0=ck[:, :, kvh * 64:(kvh + 1) * 64],
                            in1=qbc[:, hd * 64:(hd + 1) * 64].unsqueeze(1).to_broadcast([128, 16, 64]), op=ALU.mult), r=["ck", "qbc"], w=["dtmp"])
                        V(lambda hd=hd: nc.vector.tensor_reduce(out=sc[:, hd, :], in_=tmp[:], axis=AX.X, op=ALU.add), r=["dtmp"], w=["dsc"])
                        V(lambda hd=hd, g=g, kvh=kvh: nc.vector.scalar_tensor_tensor(
                            out=sc[:, hd, :], in0=dd[:, 0:16], scalar=-(2.0 ** (-(kvh * 4 + g + 1))), in1=sc[:, hd, :],
                            op0=ALU.mult, op1=ALU.add), r=["dsc", "dd"], w=["dsc"])
                    V(lambda: nc.vector.tensor_reduce(out=sm[:, 0:8], in_=sc[:], axis=AX.X, op=ALU.max), r=["dsc"], w=["dsm"])
                    V(lambda: nc.vector.tensor_tensor(out=sm[:, 0:8], in0=sm[:, 0:8], in1=dd[:, 32:40], op=ALU.max), r=["dsm", "dd"], w=["dsm"])
                    V(lambda: nc.vector.tensor_tensor(out=sc[:], in0=sc[:], in1=sm[:, 0:8].unsqueeze(2).to_broadcast([128, 8, 16]), op=ALU.subtract),
                      r=["dsc", "dsm"], w=["dsc"])
                    A(lambda: nc.scalar.activation(out=sc[:], in_=sc[:], func=AF.Exp), r=["dsc"], w=["dsc"])
                    V(lambda: nc.vector.tensor_reduce(out=sm[:, 8:16], in_=sc[:], axis=AX.X, op=ALU.add), r=["dsc"], w=["dsm"])
                    for hd in range(8):
                        kvh = hd % 2
                        V(lambda hd=hd, kvh=kvh: nc.vector.tensor_tensor(
                            out=tmp[:].rearrange("p j d -> p d j"), in0=cv[:, :, kvh * 64:(kvh + 1) * 64].rearrange("p j d -> p d j"),
                            in1=sc[:, hd, :].unsqueeze(1).to_broadcast([128, 64, 16]), op=ALU.mult), r=["cv", "dsc"], w=["dtmp"])
                        V(lambda hd=hd: nc.vector.tensor_reduce(out=op_[:, hd, :], in_=tmp[:].rearrange("p j d -> p d j"), axis=AX.X, op=ALU.add),
                          r=["dtmp"], w=["dop"])
                    sn = P.sb("d_sn", [NS, 64], stack=st3)
                    tn = P.sb("d_tn", [NS, 512], stack=st3)
                    V(lambda: nc.vector.tensor_tensor(
                        out=tn[:].rearrange("p (g k d) -> p g k d", g=4, k=2), in0=qtm[:].rearrange("p (g k d) -> p g k d", g=4, k=2),
                        in1=kvn[:, 0:128].rearrange("p (k d) -> p k d", k=2).unsqueeze(1).to_broadcast([NS, 4, 2, 64]), op=ALU.mult),
                      r=["qtm", "kvn"], w=["dtn"])
                    V(lambda: nc.vector.tensor_reduce(out=sn[:, 0:8], in_=tn[:].rearrange("p (h d) -> p h d", d=64), axis=AX.X, op=ALU.add),
                      r=["dtn"], w=["dsn"])
                    mT = P.sb("d_mT", [8, NS, 8], stack=st3)
                    m2 = P.sb("d_m2", [8, 64], stack=st3)
                    T(lambda: nc.tensor.transpose(pbank[1][0:8, 0:128], sm[:, 0:8], ident), r=["dsm", "cst"], w=["pb1"])
                    V(lambda: nc.vector.tensor_copy(out=mT[:].rearrange("p s k -> p (s k)"), in_=pbank[1][0:8, 0:128]), r=["pb1"], w=["dmT"])
                    T(lambda: nc.tensor.transpose(pbank[2][0:8, 0:NS], sn[:, 0:8], ident[0:NS, 0:NS]), r=["dsn", "cst"], w=["pb2"])
                    V(lambda: nc.vector.tensor_copy(out=m2[:, 16:32], in_=pbank[2][0:8, 0:NS]), r=["pb2"], w=["dm2"])
                    V(lambda: nc.vector.tensor_reduce(out=m2[:, 0:16], in_=mT[:], axis=AX.X, op=ALU.max), r=["dmT"], w=["dm2"])
                    V(lambda: nc.vector.tensor_tensor(out=m2[:, 0:16], in0=m2[:, 0:16], in1=m2[:, 16:32], op=ALU.max), r=["dm2"], w=["dm2"])
                    V(lambda: nc.vector.tensor_tensor(out=mT[:], in0=mT[:], in1=m2[:, 0:16].unsqueeze(2).to_broadcast([8, NS, 8]), op=ALU.subtract),
                      r=["dmT", "dm2"], w=["dmT"])
                    A(lambda: nc.scalar.activation(out=mT[:], in_=mT[:], func=AF.Exp), r=["dmT"], w=["dmT"])
                    V(lambda: nc.vector.tensor_tensor(out=m2[:, 32:48], in0=m2[:, 16:32], in1=m2[:, 0:16], op=ALU.subtract), r=["dm2"], w=["dm2"])
                    A(lambda: nc.scalar.activation(out=m2[:, 32:48], in_=m2[:, 32:48], func=AF.Exp), r=["dm2"], w=["dm2"])
                    A(lambda: nc.scalar.activation(out=m2[:, 48:64], in_=m2[:, 0:16], func=AF.Exp, scale=-1.0, bias=dd[0:8, 40:41]),
                      r=["dm2", "dd"], w=["dm2"])
                    T(lambda: nc.tensor.transpose(pbank[1][:, 0:8], mT[:].rearrange("p s k -> p (s k)"), ident[0:8, 0:8]), r=["dmT", "cst"], w=["pb1"])
                    V(lambda: nc.vector.tensor_copy(out=sm[:, 16:24], in_=pbank[1][:, 0:8]), r=["pb1"], w=["dsm"])
                    V(lambda: nc.vector.tensor_tensor(out=sm[:, 8:16], in0=sm[:, 8:16], in1=sm[:, 16:24], op=ALU.mult), r=["dsm"], w=["dsm"])
                    V(lambda: nc.vector.tensor_tensor(out=op_[:], in0=op_[:], in1=sm[:, 16:24].unsqueeze(2).to_broadcast([128, 8, 64]), op=ALU.mult),
                      r=["dop", "dsm"], w=["dop"])
                    T(lambda: nc.tensor.matmul(pbank[2][0:NS, 0:512], dd[:, 16:32], op_[:].rearrange("p h d -> p (h d)"), start=True, stop=True),
                      r=["dd", "dop"], w=["pb2"])
                    T(lambda: nc.tensor.matmul(pbank[3][0:NS, 0:8], dd[:, 16:32], sm[:, 8:16], start=True, stop=True), r=["dd", "dsm"], w=["pb3"])
                    T(lambda: nc.tensor.transpose(pbank[4][0:NS, 0:8], m2[:, 32:48], ident[0:8, 0:8]), r=["dm2", "cst"], w=["pb4"])
                    T(lambda: nc.tensor.transpose(pbank[4][0:NS, 8:16], m2[:, 48:64], ident[0:8, 0:8]), r=["dm2", "cst"], w=["pb4"])
                    V(lambda: nc.vector.tensor_copy(out=sn[:, 8:24], in_=pbank[4][0:NS, 0:16]), r=["pb4"], w=["dsn"])
                    V(lambda: nc.vector.tensor_tensor(out=sn[:, 24:32], in0=sn[:, 8:16], in1=sn[:, 16:24], op=ALU.add), r=["dsn"], w=["dsn"])
                    V(lambda: nc.vector.tensor_tensor(out=sn[:, 24:32], in0=sn[:, 24:32], in1=pbank[3][0:NS, 0:8], op=ALU.add), r=["dsn", "pb3"], w=["dsn"])
                    V(lambda: nc.vector.reciprocal(out=sn[:, 24:32], in_=sn[:, 24:32]), r=["dsn"], w=["dsn"])
                    V(lambda: nc.vector.tensor_tensor(
                        out=tn[:].rearrange("p (g k d) -> p g k d", g=4, k=2),
                        in0=kvn[:, 128:256].rearrange("p (k d) -> p k d", k=2).unsqueeze(1).to_broadcast([NS, 4, 2, 64]),
                        in1=sn[:, 8:16].rearrange("p (g k) -> p g k", k=2).unsqueeze(3).to_broadcast([NS, 4, 2, 64]), op=ALU.mult),
                      r=["kvn", "dsn"], w=["dtn"])
                    V(lambda: nc.vector.tensor_tensor(out=tn[:], in0=tn[:], in1=pbank[2][0:NS, 0:512], op=ALU.add), r=["dtn", "pb2"], w=["dtn"])
                    V(lambda: nc.vector.tensor_tensor(out=tn[:].rearrange("p (h d) -> p h d", d=64), in0=tn[:].rearrange("p (h d) -> p h d", d=64),
                                                      in1=sn[:, 24:32].unsqueeze(2).to_broadcast([NS, 8, 64]), op=ALU.mult), r=["dtn", "dsn"], w=["dtn"])
                    on = P.sb("d_on", [NS, 512], stack=st3)
                    for kvh in range(2):
                        V(lambda kvh=kvh: nc.vector.tensor_copy(
                            out=on[:, kvh * 256:(kvh + 1) * 256].rearrange("p (g d) -> p g d", g=4),
                            in_=tn[:].rearrange("p (g k d) -> p g k d", g=4, k=2)[:, :, kvh, :]), r=["dtn"], w=["don"])
                    for kc in range(4):
                        T(lambda kc=kc: nc.tensor.transpose(pbank[5][:, kc * NS:(kc + 1) * NS], on[:, kc * 128:(kc + 1) * 128], ident[0:NS, 0:NS]),
                          r=["don", "cst"], w=["pb5"])
                    V(lambda: nc.vector.tensor_copy(out=o_as[:], in_=pbank[5][:, 0:4 * NS].rearrange("p (k s) -> p k s", s=NS)), r=["pb5"], w=["o_as"])
                    S.barrier()

                with contextlib.ExitStack() as st3:
                    Sin = P.sb("g_Sin", [128, NS, 128], stack=st3)
                    Sout = P.sb("g_Sout", [128, NS, 128], stack=st3)
                    tSa = P.sb("g_tSa", [128, NS, 128], stack=st3)
                    ob = P.sb("g_ob", [128, 4, NS], stack=st3)
                    kk = P.sb("g_kk", [128, 4, NS], stack=st3)
                    for hh in range(4):
                        V(lambda hh=hh: nc.vector.tensor_scalar(out=hq[:, hh, 1, :], in0=hq[:, hh, 1, :], scalar1=oml[:, hh:hh + 1],
                                                                scalar2=lbv[:, hh:hh + 1], op0=ALU.mult, op1=ALU.add), r=["hq", "oml", "lbv"], w=["hq"])
                        V(lambda hh=hh: nc.vector.tensor_scalar(out=kk[:, hh, :], in0=hq[:, hh, 1, :], scalar1=-1.0, scalar2=1.0,
                                                                op0=ALU.mult, op1=ALU.add), r=["hq"], w=["gkk"])
                    for hh in range(4):
                        S.dma("sp", lambda hh=hh: nc.sync.dma_start(out=Sin[:], in_=sh_in[:, hh, :, :].rearrange("s k v -> k s v")), writes=["Sin"])
                        for s_ in range(NS):
                            pi = s_ // 4
                            T(lambda s_=s_, pi=pi: nc.tensor.matmul(pbank[pi][:, (s_ % 4) * 128:(s_ % 4 + 1) * 128], ident[0:NS, s_:s_ + 1].to_broadcast([NS, 128]),
                                                                    hi[:, hh, :], start=True, stop=True), r=["hi", "cst"], w=[f"pb{pi}"])
                        for s_ in range(NS):
                            pi = s_ // 4
                            V(lambda s_=s_, pi=pi: nc.vector.tensor_scalar(out=tSa[:, s_, :], in0=pbank[pi][:, (s_ % 4) * 128:(s_ % 4 + 1) * 128], scalar1=kk[:, hh, s_:s_ + 1],
                                                                           scalar2=None, op0=ALU.mult), r=[f"pb{pi}", "gkk"], w=[f"tS{s_}"])
                        for s_ in range(NS):
                            V(lambda s_=s_: nc.vector.scalar_tensor_tensor(out=Sout[:, s_, :], in0=Sin[:, s_, :], scalar=hq[:, hh, 1, s_:s_ + 1],
                                                                           in1=tSa[:, s_, :], op0=ALU.mult, op1=ALU.add),
                              r=["Sin", "hq", f"tS{s_}"], w=[f"Sout{s_}"])
                        for s_ in range(NS):
                            T(lambda s_=s_: nc.tensor.matmul(pbank[5][:, hh * NS + s_:hh * NS + s_ + 1], Sout[:, s_, :], hq[:, hh, 0, s_:s_ + 1],
                                                             start=True, stop=True), r=[f"Sout{s_}", "hq"], w=["pb5"])
                        S.dma("sp", lambda hh=hh: nc.sync.dma_start(out=o_sh[:, hh, :, :].rearrange("s k v -> k s v"), in_=Sout[:]),
                              reads=[f"Sout{s_}" for s_ in range(NS)])
                    V(lambda: nc.vector.tensor_copy(out=ob[:], in_=pbank[5][:, 0:4 * NS].rearrange("p (h s) -> p h s", s=NS)), r=["pb5"], w=["gob"])
                    sqs = P.sb("g_sq", [128, 4 * NS], BF16, stack=st3)
                    rs_ = P.sb("g_rs", [128, 4 * NS], stack=st3)
                    A(lambda: nc.scalar.activation(out=sqs[:], in_=ob[:].rearrange("p h s -> p (h s)"), func=AF.Square), r=["gob"], w=["gsq"])
                    T(lambda: nc.tensor.matmul(pbank[4][:, 0:4 * NS], ones128[:], sqs[:], start=True, stop=True), r=["gsq", "ones128"], w=["pb4"])
                    V(lambda: nc.vector.tensor_scalar(out=rs_[:], in0=pbank[4][:, 0:4 * NS], scalar1=1e-6, scalar2=None, op0=ALU.add), r=["pb4"], w=["grs"])
                    A(lambda: nc.scalar.activation(out=rs_[:], in_=rs_[:], func=AF.Ln), r=["grs"], w=["grs"])
                    A(lambda: nc.scalar.activation(out=rs_[:], in_=rs_[:], func=AF.Exp, scale=-0.5), r=["grs"], w=["grs"])
                    V(lambda: nc.vector.tensor_tensor(out=rs_[:], in0=rs_[:], in1=ob[:].rearrange("p h s -> p (h s)"), op=ALU.mult), r=["grs", "gob"], w=["grs"])
                    V(lambda: nc.vector.scalar_tensor_tensor(out=o_bs[:], in0=rs_[:].rearrange("p (h s) -> p h s", s=NS), scalar=hg_t[:, 0:1],
                                                             in1=hq[:, :, 2, :], op0=ALU.mult, op1=ALU.mult), r=["grs", "hq", "hg"], w=["o_bs"])
                    S.barrier()

            st_hm.close()
            with contextlib.ExitStack() as st2:
                wo = P.sb("o_wo", [128, 8, D], BF16, stack=st2)
                S.dma("pool", lambda: nc.gpsimd.dma_start(out=wo[:], in_=wout0[:, :, :]), writes=["wo"])
                xb = P.sb("o_xb", [128, DC, TT], BF16, stack=st2)
                sq = P.sb("o_sq", [128, DC, TT], BF16, stack=st2)
                r = [P.sb(f"o_r{i}", [128, TT], stack=st2) for i in range(2)]
                lt = [P.sb(f"o_lt{i}", [128, TT], stack=st2) for i in range(4)]
                for (kind, t0, n) in tiles:
                    xv = xview(kind, t0, n)
                    xk = f"x{kind}{t0}"
                    for co in range(DC):
                        pd = pbank[co % 2]
                        for kc in range(8):
                            if kind == "p":
                                src = o_a[:, kc, t0:t0 + n] if kc < 4 else o_b[:, kc - 4, t0:t0 + n]
                            else:
                                src = o_as[:, kc, :] if kc < 4 else o_bs[:, kc - 4, :]
                            T(lambda kc=kc, src=src: nc.tensor.matmul(pd[:, :n], wo[:, kc, co * 128:(co + 1) * 128], src,
                                                                      start=(kc == 0), stop=(kc == 7)), r=["wo", "o_a", "o_b", "o_as", "o_bs"], w=[f"pb{co % 2}"])
                        rr = r[co % 2]
                        if kind == "p":
                            A(lambda: nc.scalar.activation(out=rr[:, :n], in_=pd[:, :n], func=AF.Copy, scale=mods[:, l, jg * 8 + co, 0:1]),
                              r=[f"pb{co % 2}", "mods"], w=[f"r{co % 2}"])
                        else:
                            V(lambda: nc.vector.tensor_tensor(out=rr[:, :n], in0=pd[:, :n], in1=mods[:, l, jg * 8 + co, 1:1 + NS], op=ALU.mult),
                              r=[f"pb{co % 2}", "mods"], w=[f"r{co % 2}"])
                        V(lambda: nc.vector.scalar_tensor_tensor(out=xv[:, co, :], in0=xv[:, co, :], scalar=DN_ALPHA, in1=rr[:, :n],
                                                                 op0=ALU.mult, op1=ALU.add), r=[f"r{co % 2}", xk, f"{xk}c{co}"], w=[f"{xk}c{co}"])
                        A(lambda: nc.scalar.copy(out=xb[:, co, :n], in_=xv[:, co, :]), r=[f"{xk}c{co}"], w=[f"xb{co}"])
                        A(lambda: nc.scalar.activation(out=sq[:, co, :n], in_=xv[:, co, :], func=AF.Square), r=[f"{xk}c{co}"], w=[f"sq{co}"])
                    layer_norm_tile(xk, xv, n, xb, sq, lt, (l * 3 + 1) * 8)
                S.barrier()

    win1 = P.din("win1_r", [9, 128, DC, 512])
    wout1 = P.din("wout1_r", [128, 8, D])
    convw = P.din("convw_r", [128, 3, 8, 4])
    abc_in = P.din("abc", [8, 2])
    gng = P.din("gdn_g", [128, 1])
    tri_in = P.din("tri", [128, 1280])
    xsrc1 = nc.dram_tensor("xch_src1", [128, 32], F32)
    xdst1 = nc.dram_tensor("xch_dst1", [4 * 128, 32], F32)
    gscr = nc.dram_tensor("gdn_scr", [32, 128, 2080], F32)
    xsrc2 = nc.dram_tensor("xch_src2", [128, 8 * 256], F32)
    xdst2 = nc.dram_tensor("xch_dst2", [4 * 128, 8 * 256], F32)
    o_gdn = P.dout("o_gdn", [128, 8, 128])
    o_gconv = P.dout("o_gconv", [128, 3, 8, 3])
    sg_in = P.din("sg_in", [NS, 8, 128, 128])
    sc_in = P.din("sc_in", [NS, 3, 3072])
    o_sg = P.dout("o_sg", [NS, 8, 128, 128])
    o_sc = P.dout("o_sc", [NS, 3, 3072])

    def mixer_odd():
        l = 1
        jsh, jsc, jg = 3, 4, 5
        NTK = 512
        with contextlib.ExitStack() as st:
            pc_t = P.sb("n_pc", [128, 16], stack=st)
            cw = P.sb("n_cw", [128, 3, 8, 4], stack=st)
            abc = P.sb("n_abc", [8, 2], stack=st)
            gg_t = P.sb("n_gg", [128, 1], stack=st)
            tri = P.sb("n_tri", [128, 1280], stack=st)
            Sst = P.sb("n_S", [128, 8, 256], stack=st)
            carry = P.sb("n_carry", [128, 3, 8, 3], stack=st)
            carry0 = P.sb("n_carry0", [128, 3, 8, 3], stack=st)
            hmh = P.sb("n_hmh", [128, DC, 4], BF16, stack=st)
            hmp = P.sb("n_hmp", [128, DC, NTK], BF16, stack=st)
            hl4 = P.sb("n_hl4", [128, DC, 4], stack=st)
            wbuf = [P.sb(f"n_w{i}", [128, DC, 512], BF16, stack=st) for i in range(2)]
            wab = P.sb("n_wab", [128, DC, 16], BF16, stack=st)
            og = P.sb("n_og", [128, 8, NTK], BF16, stack=st)
            S.dma("sp", lambda: nc.sync.dma_start(out=pc_t[:], in_=pcore[:, :]), writes=["pc"])
            S.dma("sp", lambda: nc.sync.dma_start(out=cw[:], in_=convw[:, :, :, :]), writes=["cw"])
            S.dma("sp", lambda: nc.sync.dma_start(out=abc[:], in_=abc_in[:, :]), writes=["abc"])
            S.dma("sp", lambda: nc.sync.dma_start(out=gg_t[:], in_=gng[:, :]), writes=["gg"])
            S.dma("sp", lambda: nc.sync.dma_start(out=tri[:], in_=tri_in[:, :]), writes=["tri"])
            S.dma("pool", lambda: nc.gpsimd.dma_start(out=wab[:], in_=win1[8, :, :, 0:16]), writes=["wab"])
            Uincl = tri[:, 0:128]
            Ustrict = tri[:, 128:256]
            A(lambda: nc.scalar.activation(out=abc[:, 1:2], in_=abc[:, 1:2], func=AF.Exp), r=["abc"], w=["abc"])
            V(lambda: nc.vector.tensor_scalar(out=abc[:, 1:2], in0=abc[:, 1:2], scalar1=-1.0, scalar2=None, op0=ALU.mult), r=["abc"], w=["abc"])
            def make_hm(t0):
                for c in range(DC):
                    V(lambda c=c: nc.vector.tensor_scalar(
                        out=hmp[:, c, :], in0=xp[:, c, t0:t0 + NTK],
                        scalar1=mods[:, l, jsc * 8 + c, 0:1], scalar2=mods[:, l, jsh * 8 + c, 0:1],
                        op0=ALU.mult, op1=ALU.add), r=["mods", f"xp{t0}"], w=["hm"])

            for c in range(DC):
                V(lambda c=c: nc.vector.tensor_scalar(
                    out=hl4[:, c, :], in0=xp[:, c, TP - 4:TP],
                    scalar1=mods[:, l, jsc * 8 + c, 0:1], scalar2=mods[:, l, jsh * 8 + c, 0:1],
                    op0=ALU.mult, op1=ALU.add), r=["mods", f"xp{TP - TT}"], w=["hl4"])
            with contextlib.ExitStack() as st2:
                h4 = P.sb("e_h4", [128, 32], stack=st2)
                g4 = P.sb("e_g4", [128, 4, 32], stack=st2)
                V(lambda: nc.vector.tensor_copy(out=h4[:].rearrange("p (c t) -> p c t", t=4), in_=hl4[:]), r=["hl4"], w=["h4"])
                S.dma("pool", lambda: nc.gpsimd.dma_start(out=xsrc1[:, :], in_=h4[:]), reads=["h4"], writes=["xsrc1"])
                if "nocc" not in dbg:
                    S.collective(lambda: nc.gpsimd.collective_compute(
                        "AllGather", ALU.bypass, replica_groups=([[0, 1, 2, 3]] if "half" in dbg else [[0, 1, 2, 3], [4, 5, 6, 7]]),
                        ins=[xsrc1.ap().opt()], outs=[xdst1.ap().opt()]), reads=["xsrc1"], writes=["xdst1"])
                S.dma("sp", lambda: nc.sync.dma_start(out=g4[:], in_=xdst1[:, :].rearrange("(r p) n -> p r n", p=128)), reads=["xdst1"], writes=["g4"])
                V(lambda: nc.vector.tensor_scalar(out=h4[:], in0=g4[:, 0, :], scalar1=pc_t[:, 0:1], scalar2=None, op0=ALU.mult), r=["g4", "pc", "h4"], w=["h4"])
                for r_ in range(1, 4):
                    V(lambda r_=r_: nc.vector.scalar_tensor_tensor(out=h4[:], in0=g4[:, r_, :], scalar=pc_t[:, r_:r_ + 1], in1=h4[:],
                                                                   op0=ALU.mult, op1=ALU.add), r=["g4", "pc", "h4"], w=["h4"])
                V(lambda: nc.vector.tensor_copy(out=hmh[:], in_=h4[:].rearrange("p (c t) -> p c t", t=4)), r=["h4"], w=["hmh"])
                S.barrier()

            def load_w(i, b):
                S.dma("pool", lambda: nc.gpsimd.dma_start(out=wbuf[b][:], in_=win1[i]), writes=[f"nw{b}"])

            def proj_fm(pi, n, wt, col0, src):
                for c in range(DC):
                    T(lambda c=c: nc.tensor.matmul(pbank[pi][:, 0:n], wt[:, c, col0:col0 + 128], src[:, c, 0:n],
                                                   start=(c == 0), stop=(c == DC - 1)), r=["nw0", "nw1", "hm", "hmh"], w=[f"pb{pi}"])

            load_w(0, 0)
            for h in range(8):
                if h + 1 < 8:
                    load_w(h + 1, (h + 1) % 2)
                for j3 in range(3):
                    proj_fm(j3, 4, wbuf[h % 2], j3 * 128, hmh)
                    V(lambda j3=j3, h=h: nc.vector.tensor_copy(out=carry0[:, j3, h, :], in_=pbank[j3][:, 1:4]), r=[f"pb{j3}"], w=["carry0"])

            def gdn_pass(mode):
                aug = (mode == "A")
                VW = 256 if aug else 128
                V(lambda: nc.vector.tensor_copy(out=carry[:], in_=carry0[:]), r=["carry0"], w=["carry"])
                for piece in range(TP // NTK):
                    t0 = piece * NTK
                    make_hm(t0)
                    with contextlib.ExitStack() as st2:
                        GT = P.sb("p_GT", [8, NTK], stack=st2)
                        BT = P.sb("p_BT", [8, NTK], stack=st2)
                        Gc = P.sb("p_Gc", [128, 4, 8], stack=st2)
                        Bc = P.sb("p_Bc", [128, 4, 8], stack=st2)
                        PC = [P.sb(f"p_PC{j}", [128, 3 + NTK], stack=st2) for j in range(3)]
                        CV = [None] + [P.sb(f"p_CV{j}", [128, NTK], stack=st2) for j in (1, 2)]
                        SQb = P.sb("p_SQ", [128, NTK], BF16, stack=st2)
                        NQ = 1 if aug else 2
                        RNs = [P.sb(f"p_RN{i}", [128, NTK], stack=st2) for i in range(NQ)]
                        CVQ = [P.sb(f"p_CVq{i}", [128, NTK], stack=st2) for i in range(NQ)]
                        GSbs = [P.sb(f"p_GS{i}", [128, NTK], BF16, stack=st2) for i in range(NQ)]
                        OOs = [P.sb(f"p_OO{i}", [128, NTK], stack=st2) for i in range(NQ)]
                        PKs = [P.sb(f"p_PK{i}", [128, 2080], stack=st2) for i in range(2)]
                        DmT = P.sb("p_DmT", [128, 4, 128], stack=st2)
                        QKTs = [P.sb(f"p_QKT{i}", [128, 4, 128], stack=st2) for i in range(NQ)]
                        if aug:
                            RN0 = [P.sb(f"p_RN0{i}", [128, 4, 128], stack=st2) for i in range(2)]
                            WK = [P.sb(f"p_WK{i}", [128, 4, 128], stack=st2) for i in range(4)]
                            Xt = P.sb("p_Xt", [128, 4, 128], stack=st2)
                        U = P.sb("p_U", [128, 128], stack=st2)
                        WT = P.sb("p_WT", [128, 128], stack=st2)
                        DL = P.sb("p_DL", [128, 256], stack=st2)
                        for c in range(DC):
                            T(lambda c=c: nc.tensor.matmul(pbank[0][0:8, 0:NTK], wab[:, c, 0:8], hmp[:, c, :], start=(c == 0), stop=(c == DC - 1)),
                              r=["wab", "hm"], w=["pb0"])
                        for c in range(DC):
                            T(lambda c=c: nc.tensor.matmul(pbank[1][0:8, 0:NTK], wab[:, c, 8:16], hmp[:, c, :], start=(c == 0), stop=(c == DC - 1)),
                              r=["wab", "hm"], w=["pb1"])
                        A(lambda: nc.scalar.activation(out=GT[:], in_=pbank[0][0:8, 0:NTK], func=AF.Exp, bias=abc[:, 0:1]), r=["pb0", "abc"], w=["GT"])
                        V(lambda: nc.vector.tensor_scalar(out=GT[:], in0=GT[:], scalar1=1.0, scalar2=None, op0=ALU.add), r=["GT"], w=["GT"])
                        A(lambda: nc.scalar.activation(out=GT[:], in_=GT[:], func=AF.Ln), r=["GT"], w=["GT"])
                        V(lambda: nc.vector.tensor_scalar(out=GT[:], in0=GT[:], scalar1=abc[:, 1:2], scalar2=None, op0=ALU.mult), r=["GT", "abc"], w=["GT"])
                        for blk in range(4):
                            V(lambda blk=blk: nc.vector.tensor_tensor_scan(
                                out=GT[:, blk * 128:(blk + 1) * 128], data0=onesf[0:8, 0:128], data1=GT[:, blk * 128:(blk + 1) * 128],
                                initial=0.0, op0=ALU.mult, op1=ALU.add), r=["GT", "cst"], w=["GT"])
                        A(lambda: nc.scalar.activation(out=BT[:], in_=pbank[1][0:8, 0:NTK], func=AF.Sigmoid), r=["pb1"], w=["BT"])
                        for blk in range(4):
                            T(lambda blk=blk: nc.tensor.transpose(pbank[2][:, blk * 8:blk * 8 + 8], GT[:, blk * 128:(blk + 1) * 128], ident[0:8, 0:8]),
                              r=["GT", "cst"], w=["pb2"])
                            T(lambda blk=blk: nc.tensor.transpose(pbank[2][:, 32 + blk * 8:32 + blk * 8 + 8], BT[:, blk * 128:(blk + 1) * 128], ident[0:8, 0:8]),
                              r=["BT", "cst"], w=["pb2"])
                        V(lambda: nc.vector.tensor_copy(out=Gc[:].rearrange("p b h -> p (b h)"), in_=pbank[2][:, 0:32]), r=["pb2"], w=["Gc"])
                        V(lambda: nc.vector.tensor_copy(out=Bc[:].rearrange("p b h -> p (b h)"), in_=pbank[2][:, 32:64]), r=["pb2"], w=["Bc"])
                        if "gA1" in dbg:
                            S.barrier()
                            return
                        UB, SB = (5, 7) if aug else (2, 6)

                        def head_views(h):
                            wt = wbuf[h % 2]
                            par = h % 2
                            kp = f"k{par}"
                            PKc = PKs[par]
                            X = PKc[:, 0:512].rearrange("p (b k) -> p b k", k=128)
                            BV = PKc[:, 512:1024].rearrange("p (b k) -> p b k", k=128)
                            BGK = PKc[:, 1024:1536].rearrange("p (b k) -> p b k", k=128)
                            KDC = PKc[:, 1536:2048].rearrange("p (b k) -> p b k", k=128)
                            sc4 = PKc[:, 2048:2080].rearrange("p (b k) -> p b k", k=8)
                            pkkeys = [f"{kp}{nm}{b_}" for nm in ("X", "BV", "BGK", "KDC", "sc4") for b_ in range(4)]

                            qi = par % NQ
                            kq = f"q{qi}"
                            return wt, par, kp, PKc, X, BV, BGK, KDC, sc4, pkkeys, CVQ[qi], QKTs[qi], OOs[qi], GSbs[qi], RNs[qi], kq

                        def head_front(h):
                            wt, par, kp, PKc, X, BV, BGK, KDC, sc4, pkkeys, CVq, QKT, OO, GSb, RN, kq = head_views(h)
                            EG, NB = OO, RN
                            cvt = lambda j_: CVq if j_ == 0 else CV[j_]
                            ck = lambda j_: f"CV{j_}" + (kq if j_ == 0 else "")
                            if piece == 0 and h == 0:
                                load_w(0, 0)
                            if not (piece == TP // NTK - 1 and h == 7):
                                load_w((h + 1) % 8, (h + 1) % 2)
                            if not aug:
                                S.dma("sp", lambda: nc.sync.dma_start(out=PKc[:], in_=gscr[piece * 8 + h]), reads=[f"gscr{piece * 8 + h}"], writes=pkkeys)
                            for j3 in ([1, 2] if aug else [0, 1]):
                                proj_fm(j3, NTK, wt, j3 * 128, hmp)
                                V(lambda j3=j3: nc.vector.tensor_copy(out=PC[j3][:, 0:3], in_=carry[:, j3, h, :]), r=["carry"], w=[f"PC{j3}"])
                                A(lambda j3=j3: nc.scalar.copy(out=PC[j3][:, 3:3 + NTK], in_=pbank[j3][:, 0:NTK]), r=[f"pb{j3}"], w=[f"PC{j3}"])
                                V(lambda j3=j3: nc.vector.tensor_copy(out=carry[:, j3, h, :], in_=PC[j3][:, NTK:NTK + 3]), r=[f"PC{j3}"], w=["carry"])
                                V(lambda j3=j3: nc.vector.tensor_scalar(out=cvt(j3)[:], in0=PC[j3][:, 0:NTK], scalar1=cw[:, j3, h, 0:1], scalar2=None,
                                                                        op0=ALU.mult), r=[f"PC{j3}", "cw"], w=[ck(j3)])
                                for tap in range(1, 4):
                                    V(lambda j3=j3, tap=tap: nc.vector.scalar_tensor_tensor(
                                        out=cvt(j3)[:], in0=PC[j3][:, tap:tap + NTK], scalar=cw[:, j3, h, tap:tap + 1], in1=cvt(j3)[:],
                                        op0=ALU.mult, op1=ALU.add), r=[f"PC{j3}", "cw", ck(j3)], w=[ck(j3)])
                                A(lambda j3=j3: nc.scalar.activation(out=cvt(j3)[:], in_=cvt(j3)[:], func=AF.Silu), r=[ck(j3)], w=[ck(j3)])
                                if j3 < 2:
                                    A(lambda j3=j3: nc.scalar.activation(out=SQb[:], in_=cvt(j3)[:], func=AF.Square), r=[ck(j3)], w=["SQb"])
                                    T(lambda: nc.tensor.matmul(pbank[3][:, 0:NTK], ones128[:], SQb[:], start=True, stop=True), r=["SQb", "ones128"], w=["pb3"])
                                    V(lambda: nc.vector.tensor_scalar(out=RN[:], in0=pbank[3][:, 0:NTK], scalar1=128.0, scalar2=1e-6,
                                                                      op0=ALU.mult, op1=ALU.add), r=["pb3"], w=[f"RN{kq}"])
                                    A(lambda: nc.scalar.activation(out=RN[:], in_=RN[:], func=AF.Ln), r=[f"RN{kq}"], w=[f"RN{kq}"])
                                    A(lambda: nc.scalar.activation(out=RN[:], in_=RN[:], func=AF.Exp, scale=-0.5), r=[f"RN{kq}"], w=[f"RN{kq}"])
                                    if j3 == 0:
                                        V(lambda: nc.vector.scalar_tensor_tensor(out=CVq[:], in0=CVq[:], scalar=128.0 ** -0.5, in1=RN[:],
                                                                                 op0=ALU.mult, op1=ALU.mult), r=[ck(0), f"RN{kq}"], w=[ck(0)])
                                    else:
                                        V(lambda: nc.vector.tensor_tensor(out=CV[1][:], in0=CV[1][:], in1=RN[:], op=ALU.mult), r=["CV1", f"RN{kq}"], w=["CV1"])
                            if not aug:
                                proj_fm(3, NTK, wt, 384, hmp)
                                A(lambda: nc.scalar.activation(out=GSb[:], in_=pbank[3][:, 0:NTK], func=AF.Silu), r=["pb3"], w=[f"GSb{kq}"])
                            if h == 7 and piece == TP // NTK - 1:
                                if aug:
                                    S.dma("sp", lambda: nc.sync.dma_start(out=o_gconv[:, 1:3, :, :], in_=carry[:, 1:3, :, :]), reads=["carry"])
                                else:
                                    S.dma("sp", lambda: nc.sync.dma_start(out=o_gconv[:, 0:1, :, :], in_=carry[:, 0:1, :, :]), reads=["carry"])
                            yield
                            T(lambda: nc.tensor.matmul(pbank[4][:, 0:NTK], ident[0:8, h:h + 1].to_broadcast([8, 128]), GT[:], start=True, stop=True),
                              r=["GT", "cst"], w=["pb4"])
                            V(lambda: nc.vector.tensor_copy(out=OO[:], in_=pbank[4][:, 0:NTK]), r=["pb4"], w=[f"OO{kq}"])
                            if aug:
                                T(lambda: nc.tensor.matmul(pbank[3][:, 0:NTK], ident[0:8, h:h + 1].to_broadcast([8, 128]), BT[:], start=True, stop=True),
                                  r=["BT", "cst"], w=["pb3"])
                                V(lambda: nc.vector.tensor_scalar(out=NB[:], in0=pbank[3][:, 0:NTK], scalar1=-1.0, scalar2=None, op0=ALU.mult), r=["pb3"], w=[f"RN{kq}"])
                            for blk in range(4):
                                bs = slice(blk * 128, (blk + 1) * 128)
                                if aug:
                                    V(lambda: nc.vector.tensor_scalar(out=sc4[:, blk, 3:4], in0=OO[:, blk * 128 + 127:blk * 128 + 128], scalar1=Gc[:, blk, h:h + 1],
                                                                      scalar2=None, op0=ALU.subtract), r=[f"OO{kq}", "Gc"], w=[f"{kp}sc4{blk}"])
                                    A(lambda: nc.scalar.activation(out=sc4[:, blk, 1:2], in_=sc4[:, blk, 3:4], func=AF.Exp), r=[f"{kp}sc4{blk}"], w=[f"{kp}sc4{blk}"])
                                    A(lambda: nc.scalar.activation(out=sc4[:, blk, 2:3], in_=OO[:, blk * 128 + 127:blk * 128 + 128], func=AF.Exp), r=[f"OO{kq}"], w=[f"{kp}sc4{blk}"])
                                    A(lambda: nc.scalar.activation(out=sc4[:, blk, 0:1], in_=Gc[:, blk, h:h + 1], func=AF.Exp), r=["Gc"], w=[f"{kp}sc4{blk}"])
                                    V(lambda: nc.vector.tensor_tensor(out=sc4[:, blk, 0:1], in0=sc4[:, blk, 0:1], in1=Bc[:, blk, h:h + 1], op=ALU.mult),
                                      r=["Bc", f"{kp}sc4{blk}"], w=[f"{kp}sc4{blk}"])
                                V(lambda: nc.vector.tensor_scalar(out=DmT[:, blk, :], in0=OO[:, bs], scalar1=Gc[:, blk, h:h + 1], scalar2=0.0,
                                                                  op0=ALU.subtract, op1=ALU.min), r=[f"OO{kq}", "Gc"], w=[f"DmT{blk}"])
                                A(lambda: nc.scalar.activation(out=DmT[:, blk, :], in_=DmT[:, blk, :], func=AF.Exp), r=[f"DmT{blk}"], w=[f"DmT{blk}"])
                            if not aug:
                                A(lambda: nc.scalar.activation(out=OO[:], in_=OO[:], func=AF.Exp), r=[f"OO{kq}"], w=[f"OO{kq}"])
                            if aug:
                                for blk in range(4):
                                    bs = slice(blk * 128, (blk + 1) * 128)
                                    T(lambda: nc.tensor.transpose(pbank[6][:, blk * 128:(blk + 1) * 128], CV[1][:, bs], ident), r=["CV1", "cst"], w=["pb6"])
                                    T(lambda: nc.tensor.transpose(pbank[4][:, blk * 128:(blk + 1) * 128], CV[2][:, bs], ident), r=["CV2", "cst"], w=["pb4"])
                                for blk in range(4):
                                    A(lambda blk=blk: nc.scalar.activation(out=BGK[:, blk, :], in_=pbank[6][:, blk * 128:(blk + 1) * 128], func=AF.Copy, scale=sc4[:, blk, 0:1]),
                                      r=["pb6", f"{kp}sc4{blk}"], w=[f"{kp}BGK{blk}"])
                                    A(lambda blk=blk: nc.scalar.activation(out=KDC[:, blk, :], in_=pbank[6][:, blk * 128:(blk + 1) * 128], func=AF.Copy, scale=sc4[:, blk, 1:2]),
                                      r=["pb6", f"{kp}sc4{blk}"], w=[f"{kp}KDC{blk}"])
                                    V(lambda blk=blk: nc.vector.tensor_scalar(out=BV[:, blk, :], in0=pbank[4][:, blk * 128:(blk + 1) * 128], scalar1=Bc[:, blk, h:h + 1],
                                                                              scalar2=None, op0=ALU.mult), r=["pb4", "Bc"], w=[f"{kp}BV{blk}"])
                                for blk in range(4):
                                    bs = slice(blk * 128, (blk + 1) * 128)
                                    T(lambda: nc.tensor.matmul(pbank[6][:, bs], CV[1][:, bs], CV[1][:, bs], start=True, stop=True), r=["CV1"], w=["pb6"])
                            else:
                                for blk in range(4):
                                    bs = slice(blk * 128, (blk + 1) * 128)
                                    T(lambda: nc.tensor.matmul(pbank[7][:, bs], CV[1][:, bs], CVq[:, bs], start=True, stop=True), r=["CV1", ck(0)], w=["pb7"])
                            if not aug:
                                V(lambda: nc.vector.tensor_tensor(out=CVq[:], in0=CVq[:], in1=EG[:], op=ALU.mult), r=[ck(0), f"OO{kq}"], w=[ck(0)])
                            QG = CVq
                            if aug:
                                R0, N0 = RN0[0], RN0[1]
                                for blk in range(4):
                                    bs = slice(blk * 128, (blk + 1) * 128)
                                    V(lambda: nc.vector.tensor_tensor(out=DmT[:, blk, :], in0=DmT[:, blk, :], in1=Ustrict, op=ALU.mult), r=[f"DmT{blk}", "tri"], w=[f"DmT{blk}"])
                                    V(lambda: nc.vector.tensor_tensor(out=DmT[:, blk, :], in0=DmT[:, blk, :], in1=NB[:, bs], op=ALU.mult), r=[f"DmT{blk}", f"RN{kq}"], w=[f"DmT{blk}"])
                                    V(lambda: nc.vector.tensor_tensor(out=R0[:, blk, :], in0=DmT[:, blk, :], in1=pbank[6][:, bs], op=ALU.mult), r=[f"DmT{blk}", "pb6"], w=[f"R0_{blk}"])
                                for blk in range(4):
                                    T(lambda blk=blk: nc.tensor.transpose(pbank[3][:, blk * 128:(blk + 1) * 128], R0[:, blk, :], ident), r=[f"R0_{blk}", "cst"], w=["pb3"])
                                V(lambda: nc.vector.tensor_copy(out=N0[:].rearrange("p b k -> p (b k)"), in_=pbank[3][:, 0:512]), r=["pb3"], w=[f"N0_{b_}" for b_ in range(4)])
                                allk = lambda nm: [f"{nm}{b_}" for b_ in range(4)]
                                grk = lambda nm, g_: [f"{nm}{b_}" for b_ in (2 * g_, 2 * g_ + 1)]
                                gfl = lambda t, g_: t[:, 2 * g_:2 * g_ + 2, :].rearrange("p b k -> p (b k)")
                                GB = ((0, 1, 2), (3, 4, 6))
                                m16u = tri[:, 256:384].unsqueeze(1).to_broadcast([128, 4, 128])
                                m16l = tri[:, 384:512].unsqueeze(1).to_broadcast([128, 4, 128])
                                Rd, Nd, Rd2, Nd2 = WK[0], WK[1], WK[2], WK[3]
                                V(lambda: nc.vector.tensor_tensor(out=Rd[:], in0=R0[:], in1=m16u, op=ALU.mult), r=allk("R0_") + ["tri"], w=allk("Rd"))
                                V(lambda: nc.vector.tensor_tensor(out=Nd[:], in0=N0[:], in1=m16l, op=ALU.mult), r=allk("N0_") + ["tri"], w=allk("Nd"))
                                V(lambda: nc.vector.tensor_tensor(out=X[:], in0=Rd[:], in1=ident.unsqueeze(1).to_broadcast([128, 4, 128]), op=ALU.add),
                                  r=allk("Rd") + ["cst"], w=allk(kp + "X"))
                                for lev in range(3):
                                    for g_ in range(2):
                                        b0_, b1_, b2_ = GB[g_]
                                        for q_ in range(2):
                                            blk = 2 * g_ + q_
                                            qs = slice(q_ * 128, (q_ + 1) * 128)
                                            T(lambda: nc.tensor.matmul(pbank[b0_][:, qs], Nd[:, blk, :], Rd[:, blk, :], start=True, stop=True), r=[f"Nd{blk}", f"Rd{blk}"], w=[f"pb{b0_}"])
                                            T(lambda: nc.tensor.matmul(pbank[b1_][:, qs], Rd[:, blk, :], Nd[:, blk, :], start=True, stop=True), r=[f"Nd{blk}", f"Rd{blk}"], w=[f"pb{b1_}"])
                                    for g_ in range(2):
                                        b0_, b1_, b2_ = GB[g_]
                                        A(lambda: nc.scalar.copy(out=gfl(Rd2, g_), in_=pbank[b0_][:, 0:256]), r=[f"pb{b0_}"], w=grk("Rd2", g_))
                                        V(lambda: nc.vector.tensor_copy(out=gfl(Nd2, g_), in_=pbank[b1_][:, 0:256]), r=[f"pb{b1_}"], w=grk("Nd2", g_))
                                    for g_ in range(2):
                                        b0_, b1_, b2_ = GB[g_]
                                        for q_ in range(2):
                                            blk = 2 * g_ + q_
                                            qs = slice(q_ * 128, (q_ + 1) * 128)
                                            T(lambda: nc.tensor.matmul(pbank[b2_][:, qs], Nd2[:, blk, :], X[:, blk, :], start=True, stop=True), r=[f"Nd2{blk}", f"{kp}X{blk}"], w=[f"pb{b2_}"])
                                    for g_ in range(2):
                                        b0_, b1_, b2_ = GB[g_]
                                        V(lambda: nc.vector.tensor_tensor(out=gfl(X, g_), in0=gfl(X, g_), in1=pbank[b2_][:, 0:256], op=ALU.add), r=[f"pb{b2_}"] + grk(kp + "X", g_), w=grk(kp + "X", g_))
                                    Rd, Nd, Rd2, Nd2 = Rd2, Nd2, Rd, Nd
                                    for b_ in range(4):
                                        for a_, c_ in (("Rd", "Rd2"), ("Nd", "Nd2")):
                                            sa, sc = S.res.get(f"{a_}{b_}"), S.res.get(f"{c_}{b_}")
                                            if sc is not None:
                                                S.res[f"{a_}{b_}"] = sc
                                            elif f"{a_}{b_}" in S.res:
                                                del S.res[f"{a_}{b_}"]
                                            if sa is not None:
                                                S.res[f"{c_}{b_}"] = sa
                                            elif f"{c_}{b_}" in S.res:
                                                del S.res[f"{c_}{b_}"]
                                    yield
                                for g_ in range(2):
                                    b0_ = GB[g_][0]
                                    for q_ in range(2):
                                        blk = 2 * g_ + q_
                                        T(lambda: nc.tensor.transpose(pbank[b0_][:, q_ * 128:(q_ + 1) * 128], X[:, blk, :], ident), r=[f"{kp}X{blk}", "cst"], w=[f"pb{b0_}"])
                                for g_ in range(2):
                                    b0_ = GB[g_][0]
                                    A(lambda: nc.scalar.copy(out=gfl(Xt, g_), in_=pbank[b0_][:, 0:256]), r=[f"pb{b0_}"], w=grk("Xt", g_))
                                Rl, Nl, Y, Yt = WK[0], WK[1], WK[2], WK[3]
                                for li in range(3):
                                    mu = tri[:, 512 + li * 256:640 + li * 256].unsqueeze(1).to_broadcast([128, 4, 128])
                                    ml = tri[:, 640 + li * 256:768 + li * 256].unsqueeze(1).to_broadcast([128, 4, 128])
                                    V(lambda: nc.vector.tensor_tensor(out=Rl[:], in0=R0[:], in1=mu, op=ALU.mult), r=allk("R0_") + ["tri"], w=allk("Rl") + allk("Rd") + allk("Rd2"))
                                    V(lambda: nc.vector.tensor_tensor(out=Nl[:], in0=N0[:], in1=ml, op=ALU.mult), r=allk("N0_") + ["tri"], w=allk("Nl") + allk("Nd") + allk("Nd2"))
                                    for g_ in range(2):
                                        b0_, b1_, b2_ = GB[g_]
                                        for q_ in range(2):
                                            blk = 2 * g_ + q_
                                            qs = slice(q_ * 128, (q_ + 1) * 128)
                                            T(lambda: nc.tensor.matmul(pbank[b0_][:, qs], Nl[:, blk, :], X[:, blk, :], start=True, stop=True), r=[f"Nl{blk}", f"{kp}X{blk}"], w=[f"pb{b0_}"])
                                            if li < 2:
                                                T(lambda: nc.tensor.matmul(pbank[b1_][:, qs], Rl[:, blk, :], Xt[:, blk, :], start=True, stop=True), r=[f"Rl{blk}", f"Xt{blk}"], w=[f"pb{b1_}"])
                                    for g_ in range(2):
                                        b0_, b1_, b2_ = GB[g_]
                                        A(lambda: nc.scalar.copy(out=gfl(Y, g_), in_=pbank[b0_][:, 0:256]), r=[f"pb{b0_}"], w=grk("Y", g_) + grk("Rd", g_) + grk("Rd2", g_) + grk("Nd", g_) + grk("Nd2", g_))
                                        if li < 2:
                                            V(lambda: nc.vector.tensor_copy(out=gfl(Yt, g_), in_=pbank[b1_][:, 0:256]), r=[f"pb{b1_}"], w=grk("Yt", g_) + grk("Rd", g_) + grk("Rd2", g_) + grk("Nd", g_) + grk("Nd2", g_))
                                    for g_ in range(2):
                                        b0_, b1_, b2_ = GB[g_]
                                        for q_ in range(2):
                                            blk = 2 * g_ + q_
                                            qs = slice(q_ * 128, (q_ + 1) * 128)
                                            T(lambda: nc.tensor.matmul(pbank[b2_][:, qs], Xt[:, blk, :], Y[:, blk, :], start=True, stop=True), r=[f"Xt{blk}", f"Y{blk}"], w=[f"pb{b2_}"])
                                            if li < 2:
                                                T(lambda: nc.tensor.matmul(pbank[b0_][:, qs], X[:, blk, :], Yt[:, blk, :], start=True, stop=True), r=[f"{kp}X{blk}", f"Yt{blk}"], w=[f"pb{b0_}"])
                                    for g_ in range(2):
                                        b0_, b1_, b2_ = GB[g_]
                                        V(lambda: nc.vector.tensor_tensor(out=gfl(X, g_), in0=gfl(X, g_), in1=pbank[b2_][:, 0:256], op=ALU.add), r=[f"pb{b2_}"] + grk(kp + "X", g_), w=grk(kp + "X", g_))
                                        if li < 2:
                                            V(lambda: nc.vector.tensor_tensor(out=gfl(Xt, g_), in0=gfl(Xt, g_), in1=pbank[b0_][:, 0:256], op=ALU.add), r=[f"pb{b0_}"] + grk("Xt", g_), w=grk("Xt", g_))
                                    yield
                                S.dma("sp", lambda: nc.sync.dma_start(out=gscr[piece * 8 + h], in_=PKc[:]), reads=pkkeys, writes=[f"gscr{piece * 8 + h}"])
                            else:
                                for blk in range(4):
                                    bs = slice(blk * 128, (blk + 1) * 128)
                                    V(lambda: nc.vector.tensor_tensor(out=QKT[:, blk, :], in0=DmT[:, blk, :], in1=Uincl, op=ALU.mult), r=[f"DmT{blk}", "tri"], w=[f"QKT{kq}{blk}"])
                                    V(lambda: nc.vector.tensor_tensor(out=QKT[:, blk, :], in0=QKT[:, blk, :], in1=pbank[7][:, bs], op=ALU.mult), r=[f"QKT{kq}{blk}", "pb7"], w=[f"QKT{kq}{blk}"])

                            yield

                        def head_back(h):
                            wt, par, kp, PKc, X, BV, BGK, KDC, sc4, pkkeys, CVq, QKT, OO, GSb, RN, kq = head_views(h)
                            EG, NB = OO, RN
                            cvt = lambda j_: CVq if j_ == 0 else CV[j_]
                            ck = lambda j_: f"CV{j_}" + (kq if j_ == 0 else "")
                            QG = CVq
                            for blk in range(4):
                                bs = slice(blk * 128, (blk + 1) * 128)
                                T(lambda: nc.tensor.matmul(pbank[UB][:, 0:128], X[:, blk, :], BV[:, blk, :], start=True, stop=True), r=[f"{kp}X{blk}", f"{kp}BV{blk}"], w=[f"pb{UB}"])
                                T(lambda: nc.tensor.matmul(pbank[UB][:, 128:256], BGK[:, blk, :], X[:, blk, :], start=True, stop=True), r=[f"{kp}X{blk}", f"{kp}BGK{blk}"], w=[f"pb{UB}"])
                                yield
                                V(lambda: nc.vector.tensor_copy(out=U[:], in_=pbank[UB][:, 0:128]), r=[f"pb{UB}"], w=["U"])
                                V(lambda: nc.vector.tensor_copy(out=WT[:], in_=pbank[UB][:, 128:256]), r=[f"pb{UB}"], w=["WT"])
                                T(lambda: nc.tensor.matmul(pbank[SB][:, 0:VW], WT[:], Sst[:, h, 0:VW], start=True, stop=True), r=["WT", f"S{h}"], w=[f"pb{SB}"])
                                yield
                                V(lambda: nc.vector.tensor_tensor(out=DL[:, 0:128], in0=U[:], in1=pbank[SB][:, 0:128], op=ALU.subtract), r=["U", f"pb{SB}"], w=["DL"])
                                if aug:
                                    V(lambda: nc.vector.tensor_scalar(out=DL[:, 128:256], in0=pbank[SB][:, 128:256], scalar1=-1.0, scalar2=None, op0=ALU.mult), r=[f"pb{SB}"], w=["DL"])
                                else:
                                    T(lambda: nc.tensor.matmul(pbank[5][:, 0:128], Sst[:, h, 0:128], QG[:, bs], start=True, stop=False), r=[f"S{h}", ck(0)], w=["pb5"])
                                    T(lambda: nc.tensor.matmul(pbank[5][:, 0:128], DL[:, 0:128], QKT[:, blk, :], start=False, stop=True), r=["DL", f"QKT{kq}{blk}"], w=["pb5"])
                                    A(lambda: nc.scalar.copy(out=OO[:, bs], in_=pbank[5][:, 0:128]), r=["pb5"], w=[f"OO{kq}"])
                                T(lambda: nc.tensor.matmul(pbank[SB][:, 256:256 + VW], KDC[:, blk, :], DL[:, 0:VW], start=True, stop=True), r=[f"{kp}KDC{blk}", "DL"], w=[f"pb{SB}"])
                                yield
                                V(lambda: nc.vector.scalar_tensor_tensor(out=Sst[:, h, 0:VW], in0=Sst[:, h, 0:VW], scalar=sc4[:, blk, 2:3], in1=pbank[SB][:, 256:256 + VW],
                                                                         op0=ALU.mult, op1=ALU.add), r=[f"S{h}", f"{kp}sc4{blk}", f"pb{SB}"], w=[f"S{h}"])
                                yield
                            if not aug:
                                A(lambda: nc.scalar.activation(out=SQb[:], in_=OO[:], func=AF.Square), r=[f"OO{kq}"], w=["SQb"])
                                T(lambda: nc.tensor.matmul(pbank[UB][:, 0:NTK], ones128[:], SQb[:], start=True, stop=True), r=["SQb", "ones128"], w=[f"pb{UB}"])
                                V(lambda: nc.vector.tensor_scalar(out=RN[:], in0=pbank[UB][:, 0:NTK], scalar1=1e-6, scalar2=None, op0=ALU.add), r=[f"pb{UB}"], w=[f"RN{kq}"])
                                A(lambda: nc.scalar.activation(out=RN[:], in_=RN[:], func=AF.Ln), r=[f"RN{kq}"], w=[f"RN{kq}"])
                                A(lambda: nc.scalar.activation(out=RN[:], in_=RN[:], func=AF.Exp, scale=-0.5), r=[f"RN{kq}"], w=[f"RN{kq}"])
                                V(lambda: nc.vector.tensor_tensor(out=OO[:], in0=OO[:], in1=RN[:], op=ALU.mult), r=[f"OO{kq}", f"RN{kq}"], w=[f"OO{kq}"])
                                V(lambda: nc.vector.scalar_tensor_tensor(out=og[:, h, :], in0=OO[:], scalar=gg_t[:, 0:1], in1=GSb[:], op0=ALU.mult, op1=ALU.mult),
                                  r=[f"OO{kq}", f"GSb{kq}", "gg"], w=["og"])


                            yield

                        if True:
                            for _ in head_front(0):
                                pass
                            for h in range(8):
                                gb = head_back(h)
                                gf = head_front(h + 1) if h + 1 < 8 else iter(())
                                done_b = done_f = False
                                while not (done_b and done_f):
                                    if not done_f:
                                        try:
                                            next(gf)
                                        except StopIteration:
                                            done_f = True
                                    if not done_b:
                                        try:
                                            next(gb)
                                        except StopIteration:
                                            done_b = True
                    S.barrier()
                    if not aug:
                        yield piece

            for h in range(8):
                V(lambda h=h: nc.vector.memset(Sst[:, h, 0:128], 0.0), w=[f"S{h}"])
                V(lambda h=h: nc.vector.tensor_copy(out=Sst[:, h, 128:256], in_=ident), r=["cst"], w=[f"S{h}"])
            if "stop0" in dbg:
                return
            with scope("gdnA"):
                for _ in gdn_pass("A"):
                    pass
            if "stopA" in dbg:
                return
            with contextlib.ExitStack() as st2:
                xs2 = P.sb("e_xs2", [128, 8, 256], stack=st2)
                for h in range(8):
                    V(lambda h=h: nc.vector.tensor_copy(out=xs2[:, h, 0:128], in_=Sst[:, h, 0:128]), r=[f"S{h}"], w=["xs2"])
                    T(lambda h=h: nc.tensor.transpose(pbank[h % 2][:, 0:128], Sst[:, h, 128:256], ident), r=[f"S{h}", "cst"], w=[f"pb{h % 2}"])
                    V(lambda h=h: nc.vector.tensor_copy(out=xs2[:, h, 128:256], in_=pbank[h % 2][:, 0:128]), r=[f"pb{h % 2}"], w=["xs2"])
                S.dma("pool", lambda: nc.gpsimd.dma_start(out=xsrc2[:, :], in_=xs2[:].rearrange("p h k -> p (h k)")), reads=["xs2"], writes=["xsrc2"])
                S.collective(lambda: nc.gpsimd.collective_compute(
                    "AllGather", ALU.bypass, replica_groups=([[0, 1, 2, 3]] if "half" in dbg else [[0, 1, 2, 3], [4, 5, 6, 7]]),
                    ins=[xsrc2.ap().opt()], outs=[xdst2.ap().opt()]), reads=["xsrc2"], writes=["xdst2"])
                for h in range(8):
                    V(lambda h=h: nc.vector.memset(Sst[:, h, 0:128], 0.0), r=["xs2"], w=[f"S{h}"])
                gr = P.sb("e_gr", [128, 8, 256], stack=st2)
                dS = P.sb("e_dS", [128, 128], stack=st2)
                for r_ in range(3):
                    S.dma("sp", lambda r_=r_: nc.sync.dma_start(out=gr[:].rearrange("p h k -> p (h k)"), in_=xdst2[r_ * 128:(r_ + 1) * 128, :]),
                          reads=["xdst2"], writes=["gr"])
                    for h in range(8):
                        T(lambda h=h: nc.tensor.matmul(pbank[h % 2][:, 0:128], gr[:, h, 128:256], Sst[:, h, 0:128], start=True, stop=True),
                          r=["gr", f"S{h}"], w=[f"pb{h % 2}"])
                        V(lambda h=h: nc.vector.tensor_tensor(out=dS[:], in0=gr[:, h, 0:128], in1=pbank[h % 2][:, 0:128], op=ALU.add), r=["gr", f"pb{h % 2}"], w=["dS"])
                        V(lambda h=h: nc.vector.tensor_tensor(out=dS[:], in0=dS[:], in1=Sst[:, h, 0:128], op=ALU.subtract), r=["dS", f"S{h}"], w=["dS"])
                        V(lambda h=h: nc.vector.scalar_tensor_tensor(out=Sst[:, h, 0:128], in0=dS[:], scalar=pc_t[:, 4 + r_:5 + r_], in1=Sst[:, h, 0:128],
                                                                     op0=ALU.mult, op1=ALU.add), r=["dS", "pc", f"S{h}"], w=[f"S{h}"])
                S.barrier()

            if "stopX" in dbg:
                return
            o_gs = P.sb("n_ogs", [128, 8, NS], BF16, stack=st)
            with scope("gdnS"):
                sample_gdn(o_gs, wbuf, load_w, pc_t, cw, abc, gg_t)
            if "stopS" in dbg:
                return

            for piece in gdn_pass("B"):
                out_proj_ln(l, jg, wout1, [("p", piece * NTK, NTK)], lambda kc, t0, n: og[:, kc, 0:n], "og")
            for h in range(8):
                S.dma("sp", lambda h=h: nc.sync.dma_start(out=o_gdn[:, h, :], in_=Sst[:, h, 0:128]), reads=[f"S{h}"])
            out_proj_ln(l, jg, wout1, [("s", 0, NS)], lambda kc, t0, n: o_gs[:, kc, :], "o_gs")
            S.barrier()

    def out_proj_ln(l, jg, wout_dram, tl, srcfn, srckey):
        with contextlib.ExitStack() as st2:
            wo = P.sb("o_wo", [128, 8, D], BF16, stack=st2)
            S.dma("pool", lambda: nc.gpsimd.dma_start(out=wo[:], in_=wout_dram[:, :, :]), writes=["wo"])
            xb = P.sb("o_xb", [128, DC, TT], BF16, stack=st2)
            sq = P.sb("o_sq", [128, DC, TT], BF16, stack=st2)
            r = [P.sb(f"o_r{i}", [128, TT], stack=st2) for i in range(2)]
            lt = [P.sb(f"o_lt{i}", [128, TT], stack=st2) for i in range(4)]
            for (kind, t0, n) in tl:
                xv = xview(kind, t0, n)
                xk = f"x{kind}{t0}"
                for co in range(DC):
                    pd = pbank[co % 2]
                    for kc in range(8):
                        T(lambda kc=kc: nc.tensor.matmul(pd[:, :n], wo[:, kc, co * 128:(co + 1) * 128], srcfn(kc, t0, n),
                                                         start=(kc == 0), stop=(kc == 7)), r=["wo", srckey], w=[f"pb{co % 2}"])
                    rr = r[co % 2]
                    if kind == "p":
                        A(lambda: nc.scalar.activation(out=rr[:, :n], in_=pd[:, :n], func=AF.Copy, scale=mods[:, l, jg * 8 + co, 0:1]),
                          r=[f"pb{co % 2}", "mods"], w=[f"r{co % 2}"])
                    else:
                        V(lambda: nc.vector.tensor_tensor(out=rr[:, :n], in0=pd[:, :n], in1=mods[:, l, jg * 8 + co, 1:1 + NS], op=ALU.mult),
                          r=[f"pb{co % 2}", "mods"], w=[f"r{co % 2}"])
                    V(lambda: nc.vector.scalar_tensor_tensor(out=xv[:, co, :], in0=xv[:, co, :], scalar=DN_ALPHA, in1=rr[:, :n],
                                                             op0=ALU.mult, op1=ALU.add), r=[f"r{co % 2}", xk, f"{xk}c{co}"], w=[f"{xk}c{co}"])
                    A(lambda: nc.scalar.copy(out=xb[:, co, :n], in_=xv[:, co, :]), r=[f"{xk}c{co}"], w=[f"xb{co}"])
                    A(lambda: nc.scalar.activation(out=sq[:, co, :n], in_=xv[:, co, :], func=AF.Square), r=[f"{xk}c{co}"], w=[f"sq{co}"])
                layer_norm_tile(xk, xv, n, xb, sq, lt, (l * 3 + 1) * 8)
            S.barrier()

    def sample_gdn(o_gs, wbuf, load_w, pc_t, cw, abc, gg_t):
        l = 1
        jsh, jsc = 3, 4
        with contextlib.ExitStack() as st2:
            hms = P.sb("z_hms", [128, DC, NS], BF16, stack=st2)
            t16 = P.sb("z_t16", [128, NS], stack=st2)
            for c in range(DC):
                V(lambda c=c: nc.vector.tensor_tensor(out=t16[:], in0=xs[:, c, :], in1=mods[:, l, jsc * 8 + c, 1:1 + NS], op=ALU.mult),
                  r=["xs0", "mods"], w=["t16"])
                V(lambda c=c: nc.vector.tensor_tensor(out=hms[:, c, :], in0=t16[:], in1=mods[:, l, jsh * 8 + c, 1:1 + NS], op=ALU.add),
                  r=["t16", "mods"], w=["hms"])
            nr = [P.sb(f"z_nr{i}", [NS, 384], stack=st2) for i in range(2)]
            hsl = [P.sb(f"z_hsl{i}", [3 * NS, 128], stack=st2) for i in range(2)]
            hcount = [0]
            S.dma("sp", lambda: nc.sync.dma_start(out=o_sc[:, 0:2, :], in_=sc_in[:, 1:3, :]))
            QKV = P.sb("z_QKV", [128, 3, 8, NS], stack=st2)
            GS = P.sb("z_GS", [128, 8, NS], stack=st2)
            hT = P.sb("z_hT", [128, 3 * NS], stack=st2)
            pre = P.sb("z_pre", [128, NS], stack=st2)
            AB = P.sb("z_AB", [128, 8, NS], stack=st2)
            BB = P.sb("z_BB", [128, 8, NS], stack=st2)
            ab = P.sb("z_ab", [8, 2 * NS], stack=st2)
            wab = P.sb("z_wab", [128, DC, 16], BF16, stack=st2)
            S.dma("pool", lambda: nc.gpsimd.dma_start(out=wab[:], in_=win1[8, :, :, 0:16]), writes=["zwab"])
            for c in range(DC):
                T(lambda c=c: nc.tensor.matmul(pbank[0][0:8, 0:NS], wab[:, c, 0:8], hms[:, c, :], start=(c == 0), stop=(c == DC - 1)), r=["zwab", "hms"], w=["pb0"])
            for c in range(DC):
                T(lambda c=c: nc.tensor.matmul(pbank[1][0:8, 0:NS], wab[:, c, 8:16], hms[:, c, :], start=(c == 0), stop=(c == DC - 1)), r=["zwab", "hms"], w=["pb1"])
            A(lambda: nc.scalar.activation(out=ab[:, 0:NS], in_=pbank[0][0:8, 0:NS], func=AF.Exp, bias=abc[:, 0:1]), r=["pb0", "abc"], w=["zab"])
            V(lambda: nc.vector.tensor_scalar(out=ab[:, 0:NS], in0=ab[:, 0:NS], scalar1=1.0, scalar2=None, op0=ALU.add), r=["zab"], w=["zab"])
            A(lambda: nc.scalar.activation(out=ab[:, 0:NS], in_=ab[:, 0:NS], func=AF.Ln), r=["zab"], w=["zab"])
            V(lambda: nc.vector.tensor_scalar(out=ab[:, 0:NS], in0=ab[:, 0:NS], scalar1=abc[:, 1:2], scalar2=None, op0=ALU.mult), r=["zab", "abc"], w=["zab"])
            A(lambda: nc.scalar.activation(out=ab[:, 0:NS], in_=ab[:, 0:NS], func=AF.Exp), r=["zab"], w=["zab"])
            A(lambda: nc.scalar.activation(out=ab[:, NS:2 * NS], in_=pbank[1][0:8, 0:NS], func=AF.Sigmoid), r=["pb1"], w=["zab"])
            for h in range(8):
                T(lambda h=h: nc.tensor.matmul(pbank[2][:, h * 2 * NS:(h + 1) * 2 * NS], ident[0:8, h:h + 1].to_broadcast([8, 128]), ab[:], start=True, stop=True),
                  r=["zab", "cst"], w=["pb2"])
            V(lambda: nc.vector.tensor_copy(out=AB[:], in_=pbank[2][:, 0:16 * NS].rearrange("p (h t s) -> p h t s", t=2, s=NS)[:, :, 0, :]), r=["pb2"], w=["AB"])
            V(lambda: nc.vector.tensor_copy(out=BB[:], in_=pbank[2][:, 0:16 * NS].rearrange("p (h t s) -> p h t s", t=2, s=NS)[:, :, 1, :]), r=["pb2"], w=["BB"])
            load_w(0, 0)
            for h in range(8):
                b = h % 2
                if h + 1 < 8:
                    load_w(h + 1, (h + 1) % 2)
                for c in range(DC):
                    T(lambda c=c: nc.tensor.matmul(pbank[3][0:NS, 0:384], hms[:, c, :], wbuf[b][:, c, 0:384], start=(c == 0), stop=(c == DC - 1)),
                      r=["nw0", "nw1", "hms"], w=["pb3"])
                V(lambda h=h: nc.vector.tensor_copy(out=nr[h % 2][:], in_=pbank[3][0:NS, 0:384]), r=["pb3"], w=[f"nr{h % 2}"])
                for j3 in range(3):
                    S.dma("sp", lambda j3=j3, h=h: nc.sync.dma_start(out=o_sc[:, 2, j3 * 1024 + h * 128:j3 * 1024 + (h + 1) * 128],
                                                                     in_=nr[h % 2][:, j3 * 128:(j3 + 1) * 128]), reads=[f"nr{h % 2}"])
                for j3 in range(3):
                    ch0 = j3 * 1024 + h * 128
                    for c in range(DC):
                        T(lambda c=c: nc.tensor.matmul(pbank[4][:, 0:NS], wbuf[b][:, c, j3 * 128:(j3 + 1) * 128], hms[:, c, :], start=(c == 0), stop=(c == DC - 1)),
                          r=["nw0", "nw1", "hms"], w=["pb4"])
                    hb = hcount[0] % 2
                    hcount[0] += 1
                    S.dma("sp", lambda: nc.sync.dma_start(out=hsl[hb][:], in_=sc_in.rearrange("s j c -> (s j) c")[:, ch0:ch0 + 128]), writes=[f"hsl{hb}"])
                    T(lambda: nc.tensor.transpose(pbank[5][:, 0:3 * NS], hsl[hb][:], ident[0:3 * NS, 0:3 * NS]), r=[f"hsl{hb}", "cst"], w=["pb5"])
                    V(lambda: nc.vector.tensor_copy(out=hT[:], in_=pbank[5][:, 0:3 * NS]), r=["pb5"], w=["hT"])
                    hv = hT[:].rearrange("p (s j) -> p s j", j=3)
                    V(lambda: nc.vector.tensor_scalar(out=pre[:], in0=pbank[4][:, 0:NS], scalar1=cw[:, j3, h, 3:4], scalar2=None, op0=ALU.mult), r=["pb4", "cw"], w=["pre"])
                    for tap in range(3):
                        V(lambda tap=tap: nc.vector.scalar_tensor_tensor(out=pre[:], in0=hv[:, :, tap], scalar=cw[:, j3, h, tap:tap + 1], in1=pre[:],
                                                                         op0=ALU.mult, op1=ALU.add), r=["hT", "cw", "pre"], w=["pre"])
                    A(lambda: nc.scalar.activation(out=QKV[:, j3, h, :], in_=pre[:], func=AF.Silu), r=["pre"], w=["QKV"])
                for c in range(DC):
                    T(lambda c=c: nc.tensor.matmul(pbank[4][:, 0:NS], wbuf[b][:, c, 384:512], hms[:, c, :], start=(c == 0), stop=(c == DC - 1)),
                      r=["nw0", "nw1", "hms"], w=["pb4"])
                A(lambda: nc.scalar.activation(out=GS[:, h, :], in_=pbank[4][:, 0:NS], func=AF.Silu), r=["pb4"], w=["zGS"])
            sqb = P.sb("z_sq", [128, 2 * 8 * NS], BF16, stack=st2)
            rn = P.sb("z_rn", [128, 2 * 8 * NS], stack=st2)
            A(lambda: nc.scalar.activation(out=sqb[:], in_=QKV[:, 0:2, :, :].rearrange("p a h s -> p (a h s)"), func=AF.Square), r=["QKV"], w=["zsq"])
            T(lambda: nc.tensor.matmul(pbank[6][:, 0:256], ones128[:], sqb[:], start=True, stop=True), r=["zsq", "ones128"], w=["pb6"])
            V(lambda: nc.vector.tensor_scalar(out=rn[:], in0=pbank[6][:, 0:256], scalar1=128.0, scalar2=1e-6, op0=ALU.mult, op1=ALU.add), r=["pb6"], w=["zrn"])
            A(lambda: nc.scalar.activation(out=rn[:], in_=rn[:], func=AF.Ln), r=["zrn"], w=["zrn"])
            A(lambda: nc.scalar.activation(out=rn[:], in_=rn[:], func=AF.Exp, scale=-0.5), r=["zrn"], w=["zrn"])
            V(lambda: nc.vector.tensor_tensor(out=QKV[:, 0:2, :, :].rearrange("p a h s -> p (a h s)"), in0=QKV[:, 0:2, :, :].rearrange("p a h s -> p (a h s)"),
                                              in1=rn[:], op=ALU.mult), r=["QKV", "zrn"], w=["QKV"])
            V(lambda: nc.vector.tensor_scalar(out=QKV[:, 0, :, :], in0=QKV[:, 0, :, :], scalar1=128.0 ** -0.5, scalar2=None, op0=ALU.mult), r=["QKV"], w=["QKV"])
            NAK = P.sb("z_NAK", [128, 8, NS], stack=st2)
            BK = P.sb("z_BK", [128, 8, NS], stack=st2)
            vtm = P.sb("z_vtm", [NS, 8, 128], stack=st2)
            V(lambda: nc.vector.scalar_tensor_tensor(out=NAK[:], in0=QKV[:, 1, :, :], scalar=-1.0, in1=AB[:], op0=ALU.mult, op1=ALU.mult), r=["QKV", "AB"], w=["NAK"])
            V(lambda: nc.vector.tensor_tensor(out=BK[:], in0=QKV[:, 1, :, :], in1=BB[:], op=ALU.mult), r=["QKV", "BB"], w=["BK"])
            for h in range(8):
                T(lambda h=h: nc.tensor.transpose(pbank[7][0:NS, (h % 4) * 128:(h % 4 + 1) * 128], QKV[:, 2, h, :], ident), r=["QKV", "cst"], w=["pb7"])
                if h % 4 == 3:
                    V(lambda h=h: nc.vector.tensor_copy(out=vtm[:, h - 3:h + 1, :].rearrange("p h v -> p (h v)"), in_=pbank[7][0:NS, 0:512]), r=["pb7"], w=["vtm"])
            Sin = P.sb("z_Sin", [128, NS, 128], stack=st2)
            Sout = P.sb("z_Sout", [128, NS, 128], stack=st2)
            tSa = P.sb("z_tSa", [128, NS, 128], stack=st2)
            ob = P.sb("z_ob", [128, 8 * NS], stack=st2)
            for h in range(8):
                S.dma("sp", lambda h=h: nc.sync.dma_start(out=Sin[:], in_=sg_in[:, h, :, :].rearrange("s k v -> k s v")), writes=["zSin"])
                banks_ = [2, 3, 6, 7]
                for s_ in range(NS):
                    pi = banks_[s_ // 4]
                    ps_ = pbank[pi][:, (s_ % 4) * 128:(s_ % 4 + 1) * 128]
                    T(lambda s_=s_, ps_=ps_: nc.tensor.matmul(ps_, ident[0:NS, s_:s_ + 1].to_broadcast([NS, 128]), vtm[:, h, :], start=True, stop=False),
                      r=["vtm", "cst"], w=[f"pb{pi}"])
                    T(lambda s_=s_, ps_=ps_: nc.tensor.matmul(ps_, NAK[:, h, s_:s_ + 1].to_broadcast([128, 128]), Sin[:, s_, :], start=False, stop=True),
                      r=["NAK", "zSin"], w=[f"pb{pi}"])
                for s_ in range(NS):
                    pi = banks_[s_ // 4]
                    V(lambda s_=s_, pi=pi: nc.vector.tensor_scalar(out=tSa[:, s_, :], in0=pbank[pi][:, (s_ % 4) * 128:(s_ % 4 + 1) * 128], scalar1=BK[:, h, s_:s_ + 1],
                                                                   scalar2=None, op0=ALU.mult), r=[f"pb{pi}", "BK"], w=[f"ztS{s_}"])
                for s_ in range(NS):
                    V(lambda s_=s_: nc.vector.scalar_tensor_tensor(out=Sout[:, s_, :], in0=Sin[:, s_, :], scalar=AB[:, h, s_:s_ + 1], in1=tSa[:, s_, :],
                                                                   op0=ALU.mult, op1=ALU.add), r=["zSin", "AB", f"ztS{s_}"], w=[f"zSout{s_}"])
                for s_ in range(NS):
                    T(lambda s_=s_: nc.tensor.matmul(pbank[5][:, h * NS + s_:h * NS + s_ + 1], Sout[:, s_, :], QKV[:, 0, h, s_:s_ + 1], start=True, stop=True),
                      r=[f"zSout{s_}", "QKV"], w=["pb5"])
                S.dma("sp", lambda h=h: nc.sync.dma_start(out=o_sg[:, h, :, :].rearrange("s k v -> k s v"), in_=Sout[:]), reads=[f"zSout{s_}" for s_ in range(NS)])
            V(lambda: nc.vector.tensor_copy(out=ob[:], in_=pbank[5][:, 0:8 * NS]), r=["pb5"], w=["zob"])
            A(lambda: nc.scalar.activation(out=sqb[:, 0:8 * NS], in_=ob[:], func=AF.Square), r=["zob"], w=["zsq"])
            T(lambda: nc.tensor.matmul(pbank[6][:, 0:8 * NS], ones128[:], sqb[:, 0:8 * NS], start=True, stop=True), r=["zsq", "ones128"], w=["pb6"])
            V(lambda: nc.vector.tensor_scalar(out=rn[:, 0:8 * NS], in0=pbank[6][:, 0:8 * NS], scalar1=1e-6, scalar2=None, op0=ALU.add), r=["pb6"], w=["zrn"])
            A(lambda: nc.scalar.activation(out=rn[:, 0:8 * NS], in_=rn[:, 0:8 * NS], func=AF.Ln), r=["zrn"], w=["zrn"])
            A(lambda: nc.scalar.activation(out=rn[:, 0:8 * NS], in_=rn[:, 0:8 * NS], func=AF.Exp, scale=-0.5), r=["zrn"], w=["zrn"])
            V(lambda: nc.vector.tensor_tensor(out=ob[:], in0=ob[:], in1=rn[:, 0:8 * NS], op=ALU.mult), r=["zob", "zrn"], w=["zob"])
            V(lambda: nc.vector.scalar_tensor_tensor(out=o_gs[:], in0=ob[:].rearrange("p (h s) -> p h s", s=NS), scalar=gg_t[:, 0:1], in1=GS[:],
                                                     op0=ALU.mult, op1=ALU.mult), r=["zob", "zGS", "gg"], w=["o_gs"])
            S.barrier()

    def dump(name):
        if name in dbg:
            o1 = P.dout(f"dbg_{name}_p", [128, DC, TP])
            o2 = P.dout(f"dbg_{name}_s", [128, DC, NS])
            S.dma("sp", lambda: nc.sync.dma_start(out=o1[:, :, :], in_=xp[:]))
            S.dma("sp", lambda: nc.sync.dma_start(out=o2[:, :, :], in_=xs[:]))
            S.barrier()

    def scope(name):
        return nc.named_scope(name) if "scopes" in dbg else contextlib.nullcontext()

    if "only_odd" not in dbg:
        with scope("ffn00"):
            ffn_sublayer(0, 0)
            S.barrier()
        dump("x0a")
        with scope("mix0"):
            mixer_even()
        dump("x0b")
        with scope("ffn01"):
            ffn_sublayer(0, 1)
            S.barrier()
        dump("x0c")
        with scope("ffn10"):
            ffn_sublayer(1, 0)
            S.barrier()
        dump("x1a")
    if "no_odd" not in dbg:
        with scope("mix1"):
            mixer_odd()
    dump("x1b")
    if "no_odd" not in dbg and "only_odd" not in dbg:
        with scope("ffn11"):
            ffn_sublayer(1, 1)
    S.barrier()
    S.dma("sp", lambda: nc.sync.dma_start(out=yTp[:, :, :], in_=xp[:]))
    S.dma("sp", lambda: nc.sync.dma_start(out=yTs[:, :, :], in_=xs[:]))


def _fm(a):
    T = a.shape[0]
    return np.ascontiguousarray(a.reshape(T, DC, 128).transpose(2, 1, 0))


def _unfm(a):
    return np.ascontiguousarray(a.transpose(2, 1, 0).reshape(a.shape[2], D))


def make_consts():
    c = np.zeros((128, 1800), np.float32)
    c[:, 0:128] = np.eye(128, dtype=np.float32)
    i = np.arange(128)
    c[:, 128:256] = ((i[:, None] // 16 == i[None, :] // 16) & (i[:, None] <= i[None, :])).astype(np.float32)
    c[:, 256:264] = (i[:, None] // 16 == np.arange(8)[None, :]).astype(np.float32)
    c[:, 264:264 + 512] = (np.arange(512) % 16 != 0).astype(np.float32)[None, :]
    c[:, 776:776 + 1024] = 1.0
    return c


def swa_bias_tables(first_invalid):
    q = np.arange(128)[:, None]
    k = np.arange(256)[None, :]
    dist = (128 + q) - k
    valid = (dist >= 0) & (dist < 128)
    out = np.zeros((128, 2, 256), np.float32)
    b = np.where(valid, dist, 1.0e7).astype(np.float32)
    out[:, 0] = b
    b1 = b.copy()
    if first_invalid:
        b1[:, 0:128] = 1.0e7
    out[:, 1] = b1
    return out


def prep_inputs(inp):
    f32 = np.float32
    shared = {}
    ada_w = inp["ada_w"]
    shared["ada_r"] = np.ascontiguousarray(
        ada_w.reshape(2, DC, 128, 18, 512).transpose(0, 3, 2, 1, 4))
    shared["ada_bT"] = np.ascontiguousarray(inp["ada_b"].reshape(2, 72, 128).transpose(2, 0, 1))
    shared["ln_gT"] = np.ascontiguousarray(inp["ln_g"].reshape(48, 128).T)
    shared["ln_bT"] = np.ascontiguousarray(inp["ln_b"].reshape(48, 128).T)
    wu = inp["ffn_w_up"].reshape(2, 2, DC, 128, 2, 11, 256)
    shared["wup_r"] = np.ascontiguousarray(wu.transpose(0, 1, 5, 3, 2, 4, 6)).reshape(2, 2, 11, 128, DC, 512)
    wd = inp["ffn_w_down"].reshape(2, 2, FC, 128, 4, 256)
    shared["wdn_r"] = np.ascontiguousarray(wd.transpose(0, 1, 4, 3, 2, 5))
    shared["consts"] = make_consts()
    w = inp["even_w_in"][0]
    tiles_ = []
    qcols = []
    for g in range(4):
        qcols += list(range((0 * 4 + g) * 64, (0 * 4 + g) * 64 + 64)) + list(range((1 * 4 + g) * 64, (1 * 4 + g) * 64 + 64))
    tiles_.append(w[:, qcols])
    kv = np.zeros((D, 512), f32)
    kv[:, 0:256] = w[:, 512:768]
    tiles_.append(kv)
    for hh in range(4):
        cols = []
        for base in (768, 1280, 2304, 1792):
            cols += list(range(base + hh * 128, base + (hh + 1) * 128))
        tiles_.append(w[:, cols])
    win0 = np.stack(tiles_, 0)
    shared["win0_r"] = np.ascontiguousarray(win0.reshape(6, DC, 128, 512).transpose(0, 2, 1, 3))
    shared["wout0_r"] = np.ascontiguousarray(inp["even_w_out"][0].reshape(8, 128, D).transpose(1, 0, 2))
    shared["sinks_b"] = np.ascontiguousarray(np.broadcast_to(inp["swa_sinks"][0][None, :], (128, 8))).astype(f32)
    shared["lb_logits"] = np.ascontiguousarray(inp["hgrn_lb_logits"].reshape(2, 4, 128).transpose(2, 0, 1))
    shared["hgrn_g"] = np.ascontiguousarray(inp["hgrn_norm_g"][0].reshape(128, 1))
    bias_tabs = [swa_bias_tables(False), swa_bias_tables(True)]
    w1 = inp["odd_w_in"][0]
    t1 = []
    for h in range(8):
        cols = []
        for base in (0, 1024, 2048, 3072):
            cols += list(range(base + h * 128, base + (h + 1) * 128))
        t1.append(w1[:, cols])
    ab_t = np.zeros((D, 512), f32)
    ab_t[:, 0:16] = w1[:, 4096:4112]
    t1.append(ab_t)
    win1 = np.stack(t1, 0)
    shared["win1_r"] = np.ascontiguousarray(win1.reshape(9, DC, 128, 512).transpose(0, 2, 1, 3))
    shared["wout1_r"] = np.ascontiguousarray(inp["odd_w_out"][0].reshape(8, 128, D).transpose(1, 0, 2))
    shared["convw_r"] = np.ascontiguousarray(inp["gdn_conv_w"][0].reshape(4, 3, 8, 128).transpose(3, 1, 2, 0))
    shared["abc"] = np.ascontiguousarray(np.stack([inp["gdn_dt_bias"][0], inp["gdn_a_log"][0]], 1)).astype(f32)
    shared["gdn_g"] = np.ascontiguousarray(inp["gdn_norm_g"][0].reshape(128, 1))
    ii = np.arange(128)
    S_, T_ = ii[:, None], ii[None, :]
    parts = [(S_ <= T_), (S_ < T_), (S_ // 16 == T_ // 16) & (S_ <= T_), (S_ // 16 == T_ // 16) & (S_ >= T_)]
    for m_ in (16, 32, 64):
        lev = (S_ // (2 * m_) == T_ // (2 * m_)) & (S_ // m_ != T_ // m_)
        parts += [lev & (S_ < T_), lev & (S_ > T_)]
    shared["tri"] = np.ascontiguousarray(np.concatenate([p_.astype(f32) for p_ in parts], 1))
    dt_ = np.zeros((128, 64), f32)
    pidx = np.arange(128)
    kidx = (pidx % 8)[:, None] * 16 + np.arange(16)[None, :]
    dt_[:, 0:16] = np.where(kidx >= 1, 128 - kidx, 1.0e7)
    dt_[:, 16:32] = (pidx[:, None] // 8 == np.arange(16)[None, :]).astype(f32)
    sperm = np.array([inp["swa_sinks"][0][(hd % 2) * 4 + hd // 2] for hd in range(8)], f32)
    dt_[:, 32:40] = sperm[None, :]
    dt_[0:8, 40] = sperm
    shared["dec_tab"] = dt_
    shared["dec_rep"] = np.ascontiguousarray(dt_[:, 16:32].T)
    maps = []
    for core in range(NCORES):
        b, seg = core // 4, core % 4
        m = dict(shared)
        m["xTp"] = _fm(inp["x_prompt"][b, seg * TP:(seg + 1) * TP])
        m["xTs"] = _fm(inp["x_sample"][core * NS:(core + 1) * NS, 0])
        cc = np.zeros((NCOND, D), f32)
        cc[0] = inp["c_prompt"][b]
        cc[1:1 + NS] = inp["c_sample"][core * NS:(core + 1) * NS]
        m["ccT"] = _fm(cc)
        m["swa_dist"] = bias_tabs[1 if seg == 0 else 0]
        pc = np.zeros((128, 16), f32)
        for r_ in range(4):
            pc[:, r_] = 1.0 if r_ == seg - 1 else 0.0
            pc[:, 4 + r_] = 1.0 if r_ < seg else 0.0
        m["percore"] = pc
        sl = slice(core * NS, (core + 1) * NS)
        m["ck_in"] = np.ascontiguousarray(inp["cache_swa_k"][0, sl].reshape(NS, 128, 128))
        m["cv_in"] = np.ascontiguousarray(inp["cache_swa_v"][0, sl].reshape(NS, 128, 128))
        m["sh_in"] = np.ascontiguousarray(inp["state_hgrn"][0, sl])
        m["sg_in"] = np.ascontiguousarray(inp["state_gdn"][0, sl])
        m["sc_in"] = np.ascontiguousarray(inp["state_gdn_conv"][0, sl])
        maps.append(m)
    return maps


_CACHE = {}


def run(inp, debug=(), trace=False, cores=None):
    key = tuple(sorted(debug))
    if key not in _CACHE:
        _CACHE[key] = build_program(debug)
    P = _CACHE[key]
    maps = prep_inputs(inp)
    maps = [{k: v for k, v in m.items() if k in P.inputs} for m in maps]
    if cores is not None:
        maps = [maps[c] for c in cores]
    res = run_bass_kernel_spmd(P.nc, maps, core_ids=list(range(len(maps))), **({"trace": True} if trace else {}))
    return res


def kernel(**inp):
    inp = {k: np.asarray(v) for k, v in inp.items()}
    res = run(inp)
    R = res.results
    f32 = np.float32
    y_prompt = np.zeros((2, 8192, D), f32)
    y_sample = np.zeros((128, 1, D), f32)
    p_k = np.zeros((1, 2, 128, 2, 64), f32)
    p_v = np.zeros((1, 2, 128, 2, 64), f32)
    p_h = np.zeros((1, 2, 4, 128, 128), f32)
    p_g = np.zeros((1, 2, 8, 128, 128), f32)
    p_c = np.zeros((1, 2, 3, 3072), f32)
    s_k = np.zeros((1, 128, 128, 2, 64), f32)
    s_v = np.zeros((1, 128, 128, 2, 64), f32)
    s_h = np.zeros((1, 128, 4, 128, 128), f32)
    s_g = np.zeros((1, 128, 8, 128, 128), f32)
    s_c = np.zeros((1, 128, 3, 3072), f32)
    for core in range(NCORES):
        b, seg = core // 4, core % 4
        r = R[core]
        y_prompt[b, seg * TP:(seg + 1) * TP] = _unfm(r["yTp"])
        sl = slice(core * NS, (core + 1) * NS)
        y_sample[sl, 0] = _unfm(r["yTs"])
        s_k[0, sl] = r["o_sk"].reshape(NS, 128, 2, 64)
        s_v[0, sl] = r["o_sv"].reshape(NS, 128, 2, 64)
        s_h[0, sl] = r["o_sh"]
        s_g[0, sl] = r["o_sg"]
        s_c[0, sl] = r["o_sc"]
        if seg == 3:
            p_k[0, b] = r["o_swak"].reshape(128, 2, 64)
            p_v[0, b] = r["o_swav"].reshape(128, 2, 64)
            p_h[0, b] = r["o_hgrn"].transpose(1, 0, 2)
            p_g[0, b] = r["o_gdn"].transpose(1, 0, 2)
            p_c[0, b] = r["o_gconv"].transpose(3, 1, 2, 0).reshape(3, 3072)
    return (y_prompt, y_sample, p_k, p_v, p_h, p_g, p_c, s_k, s_v, s_h, s_g, s_c)
```

```python
import contextlib
import numpy as np
import concourse.bass as bass
import concourse.mybir as mybir
from concourse.bass_utils import run_bass_kernel_spmd

F32 = mybir.dt.float32
BF16 = mybir.dt.bfloat16
AF = mybir.ActivationFunctionType
ALU = mybir.AluOpType
AX = mybir.AxisListType

NCORES = 8
D = 1024
DC = 8
TP = 2048
NS = 16
NCOND = 18
DFF = 2816
FC = 22
TT = 512
DN_ALPHA = 4.0 ** 0.25
LN_EPS = 1e-5
EPOCH = 30000


class Sync:
    def __init__(self, nc, es, n_dma_sems=32):
        self.nc = nc
        self.es = es
        self.eng = {"pe": nc.tensor, "act": nc.scalar, "dve": nc.vector, "pool": nc.gpsimd, "sp": nc.sync}
        self.count = {e: 0 for e in self.eng}
        self.epoch = {e: 0 for e in self.eng}
        self.sems = {}
        for e in self.eng:
            self.sems[(e, 0)] = es.enter_context(nc.semaphore(f"s_{e}_0"))
        self.known = {e: {} for e in self.eng}
        self.dma_sems = [es.enter_context(nc.semaphore(f"dsem{i}")) for i in range(n_dma_sems)]
        self.dma_val = [0] * n_dma_sems
        self.dma_next = 0
        self.cc_sem = es.enter_context(nc.semaphore("ccsem"))
        self.cc_val = 0
        self.res = {}
        self.n_inst = {e: 0 for e in self.eng}

    def _wait(self, e, tok):
        if tok is None:
            return
        sem, val, src = tok
        if src == e and e == "pe":
            return
        k = self.known[e]
        sid = id(sem)
        if k.get(sid, 0) >= val:
            return
        self.eng[e].wait_ge(sem, val)
        k[sid] = val

    def _deps(self, e, reads, writes):
        for key in reads:
            st = self.res.get(key)
            if st is not None:
                self._wait(e, st["w"])
        for key in writes:
            st = self.res.get(key)
            if st is not None:
                self._wait(e, st["w"])
                for t in st["r"]:
                    self._wait(e, t)

    def _update(self, tok, reads, writes):
        for key in reads:
            st = self.res.setdefault(key, {"w": None, "r": []})
            st["r"] = [t for t in st["r"] if not (t[2] is not None and t[2] == tok[2] and t[0] is tok[0])]
            st["r"].append(tok)
        for key in writes:
            self.res[key] = {"w": tok, "r": []}

    def op(self, e, fn, reads=(), writes=()):
        self._deps(e, reads, writes)
        if self.count[e] >= EPOCH:
            self.epoch[e] += 1
            self.count[e] = 0
            self.sems[(e, self.epoch[e])] = self.es.enter_context(self.nc.semaphore(f"s_{e}_{self.epoch[e]}"))
        sem = self.sems[(e, self.epoch[e])]
        fn().then_inc(sem, 1)
        self.count[e] += 1
        self.n_inst[e] += 1
        tok = (sem, self.count[e], e)
        self._update(tok, reads, writes)
        return tok

    def dma(self, q, fns, reads=(), writes=()):
        if callable(fns):
            fns = [fns]
        self._deps(q, reads, writes)
        i = self.dma_next
        self.dma_next = (self.dma_next + 1) % len(self.dma_sems)
        sem = self.dma_sems[i]
        if self.dma_val[i] > 0:
            self._wait(q, (sem, self.dma_val[i], None))
        for fn in fns:
            fn().then_inc(sem, 16)
            self.dma_val[i] += 16
        tok = (sem, self.dma_val[i], None)
        self._update(tok, reads, writes)
        return tok

    def collective(self, fn, reads=(), writes=()):
        self._deps("pool", reads, writes)
        fn().then_inc(self.cc_sem)
        self.cc_val += 1
        tok = (self.cc_sem, self.cc_val, None)
        self._update(tok, reads, writes)
        return tok

    def barrier(self):
        toks = []
        for e in self.eng:
            if self.count[e] > 0:
                toks.append((self.sems[(e, self.epoch[e])], self.count[e], e))
        for i, s in enumerate(self.dma_sems):
            if self.dma_val[i] > 0:
                toks.append((s, self.dma_val[i], None))
        if self.cc_val > 0:
            toks.append((self.cc_sem, self.cc_val, None))
        for e in self.eng:
            for t in toks:
                self._wait(e, t)
        self.res = {}

    def final_wait(self, e="sp"):
        self.barrier()


class Prog:
    def __init__(self, debug=()):
        self.debug = set(debug)
        self.nc = bass.Bass("TRN2", target_bir_lowering=False)
        self.es = contextlib.ExitStack()
        self.inputs = {}
        self.outputs = {}

    def din(self, name, shape, dt=F32):
        t = self.nc.dram_tensor(name, list(shape), dt, kind="ExternalInput").ap()
        self.inputs[name] = t
        return t

    def dout(self, name, shape, dt=F32):
        t = self.nc.dram_tensor(name, list(shape), dt, kind="ExternalOutput").ap()
        self.outputs[name] = t
        return t

    def sb(self, name, shape, dt=F32, stack=None):
        self.uid = getattr(self, "uid", 0) + 1
        return (stack or self.es).enter_context(self.nc.sbuf_tensor(f"{name}_{self.uid}", list(shape), dt))

    def ps(self, name, shape, dt=F32, stack=None):
        return (stack or self.es).enter_context(self.nc.psum_tensor(name, list(shape), dt))


def build_program(debug=()):
    P = Prog(debug)
    nc = P.nc
    with P.es:
        S = Sync(nc, P.es)
        P.S = S
        _build(P, nc, S)
        S.final_wait()
    return P


def _build(P, nc, S):
    dbg = P.debug
    xTp = P.din("xTp", [128, DC, TP])
    xTs = P.din("xTs", [128, DC, NS])
    ccT = P.din("ccT", [128, DC, NCOND])
    ada_r = P.din("ada_r", [2, 18, 128, DC, 512])
    ada_bT = P.din("ada_bT", [128, 2, 72])
    ln_gT = P.din("ln_gT", [128, 48])
    ln_bT = P.din("ln_bT", [128, 48])
    wup_r = P.din("wup_r", [2, 2, 11, 128, DC, 512])
    wdn_r = P.din("wdn_r", [2, 2, 4, 128, FC, 256])
    consts = P.din("consts", [128, 1800])

    yTp = P.dout("yTp", [128, DC, TP])
    yTs = P.dout("yTs", [128, DC, NS])

    xp = P.sb("xp", [128, DC, TP])
    xs = P.sb("xs", [128, DC, NS])
    mods = P.sb("mods", [128, 2, 72, NCOND])
    lng = P.sb("lng", [128, 48])
    lnb = P.sb("lnb", [128, 48])
    cst = P.sb("cst", [128, 1800])
    onesb = P.sb("onesb", [128, 128], BF16)
    adab = P.sb("adab", [128, 2, 72])
    pbank = [P.ps(f"pb{i}", [128, 512]) for i in range(8)]

    S.dma("sp", lambda: nc.sync.dma_start(out=xp[:], in_=xTp[:, :, :]), writes=["xp"])
    S.dma("sp", lambda: nc.sync.dma_start(out=xs[:], in_=xTs[:, :, :]), writes=["xs"])
    S.dma("sp", lambda: nc.sync.dma_start(out=lng[:], in_=ln_gT[:, :]), writes=["lng"])
    S.dma("sp", lambda: nc.sync.dma_start(out=lnb[:], in_=ln_bT[:, :]), writes=["lnb"])
    S.dma("sp", lambda: nc.sync.dma_start(out=cst[:], in_=consts[:, :]), writes=["cst"])
    S.dma("sp", lambda: nc.sync.dma_start(out=adab[:], in_=ada_bT[:, :, :]), writes=["adab"])
    S.op("dve", lambda: nc.vector.memset(onesb[:], 1.0 / 1024.0), writes=["onesb"])

    with contextlib.ExitStack() as st:
        cs = P.sb("cs", [128, DC, NCOND], stack=st)
        csb = P.sb("csb", [128, DC, NCOND], BF16, stack=st)
        abuf = [P.sb(f"abuf{i}", [128, DC, 512], BF16, stack=st) for i in range(3)]
        S.dma("sp", lambda: nc.sync.dma_start(out=cs[:], in_=ccT[:, :, :]), writes=["cs"])
        S.op("act", lambda: nc.scalar.activation(out=csb[:], in_=cs[:], func=AF.Silu), reads=["cs"], writes=["csb"])
        blocks = [(l, j) for l in range(2) for j in range(18)]

        def load(i):
            l, j = blocks[i]
            b = i % 3
            S.dma("pool", lambda: nc.gpsimd.dma_start(out=abuf[b][:], in_=ada_r[l, j]), writes=[f"abuf{b}"])

        for i in range(2):
            load(i)
        for i, (l, j) in enumerate(blocks):
            b = i % 3
            pb = pbank[i % 2]
            for q in range(4):
                for c in range(DC):
                    S.op("pe", lambda q=q, c=c: nc.tensor.matmul(
                        pb[:, q * NCOND:(q + 1) * NCOND], abuf[b][:, c, q * 128:(q + 1) * 128], csb[:, c, :],
                        start=(c == 0), stop=(c == DC - 1)),
                        reads=[f"abuf{b}", "csb"], writes=[f"pb{i % 2}"])
            S.op("dve", lambda: nc.vector.tensor_tensor(
                out=mods[:, l, 4 * j:4 * j + 4, :],
                in0=pb[:, 0:4 * NCOND].rearrange("p (q n) -> p q n", n=NCOND),
                in1=adab[:, l, 4 * j:4 * j + 4].unsqueeze(2).to_broadcast([128, 4, NCOND]),
                op=ALU.add), reads=[f"pb{i % 2}", "adab"], writes=["mods"])
            if i + 2 < len(blocks):
                load(i + 2)
        for l in range(2):
            for j in (1, 4, 7):
                S.op("dve", lambda l=l, j=j: nc.vector.tensor_scalar(
                    out=mods[:, l, 8 * j:8 * j + 8, :], in0=mods[:, l, 8 * j:8 * j + 8, :],
                    scalar1=1.0, scalar2=None, op0=ALU.add), reads=["mods"], writes=["mods"])
            for j in (2, 8):
                S.op("dve", lambda l=l, j=j: nc.vector.tensor_scalar(
                    out=mods[:, l, 8 * j:8 * j + 8, :], in0=mods[:, l, 8 * j:8 * j + 8, :],
                    scalar1=0.5, scalar2=None, op0=ALU.mult), reads=["mods"], writes=["mods"])
        S.barrier()

    if "mods" in dbg:
        o = P.dout("dbg_mods", [128, 2, 72, NCOND])
        S.dma("sp", lambda: nc.sync.dma_start(out=o[:, :, :, :], in_=mods[:]), reads=["mods"])

    tiles = [("p", i * TT, TT) for i in range(TP // TT)] + [("s", 0, NS)]
    if "only_s" in dbg:
        tiles = [("s", 0, NS)]
    if "only_p0" in dbg:
        tiles = [("p", 0, TT)]

    def xview(kind, t0, n):
        return (xp if kind == "p" else xs)[:, :, t0:t0 + n]

    def ffn_sublayer(l, k):
        jsh, jsc, jg = (0, 1, 2) if k == 0 else (6, 7, 8)
        lnidx = (l * 3 + (0 if k == 0 else 2)) * 8
        with contextlib.ExitStack() as st:
            h = P.sb("f_h", [128, DC, TT], BF16, stack=st)
            act = P.sb("f_act", [128, FC, TT], BF16, stack=st)
            sg = [P.sb(f"f_sg{i}", [128, TT], stack=st) for i in range(2)]
            wu = [P.sb(f"f_wu{i}", [128, DC, 512], BF16, stack=st) for i in range(3)]
            wd = [P.sb(f"f_wd{i}", [128, FC, 256], BF16, stack=st) for i in range(2)]
            xb = P.sb("f_xb", [128, DC, TT], BF16, stack=st)
            sq = P.sb("f_sq", [128, DC, TT], BF16, stack=st)
            r = [P.sb(f"f_r{i}", [128, TT], stack=st) for i in range(2)]
            lt = [P.sb(f"f_lt{i}", [128, TT], stack=st) for i in range(4)]
            wl = []
            for ti in range(len(tiles)):
                for j in range(11):
                    wl.append(("u", j))
                for j in range(4):
                    wl.append(("d", j))
            cnt = {"u": 0, "d": 0}
            slot = []
            for kind_, j in wl:
                if kind_ == "u":
                    slot.append(("u", cnt["u"] % 3)); cnt["u"] += 1
                else:
                    slot.append(("d", cnt["d"] % 2)); cnt["d"] += 1
            issued = [0]
            prev_occ = []
            last_of = {}
            for i, sl in enumerate(slot):
                prev_occ.append(last_of.get(sl, -1))
                last_of[sl] = i

            def issue_upto(n, cur):
                while issued[0] < min(n, len(wl)) and prev_occ[issued[0]] < cur:
                    i = issued[0]
                    kind_, j = wl[i]
                    _, b = slot[i]
                    if kind_ == "u":
                        S.dma("pool", lambda: nc.gpsimd.dma_start(out=wu[b][:], in_=wup_r[l, k, j]), writes=[f"wu{b}"])
                    else:
                        S.dma("pool", lambda: nc.gpsimd.dma_start(out=wd[b][:], in_=wdn_r[l, k, j]), writes=[f"wd{b}"])
                    issued[0] += 1

            wi = 0
            issue_upto(2, 0)
            def make_h(kind, t0, n):
                xv = xview(kind, t0, n)
                xk = f"x{kind}{t0}"
                for c in range(DC):
                    if kind == "p":
                        S.op("dve", lambda c=c: nc.vector.tensor_scalar(
                            out=h[:, c, :n], in0=xv[:, c, :], scalar1=mods[:, l, jsc * 8 + c, 0:1],
                            scalar2=mods[:, l, jsh * 8 + c, 0:1], op0=ALU.mult, op1=ALU.add),
                            reads=[xk, "mods"], writes=["h"])
                    else:
                        S.op("dve", lambda c=c: nc.vector.tensor_tensor(
                            out=sg[0][:, :n], in0=xv[:, c, :], in1=mods[:, l, jsc * 8 + c, 1:1 + NS], op=ALU.mult),
                            reads=[xk, "mods"], writes=["sg0"])
                        S.op("dve", lambda c=c: nc.vector.tensor_tensor(
                            out=h[:, c, :n], in0=sg[0][:, :n], in1=mods[:, l, jsh * 8 + c, 1:1 + NS], op=ALU.add),
                            reads=["sg0", "mods"], writes=["h"])

            make_h(*tiles[0])
            pending_ln = [None]
            for ti, (kind, t0, n) in enumerate(tiles):
                xv = xview(kind, t0, n)
                xk = f"x{kind}{t0}"
                if "ffn_int" in dbg and l == 0 and k == 0 and ti == 0:
                    S.barrier()
                    o = P.dout("dbg_h", [128, DC, TT], BF16)
                    S.dma("sp", lambda: nc.sync.dma_start(out=o[:, :, :], in_=h[:]), reads=["h"])
                for j in range(11):
                    _, b = slot[wi]
                    issue_upto(wi + 3, wi)
                    if j == 3 and pending_ln[0] is not None:
                        pending_ln[0]()
                        pending_ln[0] = None
                    for i2 in range(2):
                        f = 2 * j + i2
                        pg = pbank[(f % 2) * 2]
                        pu = pbank[(f % 2) * 2 + 1]
                        for c in range(DC):
                            S.op("pe", lambda c=c: nc.tensor.matmul(
                                pg[:, :n], wu[b][:, c, i2 * 128:(i2 + 1) * 128], h[:, c, :n],
                                start=(c == 0), stop=(c == DC - 1)),
                                reads=[f"wu{b}", "h"], writes=[f"pb{(f % 2) * 2}"])
                        for c in range(DC):
                            S.op("pe", lambda c=c: nc.tensor.matmul(
                                pu[:, :n], wu[b][:, c, 256 + i2 * 128:256 + (i2 + 1) * 128], h[:, c, :n],
                                start=(c == 0), stop=(c == DC - 1)),
                                reads=[f"wu{b}", "h"], writes=[f"pb{(f % 2) * 2 + 1}"])
                        S.op("act", lambda: nc.scalar.activation(out=sg[f % 2][:, :n], in_=pg[:, :n], func=AF.Silu),
                             reads=[f"pb{(f % 2) * 2}"], writes=[f"sg{f % 2}"])
                        S.op("dve", lambda f=f: nc.vector.tensor_tensor(
                            out=act[:, f, :n], in0=sg[f % 2][:, :n], in1=pu[:, :n], op=ALU.mult),
                            reads=[f"sg{f % 2}", f"pb{(f % 2) * 2 + 1}"], writes=[f"act{f}"])
                    wi += 1
                if ti + 1 < len(tiles):
                    make_h(*tiles[ti + 1])
                if "ffn_int" in dbg and l == 0 and k == 0 and ti == 0:
                    S.barrier()
                    o = P.dout("dbg_act", [128, FC, TT], BF16)
                    S.dma("sp", lambda: nc.sync.dma_start(out=o[:, :, :], in_=act[:]), reads=[f"act{f}" for f in range(FC)])
                for j in range(4):
                    _, b = slot[wi]
                    issue_upto(wi + 3, wi)
                    for i2 in range(2):
                        c_o = 2 * j + i2
                        pd = pbank[4 + (c_o % 2)]
                        for f in range(FC):
                            S.op("pe", lambda f=f: nc.tensor.matmul(
                                pd[:, :n], wd[b][:, f, i2 * 128:(i2 + 1) * 128], act[:, f, :n],
                                start=(f == 0), stop=(f == FC - 1)),
                                reads=[f"wd{b}", f"act{f}"], writes=[f"pb{4 + (c_o % 2)}"])
                        rr = r[c_o % 2]
                        if kind == "p":
                            S.op("act", lambda: nc.scalar.activation(
                                out=rr[:, :n], in_=pd[:, :n], func=AF.Copy, scale=mods[:, l, jg * 8 + c_o, 0:1]),
                                reads=[f"pb{4 + (c_o % 2)}", "mods"], writes=[f"r{c_o % 2}"])
                        else:
                            S.op("dve", lambda: nc.vector.tensor_tensor(
                                out=rr[:, :n], in0=pd[:, :n], in1=mods[:, l, jg * 8 + c_o, 1:1 + NS], op=ALU.mult),
                                reads=[f"pb{4 + (c_o % 2)}", "mods"], writes=[f"r{c_o % 2}"])
                        S.op("dve", lambda: nc.vector.scalar_tensor_tensor(
                            out=xv[:, c_o, :], in0=xv[:, c_o, :], scalar=DN_ALPHA, in1=rr[:, :n],
                            op0=ALU.mult, op1=ALU.add),
                            reads=[f"r{c_o % 2}", xk, f"{xk}c{c_o}"], writes=[f"{xk}c{c_o}"])
                        S.op("act", lambda: nc.scalar.copy(out=xb[:, c_o, :n], in_=xv[:, c_o, :]),
                             reads=[f"{xk}c{c_o}"], writes=[f"xb{c_o}"])
                        S.op("act", lambda: nc.scalar.activation(out=sq[:, c_o, :n], in_=xv[:, c_o, :], func=AF.Square),
                             reads=[f"{xk}c{c_o}"], writes=[f"sq{c_o}"])
                    wi += 1
                if "ffn_int" in dbg and l == 0 and k == 0 and ti == 0:
                    S.barrier()
                    o = P.dout("dbg_pre", [128, DC, n])
                    S.dma("sp", lambda: nc.sync.dma_start(out=o[:, :, :], in_=xv), reads=[xk])
                    S.barrier()
                pending_ln[0] = (lambda xk=xk, xv=xv, n=n: layer_norm_tile(xk, xv, n, xb, sq, lt, lnidx))
            pending_ln[0]()
            pending_ln[0] = None

    def layer_norm_tile(xk, xv, n, xb, sq, lt, lnidx):
        pm, pq = pbank[6], pbank[7]
        for c in range(DC):
            S.op("pe", lambda c=c: nc.tensor.matmul(pm[:, :n], onesb[:], xb[:, c, :n], start=(c == 0), stop=(c == DC - 1)),
                 reads=["onesb", f"xb{c}"], writes=["pb6"])
        for c in range(DC):
            S.op("pe", lambda c=c: nc.tensor.matmul(pq[:, :n], onesb[:], sq[:, c, :n], start=(c == 0), stop=(c == DC - 1)),
                 reads=["onesb", f"sq{c}"], writes=["pb7"])
        msq, var, rstd, nmr = lt
        S.op("act", lambda: nc.scalar.activation(out=msq[:, :n], in_=pm[:, :n], func=AF.Square), reads=["pb6"], writes=["lt0"])
        S.op("dve", lambda: nc.vector.tensor_tensor(out=var[:, :n], in0=pq[:, :n], in1=msq[:, :n], op=ALU.subtract),
             reads=["pb7", "lt0"], writes=["lt1"])
        S.op("dve", lambda: nc.vector.tensor_scalar(out=var[:, :n], in0=var[:, :n], scalar1=LN_EPS, scalar2=None, op0=ALU.add),
             reads=["lt1"], writes=["lt1"])
        S.op("act", lambda: nc.scalar.activation(out=var[:, :n], in_=var[:, :n], func=AF.Ln), reads=["lt1"], writes=["lt1"])
        S.op("act", lambda: nc.scalar.activation(out=rstd[:, :n], in_=var[:, :n], func=AF.Exp, scale=-0.5), reads=["lt1"], writes=["lt2"])
        S.op("dve", lambda: nc.vector.scalar_tensor_tensor(
            out=nmr[:, :n], in0=pm[:, :n], scalar=-1.0, in1=rstd[:, :n], op0=ALU.mult, op1=ALU.mult),
            reads=["pb6", "lt2"], writes=["lt3"])
        for c in range(DC):
            S.op("dve", lambda c=c: nc.vector.tensor_tensor(out=xv[:, c, :], in0=xv[:, c, :], in1=rstd[:, :n], op=ALU.mult),
                 reads=["lt2", f"{xk}c{c}"], writes=[f"{xk}c{c}"])
        for c in range(DC):
            S.op("dve", lambda c=c: nc.vector.tensor_tensor(out=xv[:, c, :], in0=xv[:, c, :], in1=nmr[:, :n], op=ALU.add),
                 reads=["lt3", f"{xk}c{c}"], writes=[f"{xk}c{c}"])
        for c in range(DC):
            S.op("act", lambda c=c: nc.scalar.activation(
                out=xv[:, c, :], in_=xv[:, c, :], func=AF.Identity,
                scale=lng[:, lnidx + c:lnidx + c + 1], bias=lnb[:, lnidx + c:lnidx + c + 1]),
                reads=[f"{xk}c{c}", "lng", "lnb"], writes=[f"{xk}c{c}", xk])

    def V(fn, r=(), w=()):
        return S.op("dve", fn, reads=r, writes=w)

    def A(fn, r=(), w=()):
        return S.op("act", fn, reads=r, writes=w)

    def T(fn, r=(), w=()):
        return S.op("pe", fn, reads=r, writes=w)

    ident = cst[:, 0:128]
    mask16T = cst[:, 128:256]
    chunkmask = cst[:, 256:264]
    rmask = cst[:, 264:264 + 512]
    onesf = cst[:, 776:776 + 1024]
    ones128 = P.sb("ones128", [128, 128], BF16)
    V(lambda: nc.vector.memset(ones128[:], 1.0 / 128.0), w=["ones128"])

    win0 = P.din("win0_r", [6, 128, DC, 512])
    wout0 = P.din("wout0_r", [128, 8, D])
    swab = P.din("swa_dist", [128, 2, 256])
    sinkb = P.din("sinks_b", [128, 8])
    lbl = P.din("lb_logits", [128, 2, 4])
    hng = P.din("hgrn_g", [128, 1])
    pcore = P.din("percore", [128, 16])
    XW = 772
    xsrc = nc.dram_tensor("xch_src0", [128, XW], F32)
    xdst = nc.dram_tensor("xch_dst0", [4 * 128, XW], F32)
    o_swak = P.dout("o_swak", [128, 128])
    o_swav = P.dout("o_swav", [128, 128])
    o_hgrn = P.dout("o_hgrn", [128, 4, 128])
    ck_in = P.din("ck_in", [NS, 128, 128])
    cv_in = P.din("cv_in", [NS, 128, 128])
    sh_in = P.din("sh_in", [NS, 4, 128, 128])
    dec_tab = P.din("dec_tab", [128, 64])
    dec_rep = P.din("dec_rep", [NS, 128])
    o_sk = P.dout("o_sk", [NS, 128, 128])
    o_sv = P.dout("o_sv", [NS, 128, 128])
    o_sh = P.dout("o_sh", [NS, 4, 128, 128])

    def mixer_even():
        l = 0
        jsh, jsc, jg = 3, 4, 5
        with contextlib.ExitStack() as st:
            sink_t = P.sb("m_sink", [128, 8], stack=st)
            lb_t = P.sb("m_lb", [128, 2, 4], stack=st)
            lbv = P.sb("m_lbv", [128, 4], stack=st)
            oml = P.sb("m_oml", [128, 4], stack=st)
            hg_t = P.sb("m_hg", [128, 1], stack=st)
            pc_t = P.sb("m_pc", [128, 16], stack=st)
            xs_t = P.sb("m_xs", [128, XW], stack=st)
            S0 = P.sb("m_S0", [128, 4, 128], stack=st)
            o_a = P.sb("m_oa", [128, 4, TP], BF16, stack=st)
            o_b = P.sb("m_ob", [128, 4, TP], BF16, stack=st)
            o_as = P.sb("m_oas", [128, 4, NS], BF16, stack=st)
            o_bs = P.sb("m_obs", [128, 4, NS], BF16, stack=st)
            st_hm = contextlib.ExitStack()
            hm = P.sb("m_hm", [128, DC, TP], BF16, stack=st_hm)
            wbuf = [P.sb(f"m_w{i}", [128, DC, 512], BF16, stack=st_hm) for i in range(2)]
            S.dma("sp", lambda: nc.sync.dma_start(out=sink_t[:], in_=sinkb[:, :]), writes=["sink"])
            S.dma("sp", lambda: nc.sync.dma_start(out=lb_t[:], in_=lbl[:, :, :]), writes=["lb_t"])
            S.dma("sp", lambda: nc.sync.dma_start(out=hg_t[:], in_=hng[:, :]), writes=["hg"])
            S.dma("sp", lambda: nc.sync.dma_start(out=pc_t[:], in_=pcore[:, :]), writes=["pc"])
            V(lambda: nc.vector.tensor_tensor(out=lbv[:], in0=lb_t[:, 1, :], in1=lb_t[:, 0, :], op=ALU.subtract), r=["lb_t"], w=["lbv"])
            A(lambda: nc.scalar.activation(out=lbv[:], in_=lbv[:], func=AF.Sigmoid), r=["lbv"], w=["lbv"])
            V(lambda: nc.vector.tensor_scalar(out=oml[:], in0=lbv[:], scalar1=-1.0, scalar2=1.0, op0=ALU.mult, op1=ALU.add), r=["lbv"], w=["oml"])
            for c in range(DC):
                for tt in range(TP // TT):
                    V(lambda c=c, tt=tt: nc.vector.tensor_scalar(
                        out=hm[:, c, tt * TT:(tt + 1) * TT], in0=xp[:, c, tt * TT:(tt + 1) * TT],
                        scalar1=mods[:, l, jsc * 8 + c, 0:1], scalar2=mods[:, l, jsh * 8 + c, 0:1],
                        op0=ALU.mult, op1=ALU.add), r=["mods", f"xp{tt * TT}"], w=["hm"])

            def load_w(i, b):
                S.dma("pool", lambda: nc.gpsimd.dma_start(out=wbuf[b][:], in_=win0[i]), writes=[f"mw{b}"])

            def proj_fm(pi, n, b, col0, t0):
                for c in range(DC):
                    T(lambda c=c: nc.tensor.matmul(pbank[pi][:, 0:n], wbuf[b][:, c, col0:col0 + 128], hm[:, c, t0:t0 + n],
                                                   start=(c == 0), stop=(c == DC - 1)), r=[f"mw{b}", "hm"], w=[f"pb{pi}"])

            def proj_tm(pi, b, col0, ncols, t0):
                for c in range(DC):
                    T(lambda c=c: nc.tensor.matmul(pbank[pi][:, 0:ncols], hm[:, c, t0:t0 + 128], wbuf[b][:, c, col0:col0 + ncols],
                                                   start=(c == 0), stop=(c == DC - 1)), r=[f"mw{b}", "hm"], w=[f"pb{pi}"])

            load_w(2, 0)
            with contextlib.ExitStack() as st2:
                F_ = P.sb("a_F", [128, TP], stack=st2)
                AA = P.sb("a_AA", [128, TP], stack=st2)
                vtm = P.sb("a_v", [128, 16, 128], stack=st2)
                ktm = [P.sb(f"a_k{i}", [128, 128], stack=st2) for i in range(2)]
                for hh in range(4):
                    b = hh % 2
                    if hh + 1 < 4:
                        load_w(3 + hh, (hh + 1) % 2)
                    for tt in range(4):
                        proj_fm(tt % 2, TT, b, 128, tt * TT)
                        A(lambda tt=tt: nc.scalar.activation(out=F_[:, tt * TT:(tt + 1) * TT], in_=pbank[tt % 2][:, :], func=AF.Sigmoid),
                          r=[f"pb{tt % 2}"], w=[f"aF{tt}"])
                    for blk in range(16):
                        proj_tm(2 + blk % 2, b, 384, 128, blk * 128)
                        V(lambda blk=blk: nc.vector.tensor_copy(out=vtm[:, blk, :], in_=pbank[2 + blk % 2][:, 0:128]),
                          r=[f"pb{2 + blk % 2}"], w=[f"av{blk}"])
                    Fk = [f"aF{tt}" for tt in range(4)]
                    V(lambda: nc.vector.tensor_scalar(out=F_[:], in0=F_[:], scalar1=oml[:, hh:hh + 1], scalar2=lbv[:, hh:hh + 1],
                                                      op0=ALU.mult, op1=ALU.add), r=Fk + ["oml", "lbv"], w=Fk)
                    A(lambda: nc.scalar.activation(out=AA[:], in_=F_[:], func=AF.Ln), r=Fk, w=["aAA"])
                    V(lambda: nc.vector.tensor_scalar(out=F_[:], in0=F_[:], scalar1=-1.0, scalar2=1.0, op0=ALU.mult, op1=ALU.add), r=Fk, w=Fk)
                    V(lambda: nc.vector.tensor_tensor_scan(out=AA[:, 0:1024], data0=onesf, data1=AA[:, 0:1024], initial=0.0,
                                                           op0=ALU.mult, op1=ALU.add), r=["aAA", "cst"], w=["aAA"])
                    V(lambda: nc.vector.tensor_tensor_scan(out=AA[:, 1024:2048], data0=onesf, data1=AA[:, 1024:2048],
                                                           initial=AA[:, 1023:1024], op0=ALU.mult, op1=ALU.add), r=["aAA", "cst"], w=["aAA"])
                    A(lambda: nc.scalar.activation(out=xs_t[:, 768 + hh:769 + hh], in_=AA[:, TP - 1:TP], func=AF.Exp), r=["aAA"], w=["xs_t"])
                    V(lambda: nc.vector.tensor_scalar(out=AA[:, 0:TP - 1], in0=AA[:, 0:TP - 1], scalar1=AA[:, TP - 1:TP], scalar2=-1.0,
                                                      op0=ALU.subtract, op1=ALU.mult), r=["aAA"], w=["aAA"])
                    V(lambda: nc.vector.memset(AA[:, TP - 1:TP], 0.0), r=["aAA"], w=["aAA"])
                    A(lambda: nc.scalar.activation(out=AA[:], in_=AA[:], func=AF.Exp), r=["aAA"], w=["aAA"])
                    V(lambda: nc.vector.tensor_tensor(out=AA[:], in0=AA[:], in1=F_[:], op=ALU.mult), r=["aAA"] + Fk, w=["aAA"])
                    for blk in range(16):
                        pi = 4 + blk % 2
                        T(lambda blk=blk, pi=pi: nc.tensor.transpose(pbank[pi][:, 0:128], AA[:, blk * 128:(blk + 1) * 128], ident),
                          r=["aAA", "cst"], w=[f"pb{pi}"])
                        V(lambda blk=blk, pi=pi: nc.vector.tensor_copy(out=ktm[blk % 2][:], in_=pbank[pi][:, 0:128]),
                          r=[f"pb{pi}"], w=[f"ak{blk % 2}"])
                        T(lambda blk=blk: nc.tensor.matmul(pbank[6][:, 0:128], ktm[blk % 2][:], vtm[:, blk, :],
                                                           start=(blk == 0), stop=(blk == 15)),
                          r=[f"ak{blk % 2}", f"av{blk}"], w=["pb6"])
                    V(lambda: nc.vector.tensor_copy(out=xs_t[:, 256 + hh * 128:256 + (hh + 1) * 128], in_=pbank[6][:, 0:128]),
                      r=["pb6"], w=["xs_t"])
                S.barrier()

            st_swa = contextlib.ExitStack()
            qT = P.sb("m_qT", [128, 4, TP], BF16, stack=st_swa)
            kT = P.sb("m_kT", [128, 17 * 128], BF16, stack=st_swa)
            vT = P.sb("m_vT", [128, 17, 128], BF16, stack=st_swa)
            load_w(0, 0)
            load_w(1, 1)
            for g in range(4):
                for tt in range(4):
                    pi = (g * 4 + tt) % 2
                    proj_fm(pi, TT, 0, g * 128, tt * TT)
                    A(lambda g=g, tt=tt, pi=pi: nc.scalar.mul(out=qT[:, g, tt * TT:(tt + 1) * TT], in_=pbank[pi][:, :], mul=0.125),
                      r=[f"pb{pi}"], w=["qT"])
            for tt in range(4):
                pi = 2 + tt % 2
                proj_fm(pi, TT, 1, 0, tt * TT)
                A(lambda tt=tt, pi=pi: nc.scalar.copy(out=kT[:, 128 + tt * TT:128 + (tt + 1) * TT], in_=pbank[pi][:, :]),
                  r=[f"pb{pi}"], w=["kT"])
            for blk in range(16):
                pi = 4 + blk % 2
                proj_tm(pi, 1, 128, 128, blk * 128)
                V(lambda blk=blk, pi=pi: nc.vector.tensor_copy(out=vT[:, 1 + blk, :], in_=pbank[pi][:, 0:128]), r=[f"pb{pi}"], w=["vT"])
                if blk == 15:
                    V(lambda pi=pi: nc.vector.tensor_copy(out=xs_t[:, 128:256], in_=pbank[pi][:, 0:128]), r=[f"pb{pi}"], w=["xs_t"])
            proj_fm(6, 128, 1, 0, TP - 128)
            V(lambda: nc.vector.tensor_copy(out=xs_t[:, 0:128], in_=pbank[6][:, 0:128]), r=["pb6"], w=["xs_t"])
            proj_tm(7, 1, 0, 128, TP - 128)
            with contextlib.ExitStack() as st2:
                ko = P.sb("x_ko", [128, 128], stack=st2)
                V(lambda: nc.vector.tensor_copy(out=ko[:], in_=pbank[7][:, 0:128]), r=["pb7"], w=["ko"])
                S.dma("sp", lambda: nc.sync.dma_start(out=o_swak[:, :], in_=ko[:]), reads=["ko"])
                S.dma("sp", lambda: nc.sync.dma_start(out=o_swav[:, :], in_=xs_t[:, 128:256]), reads=["xs_t"])
                S.dma("pool", lambda: nc.gpsimd.dma_start(out=xsrc[:, :], in_=xs_t[:]), reads=["xs_t"], writes=["xsrc"])
                S.collective(lambda: nc.gpsimd.collective_compute(
                    "AllGather", ALU.bypass, replica_groups=([[0, 1, 2, 3]] if "half" in dbg else [[0, 1, 2, 3], [4, 5, 6, 7]]),
                    ins=[xsrc.ap().opt()], outs=[xdst.ap().opt()]), reads=["xsrc"], writes=["xdst"])
                xg_t = P.sb("x_xg", [128, 4, XW], stack=st2)
                S.dma("sp", lambda: nc.sync.dma_start(out=xg_t[:], in_=xdst[:, :].rearrange("(r p) n -> p r n", p=128)),
                      reads=["xdst"], writes=["xg"])
                acc = P.sb("x_acc", [128, 256], stack=st2)
                V(lambda: nc.vector.tensor_scalar(out=acc[:], in0=xg_t[:, 0, 0:256], scalar1=pc_t[:, 0:1], scalar2=None, op0=ALU.mult),
                  r=["xg", "pc"], w=["xacc"])
                for r_ in range(1, 4):
                    V(lambda r_=r_: nc.vector.scalar_tensor_tensor(out=acc[:], in0=xg_t[:, r_, 0:256], scalar=pc_t[:, r_:r_ + 1],
                                                                   in1=acc[:], op0=ALU.mult, op1=ALU.add), r=["xg", "pc", "xacc"], w=["xacc"])
                V(lambda: nc.vector.tensor_copy(out=kT[:, 0:128], in_=acc[:, 0:128]), r=["xacc"], w=["kT"])
                V(lambda: nc.vector.tensor_copy(out=vT[:, 0, :], in_=acc[:, 128:256]), r=["xacc"], w=["vT"])
                V(lambda: nc.vector.memset(S0[:], 0.0), w=["S0"])
                coef = P.sb("x_coef", [128, 4], stack=st2)
                tmpS = P.sb("x_tmpS", [128, 128], stack=st2)
                for r_ in range(4):
                    m_r = pc_t[:, 4 + r_:5 + r_]
                    V(lambda r_=r_: nc.vector.tensor_scalar(out=coef[:], in0=xg_t[:, r_, 768:772], scalar1=-1.0, scalar2=m_r,
                                                            op0=ALU.add, op1=ALU.mult), r=["xg", "pc"], w=["xcoef"])
                    V(lambda: nc.vector.tensor_scalar(out=coef[:], in0=coef[:], scalar1=1.0, scalar2=None, op0=ALU.add), r=["xcoef"], w=["xcoef"])
                    for hh in range(4):
                        V(lambda r_=r_, hh=hh: nc.vector.tensor_scalar(out=tmpS[:], in0=xg_t[:, r_, 256 + hh * 128:256 + (hh + 1) * 128],
                                                                       scalar1=m_r, scalar2=None, op0=ALU.mult), r=["xg", "pc"], w=["xtmpS"])
                        V(lambda hh=hh: nc.vector.scalar_tensor_tensor(out=S0[:, hh, :], in0=S0[:, hh, :], scalar=coef[:, hh:hh + 1],
                                                                       in1=tmpS[:], op0=ALU.mult, op1=ALU.add), r=["xtmpS", "xcoef", "S0"], w=["S0"])
                S.barrier()

            with contextlib.ExitStack() as st2:
                bias_t = P.sb("s_bias", [128, 2, 256], stack=st2)
                S.dma("sp", lambda: nc.sync.dma_start(out=bias_t[:], in_=swab[:, :, :]), writes=["sbias"])
                s_sb = [P.sb(f"s_s{i}", [128, 4, 256], stack=st2) for i in range(1)] * 2
                pT = [P.sb(f"s_pT{i}", [128, 8, 128], BF16, stack=st2) for i in range(1)] * 2
                sm = [P.sb(f"s_m{i}", [128, 16], stack=st2) for i in range(2)]
                it = 0
                for blk in range(16):
                    for kvh in range(2):
                        u = it % 2
                        it += 1
                        ss, pp, mm = s_sb[u], pT[u], sm[u]
                        sk, pk, mk = "ss0", "spT0", f"sm{u}"
                        h0 = kvh * 4
                        for g in range(4):
                            pi = g // 2
                            T(lambda g=g, pi=pi: nc.tensor.matmul(
                                pbank[pi][:, (g % 2) * 256:(g % 2 + 1) * 256],
                                qT[64 * kvh:64 * kvh + 64, g, blk * 128:(blk + 1) * 128],
                                kT[64 * kvh:64 * kvh + 64, blk * 128:(blk + 2) * 128], start=True, stop=True),
                              r=["qT", "kT"], w=[f"pb{pi}"])
                        for g in range(4):
                            V(lambda g=g: nc.vector.scalar_tensor_tensor(
                                out=ss[:, g, :], in0=bias_t[:, 1 if blk == 0 else 0, :], scalar=-(2.0 ** (-(h0 + g + 1))),
                                in1=pbank[g // 2][:, (g % 2) * 256:(g % 2 + 1) * 256], op0=ALU.mult, op1=ALU.add),
                              r=[f"pb{g // 2}", "sbias"], w=[sk])
                        V(lambda: nc.vector.tensor_reduce(out=mm[:, 0:4], in_=ss[:], axis=AX.X, op=ALU.max), r=[sk], w=[mk])
                        V(lambda: nc.vector.tensor_tensor(out=mm[:, 0:4], in0=mm[:, 0:4], in1=sink_t[:, h0:h0 + 4], op=ALU.max), r=[mk, "sink"], w=[mk])
                        V(lambda: nc.vector.tensor_tensor(out=mm[:, 8:12], in0=sink_t[:, h0:h0 + 4], in1=mm[:, 0:4], op=ALU.subtract), r=[mk, "sink"], w=[mk])
                        V(lambda: nc.vector.tensor_scalar(out=mm[:, 0:4], in0=mm[:, 0:4], scalar1=-1.0, scalar2=None, op0=ALU.mult), r=[mk], w=[mk])
                        for g in range(4):
                            A(lambda g=g: nc.scalar.activation(out=ss[:, g, :], in_=ss[:, g, :], func=AF.Exp, bias=mm[:, g:g + 1],
                                                               accum_out=mm[:, 4 + g:5 + g]), r=[sk, mk], w=[sk, mk])
                        A(lambda: nc.scalar.activation(out=mm[:, 8:12], in_=mm[:, 8:12], func=AF.Exp), r=[mk], w=[mk])
                        V(lambda: nc.vector.tensor_tensor(out=mm[:, 8:12], in0=mm[:, 8:12], in1=mm[:, 4:8], op=ALU.add), r=[mk], w=[mk])
                        V(lambda: nc.vector.reciprocal(out=mm[:, 8:12], in_=mm[:, 8:12]), r=[mk], w=[mk])
                        V(lambda: nc.vector.tensor_tensor(out=ss[:], in0=ss[:], in1=mm[:, 8:12].unsqueeze(2).to_broadcast([128, 4, 256]), op=ALU.mult),
                          r=[sk, mk], w=[sk])
                        for g in range(4):
                            for half in range(2):
                                pi = 2 + g // 2
                                j = (g % 2) * 2 + half
                                T(lambda g=g, half=half, pi=pi, j=j: nc.tensor.transpose(
                                    pbank[pi][:, j * 128:(j + 1) * 128], ss[:, g, half * 128:(half + 1) * 128], ident),
                                  r=[sk, "cst"], w=[f"pb{pi}"])
                        for pi in range(2):
                            A(lambda pi=pi: nc.scalar.copy(out=pp[:, 4 * pi:4 * pi + 4, :], in_=pbank[2 + pi][:, :].rearrange("p (a k) -> p a k", k=128)),
                              r=[f"pb{2 + pi}"], w=[pk])
                        po = 4 + kvh
                        for g in range(4):
                            for half in range(2):
                                T(lambda g=g, half=half: nc.tensor.matmul(
                                    pbank[po][64 * (g % 2):64 * (g % 2) + 64, (g // 2) * 128:(g // 2 + 1) * 128],
                                    vT[:, blk + half, 64 * kvh:64 * kvh + 64], pp[:, g * 2 + half, :],
                                    start=(half == 0), stop=(half == 1)), r=["vT", pk], w=[f"pb{po}"])
                        A(lambda: nc.scalar.copy(out=o_a[:, 2 * kvh:2 * kvh + 2, blk * 128:(blk + 1) * 128],
                                                 in_=pbank[po][:, 0:256].rearrange("p (a k) -> p a k", k=128)), r=[f"pb{po}"], w=["o_a"])
                S.barrier()

            st_swa.close()
            NTK = 512
            with contextlib.ExitStack() as st2:
                QS = P.sb("b_QS", [128, NTK], stack=st2)
                F_ = P.sb("b_F", [128, NTK], stack=st2)
                AC = P.sb("b_AC", [128, NTK], stack=st2)
                EA = P.sb("b_EA", [128, NTK], stack=st2)
                QD = P.sb("b_QD", [128, NTK], stack=st2)
                KD = P.sb("b_KD", [128, NTK], stack=st2)
                KC = P.sb("b_KC", [128, NTK], stack=st2)
                GS = P.sb("b_GS", [128, NTK], BF16, stack=st2)
                OO = P.sb("b_OO", [128, NTK], stack=st2)
                SQ = P.sb("b_SQ", [128, NTK], BF16, stack=st2)
                DCH = P.sb("b_DCH", [128, NTK // 16], stack=st2)
                vtm = P.sb("b_v", [128, 4, 128], stack=st2)
                kdm = P.sb("b_kdm", [128, 8, 128], stack=st2)
                scm = P.sb("b_scm", [128, 128], stack=st2)
                SR = P.sb("b_SR", [128, 16, 128], stack=st2)
                load_w(2, 0)
                gc = 0
                for hh in range(4):
                    b = hh % 2
                    if hh + 1 < 4:
                        load_w(3 + hh, (hh + 1) % 2)
                    V(lambda hh=hh: nc.vector.tensor_copy(out=SR[:, gc % 16, :], in_=S0[:, hh, :]), r=["S0"], w=[f"SR{gc % 16}"])
                    for qt in range(TP // NTK):
                        t0 = qt * NTK
                        proj_fm(0, NTK, b, 0, t0)
                        A(lambda: nc.scalar.activation(out=QS[:], in_=pbank[0][:, :], func=AF.Silu), r=["pb0"], w=["QS"])
                        proj_fm(1, NTK, b, 128, t0)
                        A(lambda: nc.scalar.activation(out=F_[:], in_=pbank[1][:, :], func=AF.Sigmoid), r=["pb1"], w=["F"])
                        proj_fm(0, NTK, b, 256, t0)
                        A(lambda: nc.scalar.activation(out=GS[:], in_=pbank[0][:, :], func=AF.Silu), r=["pb0"], w=["GS"])
                        for blk in range(4):
                            proj_tm(2 + blk % 2, b, 384, 128, t0 + blk * 128)
                            V(lambda blk=blk: nc.vector.tensor_copy(out=vtm[:, blk, :], in_=pbank[2 + blk % 2][:, 0:128]),
                              r=[f"pb{2 + blk % 2}"], w=[f"bv{blk}"])
                        V(lambda: nc.vector.tensor_scalar(out=F_[:], in0=F_[:], scalar1=oml[:, hh:hh + 1], scalar2=lbv[:, hh:hh + 1],
                                                          op0=ALU.mult, op1=ALU.add), r=["F", "oml", "lbv"], w=["F"])
                        A(lambda: nc.scalar.activation(out=AC[:], in_=F_[:], func=AF.Ln), r=["F"], w=["AC"])
                        V(lambda: nc.vector.tensor_scalar(out=F_[:], in0=F_[:], scalar1=-1.0, scalar2=1.0, op0=ALU.mult, op1=ALU.add), r=["F"], w=["F"])
                        V(lambda: nc.vector.tensor_tensor_scan(out=AC[:], data0=rmask[:, 0:NTK], data1=AC[:], initial=0.0,
                                                               op0=ALU.mult, op1=ALU.add), r=["AC", "cst"], w=["AC"])
                        A(lambda: nc.scalar.activation(out=EA[:], in_=AC[:], func=AF.Exp), r=["AC"], w=["EA"])
                        V(lambda: nc.vector.tensor_tensor(out=QD[:], in0=QS[:], in1=EA[:], op=ALU.mult), r=["QS", "EA"], w=["QD"])
                        A(lambda: nc.scalar.activation(out=EA[:], in_=AC[:], func=AF.Exp, scale=-1.0), r=["AC", "QD"], w=["EA"])
                        V(lambda: nc.vector.tensor_tensor(out=KD[:], in0=F_[:], in1=EA[:], op=ALU.mult), r=["F", "EA"], w=["KD"])
                        ACv = AC[:].rearrange("p (c j) -> p c j", j=16)
                        V(lambda: nc.vector.tensor_tensor(out=KC[:].rearrange("p (c j) -> p c j", j=16),
                                                          in0=ACv[:, :, 15:16].to_broadcast([128, NTK // 16, 16]), in1=ACv, op=ALU.subtract),
                          r=["AC"], w=["KC"])
                        A(lambda: nc.scalar.activation(out=KC[:], in_=KC[:], func=AF.Exp), r=["KC"], w=["KC"])
                        V(lambda: nc.vector.tensor_tensor(out=KC[:], in0=KC[:], in1=F_[:], op=ALU.mult), r=["KC", "F"], w=["KC"])
                        A(lambda: nc.scalar.activation(out=DCH[:].unsqueeze(2), in_=ACv[:, :, 15:16], func=AF.Exp), r=["AC"], w=["DCH"])
                        for blk in range(4):
                            cs_ = slice(blk * 128, (blk + 1) * 128)
                            T(lambda: nc.tensor.transpose(pbank[4][:, 0:128], KC[:, cs_], ident), r=["KC", "cst"], w=["pb4"])
                            for c in range(8):
                                A(lambda c=c: nc.scalar.activation(out=kdm[:, c, :], in_=pbank[4][:, 0:128], func=AF.Copy, scale=chunkmask[:, c:c + 1]),
                                  r=["pb4", "cst"], w=[f"kdm{c}"])
                            T(lambda: nc.tensor.matmul(pbank[5][:, 0:128], KD[:, cs_], QD[:, cs_], start=True, stop=True), r=["KD", "QD"], w=["pb5"])
                            V(lambda: nc.vector.tensor_tensor(out=scm[:], in0=pbank[5][:, 0:128], in1=mask16T, op=ALU.mult), r=["pb5", "cst"], w=["scm"])
                            for c in range(8):
                                T(lambda c=c: nc.tensor.matmul(pbank[6 + c // 4][:, (c % 4) * 128:(c % 4 + 1) * 128], kdm[:, c, :], vtm[:, blk, :],
                                                               start=True, stop=True), r=[f"kdm{c}", f"bv{blk}"], w=[f"pb{6 + c // 4}"])
                            gcs = []
                            for c in range(8):
                                cur, nxt = gc % 16, (gc + 1) % 16
                                gcs.append(cur)
                                V(lambda c=c, cur=cur, nxt=nxt: nc.vector.scalar_tensor_tensor(
                                    out=SR[:, nxt, :], in0=SR[:, cur, :], scalar=DCH[:, blk * 8 + c:blk * 8 + c + 1],
                                    in1=pbank[6 + c // 4][:, (c % 4) * 128:(c % 4 + 1) * 128], op0=ALU.mult, op1=ALU.add),
                                  r=[f"SR{cur}", "DCH", f"pb{6 + c // 4}"], w=[f"SR{nxt}"])
                                gc += 1
                            po = 2 + blk % 2
                            T(lambda: nc.tensor.matmul(pbank[po][:, 0:128], vtm[:, blk, :], scm[:], start=True, stop=False),
                              r=[f"bv{blk}", "scm"], w=[f"pb{po}"])
                            for c in range(8):
                                T(lambda c=c: nc.tensor.matmul(pbank[po][:, 16 * c:16 * c + 16], SR[:, gcs[c], :],
                                                               QD[:, blk * 128 + 16 * c:blk * 128 + 16 * c + 16], start=False, stop=(c == 7)),
                                  r=[f"SR{gcs[c]}", "QD"], w=[f"pb{po}"])
                            A(lambda: nc.scalar.copy(out=OO[:, cs_], in_=pbank[po][:, 0:128]), r=[f"pb{po}"], w=["OO"])
                        A(lambda: nc.scalar.activation(out=SQ[:], in_=OO[:], func=AF.Square), r=["OO"], w=["SQ"])
                        T(lambda: nc.tensor.matmul(pbank[5][:, 0:NTK], ones128[:], SQ[:], start=True, stop=True), r=["SQ", "ones128"], w=["pb5"])
                        V(lambda: nc.vector.tensor_scalar(out=EA[:], in0=pbank[5][:, 0:NTK], scalar1=1e-6, scalar2=None, op0=ALU.add), r=["pb5"], w=["EA"])
                        A(lambda: nc.scalar.activation(out=EA[:], in_=EA[:], func=AF.Ln), r=["EA"], w=["EA"])
                        A(lambda: nc.scalar.activation(out=EA[:], in_=EA[:], func=AF.Exp, scale=-0.5), r=["EA"], w=["EA"])
                        V(lambda: nc.vector.tensor_tensor(out=OO[:], in0=OO[:], in1=EA[:], op=ALU.mult), r=["OO", "EA"], w=["OO"])
                        V(lambda: nc.vector.scalar_tensor_tensor(out=o_b[:, hh, t0:t0 + NTK], in0=OO[:], scalar=hg_t[:, 0:1], in1=GS[:],
                                                                 op0=ALU.mult, op1=ALU.mult), r=["OO", "GS", "hg"], w=["o_b"])
                    S.dma("sp", lambda hh=hh: nc.sync.dma_start(out=o_hgrn[:, hh, :], in_=SR[:, gc % 16, :]), reads=[f"SR{gc % 16}"])
                S.barrier()

            with contextlib.ExitStack() as st2:
                hms = P.sb("q_hms", [128, DC, NS], BF16, stack=st2)
                t16 = P.sb("q_t16", [128, NS], stack=st2)
                for c in range(DC):
                    V(lambda c=c: nc.vector.tensor_tensor(out=t16[:], in0=xs[:, c, :], in1=mods[:, l, jsc * 8 + c, 1:1 + NS], op=ALU.mult),
                      r=["xs0", "mods"], w=["t16"])
                    V(lambda c=c: nc.vector.tensor_tensor(out=hms[:, c, :], in0=t16[:], in1=mods[:, l, jsh * 8 + c, 1:1 + NS], op=ALU.add),
                      r=["t16", "mods"], w=["hms"])
                qtm = P.sb("q_qtm", [NS, 512], stack=st2)
                kvn = P.sb("q_kvn", [NS, 256], stack=st2)
                hq = P.sb("q_hq", [128, 4, 3, NS], stack=st2)
                hi = P.sb("q_hi", [NS, 4, 128], stack=st2)

                def sproj_tm(pi, b, col0, ncols):
                    for c in range(DC):
                        T(lambda c=c: nc.tensor.matmul(pbank[pi][0:NS, 0:ncols], hms[:, c, :], wbuf[b][:, c, col0:col0 + ncols],
                                                       start=(c == 0), stop=(c == DC - 1)), r=[f"mw{b}", "hms"], w=[f"pb{pi}"])

                def sproj_fm(pi, pc0, b, col0):
                    for c in range(DC):
                        T(lambda c=c: nc.tensor.matmul(pbank[pi][:, pc0:pc0 + NS], wbuf[b][:, c, col0:col0 + 128], hms[:, c, :],
                                                       start=(c == 0), stop=(c == DC - 1)), r=[f"mw{b}", "hms"], w=[f"pb{pi}"])

                load_w(0, 0)
                load_w(1, 1)
                sproj_tm(0, 0, 0, 512)
                A(lambda: nc.scalar.mul(out=qtm[:], in_=pbank[0][0:NS, 0:512], mul=0.125), r=["pb0"], w=["qtm"])
                sproj_tm(1, 1, 0, 256)
                V(lambda: nc.vector.tensor_copy(out=kvn[:], in_=pbank[1][0:NS, 0:256]), r=["pb1"], w=["kvn"])
                load_w(2, 0)
                for hh in range(4):
                    b = hh % 2
                    if hh + 1 < 4:
                        load_w(3 + hh, (hh + 1) % 2)
                    for j3 in range(3):
                        sproj_fm(2, j3 * NS, b, j3 * 128)
                    A(lambda hh=hh: nc.scalar.activation(out=hq[:, hh, 0, :], in_=pbank[2][:, 0:NS], func=AF.Silu), r=["pb2"], w=["hq"])
                    A(lambda hh=hh: nc.scalar.activation(out=hq[:, hh, 1, :], in_=pbank[2][:, NS:2 * NS], func=AF.Sigmoid), r=["pb2"], w=["hq"])
                    A(lambda hh=hh: nc.scalar.activation(out=hq[:, hh, 2, :], in_=pbank[2][:, 2 * NS:3 * NS], func=AF.Silu), r=["pb2"], w=["hq"])
                    sproj_tm(3, b, 384, 128)
                    V(lambda hh=hh: nc.vector.tensor_copy(out=hi[:, hh, :], in_=pbank[3][0:NS, 0:128]), r=["pb3"], w=["hi"])
                S.dma("sp", lambda: nc.sync.dma_start(out=o_sk[:, 0:127, :], in_=ck_in[:, 1:128, :]))
                S.dma("sp", lambda: nc.sync.dma_start(out=o_sv[:, 0:127, :], in_=cv_in[:, 1:128, :]))
                S.dma("sp", lambda: nc.sync.dma_start(out=o_sk[:, 127, :], in_=kvn[:, 0:128]), reads=["kvn"])
                S.dma("sp", lambda: nc.sync.dma_start(out=o_sv[:, 127, :], in_=kvn[:, 128:256]), reads=["kvn"])

                with contextlib.ExitStack() as st3:
                    ck = P.sb("d_ck", [128, 16, 128], stack=st3)
                    cv = P.sb("d_cv", [128, 16, 128], stack=st3)
                    dd = P.sb("d_dd", [128, 64], stack=st3)
                    S.dma("sp", lambda: nc.sync.dma_start(out=ck[:], in_=ck_in.rearrange("s (kb j) f -> (s kb) j f", j=16)), writes=["ck"])
                    S.dma("sp", lambda: nc.sync.dma_start(out=cv[:], in_=cv_in.rearrange("s (kb j) f -> (s kb) j f", j=16)), writes=["cv"])
                    S.dma("sp", lambda: nc.sync.dma_start(out=dd[:], in_=dec_tab[:, :]), writes=["dd"])
                    rep = P.sb("d_rep", [NS, 128], stack=st3)
                    S.dma("sp", lambda: nc.sync.dma_start(out=rep[:], in_=dec_rep[:, :]), writes=["rep"])
                    qbc = P.sb("d_qbc", [128, 512], stack=st3)
                    T(lambda: nc.tensor.matmul(pbank[0][:, 0:512], rep[:], qtm[:], start=True, stop=True), r=["rep", "qtm"], w=["pb0"])
                    V(lambda: nc.vector.tensor_copy(out=qbc[:], in_=pbank[0][:, 0:512]), r=["pb0"], w=["qbc"])
                    tmp = P.sb("d_tmp", [128, 16, 64], stack=st3)
                    sc = P.sb("d_sc", [128, 8, 16], stack=st3)
                    op_ = P.sb("d_op", [128, 8, 64], stack=st3)
                    sm = P.sb("d_sm", [128, 32], stack=st3)
                    for hd in range(8):
                        g, kvh = hd // 2, hd % 2
                        V(lambda hd=hd, kvh=kvh: nc.vector.tensor_tensor(
                            out=tmp[:], in0=ck[:, :, kvh * 64:(kvh + 1) * 64],
                            in1=qbc[:, hd * 64:(hd + 1) * 64].unsqueeze(1).to_broadcast([128, 16, 64]), op=ALU.mult), r=["ck", "qbc"], w=["dtmp"])
                        V(lambda hd=hd: nc.vector.tensor_reduce(out=sc[:, hd, :], in_=tmp[:], axis=AX.X, op=ALU.add), r=["dtmp"], w=["dsc"])
                        V(lambda hd=hd, g=g, kvh=kvh: nc.vector.scalar_tensor_tensor(
                            out=sc[:, hd, :], in0=dd[:, 0:16], scalar=-(2.0 ** (-(kvh * 4 + g + 1))), in1=sc[:, hd, :],
                            op0=ALU.mult, op1=ALU.add), r=["dsc", "dd"], w=["dsc"])
                    V(lambda: nc.vector.tensor_reduce(out=sm[:, 0:8], in_=sc[:], axis=AX.X, op=ALU.max), r=["dsc"], w=["dsm"])
                    V(lambda: nc.vector.tensor_tensor(out=sm[:, 0:8], in0=sm[:, 0:8], in1=dd[:, 32:40], op=ALU.max), r=["dsm", "dd"], w=["dsm"])
                    V(lambda: nc.vector.tensor_tensor(out=sc[:], in0=sc[:], in1=sm[:, 0:8].unsqueeze(2).to_broadcast([128, 8, 16]), op=ALU.subtract),
                      r=["dsc", "dsm"], w=["dsc"])
                    A(lambda: nc.scalar.activation(out=sc[:], in_=sc[:], func=AF.Exp), r=["dsc"], w=["dsc"])
                    V(lambda: nc.vector.tensor_reduce(out=sm[:, 8:16], in_=sc[:], axis=AX.X, op=ALU.add), r=["dsc"], w=["dsm"])
                    for hd in range(8):
                        kvh = hd % 2
                        V(lambda hd=hd, kvh=kvh: nc.vector.tensor_tensor(
                            out=tmp[:].rearrange("p j d -> p d j"), in0=cv[:, :, kvh * 64:(kvh + 1) * 64].rearrange("p j d -> p d j"),
                            in1=sc[:, hd, :].unsqueeze(1).to_broadcast([128, 64, 16]), op=ALU.mult), r=["cv", "dsc"], w=["dtmp"])
                        V(lambda hd=hd: nc.vector.tensor_reduce(out=op_[:, hd, :], in_=tmp[:].rearrange("p j d -> p d j"), axis=AX.X, op=ALU.add),
                          r=["dtmp"], w=["dop"])
                    sn = P.sb("d_sn", [NS, 64], stack=st3)
                    tn = P.sb("d_tn", [NS, 512], stack=st3)
                    V(lambda: nc.vector.tensor_tensor(
                        out=tn[:].rearrange("p (g k d) -> p g k d", g=4, k=2), in0=qtm[:].rearrange("p (g k d) -> p g k d", g=4, k=2),
                        in1=kvn[:, 0:128].rearrange("p (k d) -> p k d", k=2).unsqueeze(1).to_broadcast([NS, 4, 2, 64]), op=ALU.mult),
                      r=["qtm", "kvn"], w=["dtn"])
                    V(lambda: nc.vector.tensor_reduce(out=sn[:, 0:8], in_=tn[:].rearrange("p (h d) -> p h d", d=64), axis=AX.X, op=ALU.add),
                      r=["dtn"], w=["dsn"])
                    mT = P.sb("d_mT", [8, NS, 8], stack=st3)
                    m2 = P.sb("d_m2", [8, 64], stack=st3)
                    T(lambda: nc.tensor.transpose(pbank[1][0:8, 0:128], sm[:, 0:8], ident), r=["dsm", "cst"], w=["pb1"])
                    V(lambda: nc.vector.tensor_copy(out=mT[:].rearrange("p s k -> p (s k)"), in_=pbank[1][0:8, 0:128]), r=["pb1"], w=["dmT"])
                    T(lambda: nc.tensor.transpose(pbank[2][0:8, 0:NS], sn[:, 0:8], ident[0:NS, 0:NS]), r=["dsn", "cst"], w=["pb2"])
                    V(lambda: nc.vector.tensor_copy(out=m2[:, 16:32], in_=pbank[2][0:8, 0:NS]), r=["pb2"], w=["dm2"])
                    V(lambda: nc.vector.tensor_reduce(out=m2[:, 0:16], in_=mT[:], axis=AX.X, op=ALU.max), r=["dmT"], w=["dm2"])
                    V(lambda: nc.vector.tensor_tensor(out=m2[:, 0:16], in0=m2[:, 0:16], in1=m2[:, 16:32], op=ALU.max), r=["dm2"], w=["dm2"])
                    V(lambda: nc.vector.tensor_tensor(out=mT[:], in0=mT[:], in1=m2[:, 0:16].unsqueeze(2).to_broadcast([8, NS, 8]), op=ALU.subtract),
                      r=["dmT", "dm2"], w=["dmT"])
                    A(lambda: nc.scalar.activation(out=mT[:], in_=mT[:], func=AF.Exp), r=["dmT"], w=["dmT"])
                    V(lambda: nc.vector.tensor_tensor(out=m2[:, 32:48], in0=m2[:, 16:32], in1=m2[:, 0:16], op=ALU.subtract), r=["dm2"], w=["dm2"])
                    A(lambda: nc.scalar.activation(out=m2[:, 32:48], in_=m2[:, 32:48], func=AF.Exp), r=["dm2"], w=["dm2"])
                    A(lambda: nc.scalar.activation(out=m2[:, 48:64], in_=m2[:, 0:16], func=AF.Exp, scale=-1.0, bias=dd[0:8, 40:41]),
                      r=["dm2", "dd"], w=["dm2"])
                    T(lambda: nc.tensor.transpose(pbank[1][:, 0:8], mT[:].rearrange("p s k -> p (s k)"), ident[0:8, 0:8]), r=["dmT", "cst"], w=["pb1"])
                    V(lambda: nc.vector.tensor_copy(out=sm[:, 16:24], in_=pbank[1][:, 0:8]), r=["pb1"], w=["dsm"])
                    V(lambda: nc.vector.tensor_tensor(out=sm[:, 8:16], in0=sm[:, 8:16], in1=sm[:, 16:24], op=ALU.mult), r=["dsm"], w=["dsm"])
                    V(lambda: nc.vector.tensor_tensor(out=op_[:], in0=op_[:], in1=sm[:, 16:24].unsqueeze(2).to_broadcast([128, 8, 64]), op=ALU.mult),
                      r=["dop", "dsm"], w=["dop"])
                    T(lambda: nc.tensor.matmul(pbank[2][0:NS, 0:512], dd[:, 16:32], op_[:].rearrange("p h d -> p (h d)"), start=True, stop=True),
                      r=["dd", "dop"], w=["pb2"])
                    T(lambda: nc.tensor.matmul(pbank[3][0:NS, 0:8], dd[:, 16:32], sm[:, 8:16], start=True, stop=True), r=["dd", "dsm"], w=["pb3"])
                    T(lambda: nc.tensor.transpose(pbank[4][0:NS, 0:8], m2[:, 32:48], ident[0:8, 0:8]), r=["dm2", "cst"], w=["pb4"])
                    T(lambda: nc.tensor.transpose(pbank[4][0:NS, 8:16], m2[:, 48:64], ident[0:8, 0:8]), r=["dm2", "cst"], w=["pb4"])
                    V(lambda: nc.vector.tensor_copy(out=sn[:, 8:24], in_=pbank[4][0:NS, 0:16]), r=["pb4"], w=["dsn"])
                    V(lambda: nc.vector.tensor_tensor(out=sn[:, 24:32], in0=sn[:, 8:16], in1=sn[:, 16:24], op=ALU.add), r=["dsn"], w=["dsn"])
                    V(lambda: nc.vector.tensor_tensor(out=sn[:, 24:32], in0=sn[:, 24:32], in1=pbank[3][0:NS, 0:8], op=ALU.add), r=["dsn", "pb3"], w=["dsn"])
                    V(lambda: nc.vector.reciprocal(out=sn[:, 24:32], in_=sn[:, 24:32]), r=["dsn"], w=["dsn"])
                    V(lambda: nc.vector.tensor_tensor(
                        out=tn[:].rearrange("p (g k d) -> p g k d", g=4, k=2),
                        in0=kvn[:, 128:256].rearrange("p (k d) -> p k d", k=2).unsqueeze(1).to_broadcast([NS, 4, 2, 64]),
                        in1=sn[:, 8:16].rearrange("p (g k) -> p g k", k=2).unsqueeze(3).to_broadcast([NS, 4, 2, 64]), op=ALU.mult),
                      r=["kvn", "dsn"], w=["dtn"])
                    V(lambda: nc.vector.tensor_tensor(out=tn[:], in0=tn[:], in1=pbank[2][0:NS, 0:512], op=ALU.add), r=["dtn", "pb2"], w=["dtn"])
                    V(lambda: nc.vector.tensor_tensor(out=tn[:].rearrange("p (h d) -> p h d", d=64), in0=tn[:].rearrange("p (h d) -> p h d", d=64),
                                                      in1=sn[:, 24:32].unsqueeze(2).to_broadcast([NS, 8, 64]), op=ALU.mult), r=["dtn", "dsn"], w=["dtn"])
                    on = P.sb("d_on", [NS, 512], stack=st3)
                    for kvh in range(2):
                        V(lambda kvh=kvh: nc.vector.tensor_copy(
                            out=on[:, kvh * 256:(kvh + 1) * 256].rearrange("p (g d) -> p g d", g=4),
                            in_=tn[:].rearrange("p (g k d) -> p g k d", g=4, k=2)[:, :, kvh, :]), r=["dtn"], w=["don"])
                    for kc in range(4):
                        T(lambda kc=kc: nc.tensor.transpose(pbank[5][:, kc * NS:(kc + 1) * NS], on[:, kc * 128:(kc + 1) * 128], ident[0:NS, 0:NS]),
                          r=["don", "cst"], w=["pb5"])
                    V(lambda: nc.vector.tensor_copy(out=o_as[:], in_=pbank[5][:, 0:4 * NS].rearrange("p (k s) -> p k s", s=NS)), r=["pb5"], w=["o_as"])
                    S.barrier()

                with contextlib.ExitStack() as st3:
                    Sin = P.sb("g_Sin", [128, NS, 128], stack=st3)
                    Sout = P.sb("g_Sout", [128, NS, 128], stack=st3)
                    tSa = P.sb("g_tSa", [128, NS, 128], stack=st3)
                    ob = P.sb("g_ob", [128, 4, NS], stack=st3)
                    kk = P.sb("g_kk", [128, 4, NS], stack=st3)
                    for hh in range(4):
                        V(lambda hh=hh: nc.vector.tensor_scalar(out=hq[:, hh, 1, :], in0=hq[:, hh, 1, :], scalar1=oml[:, hh:hh + 1],
                                                                scalar2=lbv[:, hh:hh + 1], op0=ALU.mult, op1=ALU.add), r=["hq", "oml", "lbv"], w=["hq"])
                        V(lambda hh=hh: nc.vector.tensor_scalar(out=kk[:, hh, :], in0=hq[:, hh, 1, :], scalar1=-1.0, scalar2=1.0,
                                                                op0=ALU.mult, op1=ALU.add), r=["hq"], w=["gkk"])
                    for hh in range(4):
                        S.dma("sp", lambda hh=hh: nc.sync.dma_start(out=Sin[:], in_=sh_in[:, hh, :, :].rearrange("s k v -> k s v")), writes=["Sin"])
                        for s_ in range(NS):
                            pi = s_ // 4
                            T(lambda s_=s_, pi=pi: nc.tensor.matmul(pbank[pi][:, (s_ % 4) * 128:(s_ % 4 + 1) * 128], ident[0:NS, s_:s_ + 1].to_broadcast([NS, 128]),
                                                                    hi[:, hh, :], start=True, stop=True), r=["hi", "cst"], w=[f"pb{pi}"])
                        for s_ in range(NS):
                            pi = s_ // 4
                            V(lambda s_=s_, pi=pi: nc.vector.tensor_scalar(out=tSa[:, s_, :], in0=pbank[pi][:, (s_ % 4) * 128:(s_ % 4 + 1) * 128], scalar1=kk[:, hh, s_:s_ + 1],
                                                                           scalar2=None, op0=ALU.mult), r=[f"pb{pi}", "gkk"], w=[f"tS{s_}"])
                        for s_ in range(NS):
                            V(lambda s_=s_: nc.vector.scalar_tensor_tensor(out=Sout[:, s_, :], in0=Sin[:, s_, :], scalar=hq[:, hh, 1, s_:s_ + 1],
                                                                           in1=tSa[:, s_, :], op0=ALU.mult, op1=ALU.add),
                              r=["Sin", "hq", f"tS{s_}"], w=[f"Sout{s_}"])
                        for s_ in range(NS):
                            T(lambda s_=s_: nc.tensor.matmul(pbank[5][:, hh * NS + s_:hh * NS + s_ + 1], Sout[:, s_, :], hq[:, hh, 0, s_:s_ + 1],
                                                             start=True, stop=True), r=[f"Sout{s_}", "hq"], w=["pb5"])
                        S.dma("sp", lambda hh=hh: nc.sync.dma_start(out=o_sh[:, hh, :, :].rearrange("s k v -> k s v"), in_=Sout[:]),
                              reads=[f"Sout{s_}" for s_ in range(NS)])
                    V(lambda: nc.vector.tensor_copy(out=ob[:], in_=pbank[5][:, 0:4 * NS].rearrange("p (h s) -> p h s", s=NS)), r=["pb5"], w=["gob"])
                    sqs = P.sb("g_sq", [128, 4 * NS], BF16, stack=st3)
                    rs_ = P.sb("g_rs", [128, 4 * NS], stack=st3)
                    A(lambda: nc.scalar.activation(out=sqs[:], in_=ob[:].rearrange("p h s -> p (h s)"), func=AF.Square), r=["gob"], w=["gsq"])
                    T(lambda: nc.tensor.matmul(pbank[4][:, 0:4 * NS], ones128[:], sqs[:], start=True, stop=True), r=["gsq", "ones128"], w=["pb4"])
                    V(lambda: nc.vector.tensor_scalar(out=rs_[:], in0=pbank[4][:, 0:4 * NS], scalar1=1e-6, scalar2=None, op0=ALU.add), r=["pb4"], w=["grs"])
                    A(lambda: nc.scalar.activation(out=rs_[:], in_=rs_[:], func=AF.Ln), r=["grs"], w=["grs"])
                    A(lambda: nc.scalar.activation(out=rs_[:], in_=rs_[:], func=AF.Exp, scale=-0.5), r=["grs"], w=["grs"])
                    V(lambda: nc.vector.tensor_tensor(out=rs_[:], in0=rs_[:], in1=ob[:].rearrange("p h s -> p (h s)"), op=ALU.mult), r=["grs", "gob"], w=["grs"])
                    V(lambda: nc.vector.scalar_tensor_tensor(out=o_bs[:], in0=rs_[:].rearrange("p (h s) -> p h s", s=NS), scalar=hg_t[:, 0:1],
                                                             in1=hq[:, :, 2, :], op0=ALU.mult, op1=ALU.mult), r=["grs", "hq", "hg"], w=["o_bs"])
                    S.barrier()

            st_hm.close()
            with contextlib.ExitStack() as st2:
                wo = P.sb("o_wo", [128, 8, D], BF16, stack=st2)
                S.dma("pool", lambda: nc.gpsimd.dma_start(out=wo[:], in_=wout0[:, :, :]), writes=["wo"])
                xb = P.sb("o_xb", [128, DC, TT], BF16, stack=st2)
                sq = P.sb("o_sq", [128, DC, TT], BF16, stack=st2)
                r = [P.sb(f"o_r{i}", [128, TT], stack=st2) for i in range(2)]
                lt = [P.sb(f"o_lt{i}", [128, TT], stack=st2) for i in range(4)]
                for (kind, t0, n) in tiles:
                    xv = xview(kind, t0, n)
                    xk = f"x{kind}{t0}"
                    for co in range(DC):
                        pd = pbank[co % 2]
                        for kc in range(8):
                            if kind == "p":
                                src = o_a[:, kc, t0:t0 + n] if kc < 4 else o_b[:, kc - 4, t0:t0 + n]
                            else:
                                src = o_as[:, kc, :] if kc < 4 else o_bs[:, kc - 4, :]
                            T(lambda kc=kc, src=src: nc.tensor.matmul(pd[:, :n], wo[:, kc, co * 128:(co + 1) * 128], src,
                                                                      start=(kc == 0), stop=(kc == 7)), r=["wo", "o_a", "o_b", "o_as", "o_bs"], w=[f"pb{co % 2}"])
                        rr = r[co % 2]
                        if kind == "p":
                            A(lambda: nc.scalar.activation(out=rr[:, :n], in_=pd[:, :n], func=AF.Copy, scale=mods[:, l, jg * 8 + co, 0:1]),
                              r=[f"pb{co % 2}", "mods"], w=[f"r{co % 2}"])
                        else:
                            V(lambda: nc.vector.tensor_tensor(out=rr[:, :n], in0=pd[:, :n], in1=mods[:, l, jg * 8 + co, 1:1 + NS], op=ALU.mult),
                              r=[f"pb{co % 2}", "mods"], w=[f"r{co % 2}"])
                        V(lambda: nc.vector.scalar_tensor_tensor(out=xv[:, co, :], in0=xv[:, co, :], scalar=DN_ALPHA, in1=rr[:, :n],
                                                                 op0=ALU.mult, op1=ALU.add), r=[f"r{co % 2}", xk, f"{xk}c{co}"], w=[f"{xk}c{co}"])
                        A(lambda: nc.scalar.copy(out=xb[:, co, :n], in_=xv[:, co, :]), r=[f"{xk}c{co}"], w=[f"xb{co}"])
                        A(lambda: nc.scalar.activation(out=sq[:, co, :n], in_=xv[:, co, :], func=AF.Square), r=[f"{xk}c{co}"], w=[f"sq{co}"])
                    layer_norm_tile(xk, xv, n, xb, sq, lt, (l * 3 + 1) * 8)
                S.barrier()

    win1 = P.din("win1_r", [9, 128, DC, 512])
    wout1 = P.din("wout1_r", [128, 8, D])
    convw = P.din("convw_r", [128, 3, 8, 4])
    abc_in = P.din("abc", [8, 2])
    gng = P.din("gdn_g", [128, 1])
    tri_in = P.din("tri", [128, 1280])
    xsrc1 = nc.dram_tensor("xch_src1", [128, 32], F32)
    xdst1 = nc.dram_tensor("xch_dst1", [4 * 128, 32], F32)
    gscr = nc.dram_tensor("gdn_scr", [32, 128, 2080], F32)
    xsrc2 = nc.dram_tensor("xch_src2", [128, 8 * 256], F32)
    xdst2 = nc.dram_tensor("xch_dst2", [4 * 128, 8 * 256], F32)
    o_gdn = P.dout("o_gdn", [128, 8, 128])
    o_gconv = P.dout("o_gconv", [128, 3, 8, 3])
    sg_in = P.din("sg_in", [NS, 8, 128, 128])
    sc_in = P.din("sc_in", [NS, 3, 3072])
    o_sg = P.dout("o_sg", [NS, 8, 128, 128])
    o_sc = P.dout("o_sc", [NS, 3, 3072])

    def mixer_odd():
        l = 1
        jsh, jsc, jg = 3, 4, 5
        NTK = 512
        with contextlib.ExitStack() as st:
            pc_t = P.sb("n_pc", [128, 16], stack=st)
            cw = P.sb("n_cw", [128, 3, 8, 4], stack=st)
            abc = P.sb("n_abc", [8, 2], stack=st)
            gg_t = P.sb("n_gg", [128, 1], stack=st)
            tri = P.sb("n_tri", [128, 1280], stack=st)
            Sst = P.sb("n_S", [128, 8, 256], stack=st)
            carry = P.sb("n_carry", [128, 3, 8, 3], stack=st)
            carry0 = P.sb("n_carry0", [128, 3, 8, 3], stack=st)
            hmh = P.sb("n_hmh", [128, DC, 4], BF16, stack=st)
            hmp = P.sb("n_hmp", [128, DC, NTK], BF16, stack=st)
            hl4 = P.sb("n_hl4", [128, DC, 4], stack=st)
            wbuf = [P.sb(f"n_w{i}", [128, DC, 512], BF16, stack=st) for i in range(2)]
            wab = P.sb("n_wab", [128, DC, 16], BF16, stack=st)
            og = P.sb("n_og", [128, 8, NTK], BF16, stack=st)
            S.dma("sp", lambda: nc.sync.dma_start(out=pc_t[:], in_=pcore[:, :]), writes=["pc"])
            S.dma("sp", lambda: nc.sync.dma_start(out=cw[:], in_=convw[:, :, :, :]), writes=["cw"])
            S.dma("sp", lambda: nc.sync.dma_start(out=abc[:], in_=abc_in[:, :]), writes=["abc"])
            S.dma("sp", lambda: nc.sync.dma_start(out=gg_t[:], in_=gng[:, :]), writes=["gg"])
            S.dma("sp", lambda: nc.sync.dma_start(out=tri[:], in_=tri_in[:, :]), writes=["tri"])
            S.dma("pool", lambda: nc.gpsimd.dma_start(out=wab[:], in_=win1[8, :, :, 0:16]), writes=["wab"])
            Uincl = tri[:, 0:128]
            Ustrict = tri[:, 128:256]
            A(lambda: nc.scalar.activation(out=abc[:, 1:2], in_=abc[:, 1:2], func=AF.Exp), r=["abc"], w=["abc"])
            V(lambda: nc.vector.tensor_scalar(out=abc[:, 1:2], in0=abc[:, 1:2], scalar1=-1.0, scalar2=None, op0=ALU.mult), r=["abc"], w=["abc"])
            def make_hm(t0):
                for c in range(DC):
                    V(lambda c=c: nc.vector.tensor_scalar(
                        out=hmp[:, c, :], in0=xp[:, c, t0:t0 + NTK],
                        scalar1=mods[:, l, jsc * 8 + c, 0:1], scalar2=mods[:, l, jsh * 8 + c, 0:1],
                        op0=ALU.mult, op1=ALU.add), r=["mods", f"xp{t0}"], w=["hm"])

            for c in range(DC):
                V(lambda c=c: nc.vector.tensor_scalar(
                    out=hl4[:, c, :], in0=xp[:, c, TP - 4:TP],
                    scalar1=mods[:, l, jsc * 8 + c, 0:1], scalar2=mods[:, l, jsh * 8 + c, 0:1],
                    op0=ALU.mult, op1=ALU.add), r=["mods", f"xp{TP - TT}"], w=["hl4"])
            with contextlib.ExitStack() as st2:
                h4 = P.sb("e_h4", [128, 32], stack=st2)
                g4 = P.sb("e_g4", [128, 4, 32], stack=st2)
                V(lambda: nc.vector.tensor_copy(out=h4[:].rearrange("p (c t) -> p c t", t=4), in_=hl4[:]), r=["hl4"], w=["h4"])
                S.dma("pool", lambda: nc.gpsimd.dma_start(out=xsrc1[:, :], in_=h4[:]), reads=["h4"], writes=["xsrc1"])
                if "nocc" not in dbg:
                    S.collective(lambda: nc.gpsimd.collective_compute(
                        "AllGather", ALU.bypass, replica_groups=([[0, 1, 2, 3]] if "half" in dbg else [[0, 1, 2, 3], [4, 5, 6, 7]]),
                        ins=[xsrc1.ap().opt()], outs=[xdst1.ap().opt()]), reads=["xsrc1"], writes=["xdst1"])
                S.dma("sp", lambda: nc.sync.dma_start(out=g4[:], in_=xdst1[:, :].rearrange("(r p) n -> p r n", p=128)), reads=["xdst1"], writes=["g4"])
                V(lambda: nc.vector.tensor_scalar(out=h4[:], in0=g4[:, 0, :], scalar1=pc_t[:, 0:1], scalar2=None, op0=ALU.mult), r=["g4", "pc", "h4"], w=["h4"])
                for r_ in range(1, 4):
                    V(lambda r_=r_: nc.vector.scalar_tensor_tensor(out=h4[:], in0=g4[:, r_, :], scalar=pc_t[:, r_:r_ + 1], in1=h4[:],
                                                                   op0=ALU.mult, op1=ALU.add), r=["g4", "pc", "h4"], w=["h4"])
                V(lambda: nc.vector.tensor_copy(out=hmh[:], in_=h4[:].rearrange("p (c t) -> p c t", t=4)), r=["h4"], w=["hmh"])
                S.barrier()

            def load_w(i, b):
                S.dma("pool", lambda: nc.gpsimd.dma_start(out=wbuf[b][:], in_=win1[i]), writes=[f"nw{b}"])

            def proj_fm(pi, n, wt, col0, src):
                for c in range(DC):
                    T(lambda c=c: nc.tensor.matmul(pbank[pi][:, 0:n], wt[:, c, col0:col0 + 128], src[:, c, 0:n],
                                                   start=(c == 0), stop=(c == DC - 1)), r=["nw0", "nw1", "hm", "hmh"], w=[f"pb{pi}"])

            load_w(0, 0)
            for h in range(8):
                if h + 1 < 8:
                    load_w(h + 1, (h + 1) % 2)
                for j3 in range(3):
                    proj_fm(j3, 4, wbuf[h % 2], j3 * 128, hmh)
                    V(lambda j3=j3, h=h: nc.vector.tensor_copy(out=carry0[:, j3, h, :], in_=pbank[j3][:, 1:4]), r=[f"pb{j3}"], w=["carry0"])

            def gdn_pass(mode):
                aug = (mode == "A")
                VW = 256 if aug else 128
                V(lambda: nc.vector.tensor_copy(out=carry[:], in_=carry0[:]), r=["carry0"], w=["carry"])
                for piece in range(TP // NTK):
                    t0 = piece * NTK
                    make_hm(t0)
                    with contextlib.ExitStack() as st2:
                        GT = P.sb("p_GT", [8, NTK], stack=st2)
                        BT = P.sb("p_BT", [8, NTK], stack=st2)
                        Gc = P.sb("p_Gc", [128, 4, 8], stack=st2)
                        Bc = P.sb("p_Bc", [128, 4, 8], stack=st2)
                        PC = [P.sb(f"p_PC{j}", [128, 3 + NTK], stack=st2) for j in range(3)]
                        CV = [None] + [P.sb(f"p_CV{j}", [128, NTK], stack=st2) for j in (1, 2)]
                        SQb = P.sb("p_SQ", [128, NTK], BF16, stack=st2)
                        NQ = 1 if aug else 2
                        RNs = [P.sb(f"p_RN{i}", [128, NTK], stack=st2) for i in range(NQ)]
                        CVQ = [P.sb(f"p_CVq{i}", [128, NTK], stack=st2) for i in range(NQ)]
                        GSbs = [P.sb(f"p_GS{i}", [128, NTK], BF16, stack=st2) for i in range(NQ)]
                        OOs = [P.sb(f"p_OO{i}", [128, NTK], stack=st2) for i in range(NQ)]
                        PKs = [P.sb(f"p_PK{i}", [128, 2080], stack=st2) for i in range(2)]
                        DmT = P.sb("p_DmT", [128, 4, 128], stack=st2)
                        QKTs = [P.sb(f"p_QKT{i}", [128, 4, 128], stack=st2) for i in range(NQ)]
                        if aug:
                            RN0 = [P.sb(f"p_RN0{i}", [128, 4, 128], stack=st2) for i in range(2)]
                            WK = [P.sb(f"p_WK{i}", [128, 4, 128], stack=st2) for i in range(4)]
                            Xt = P.sb("p_Xt", [128, 4, 128], stack=st2)
                        U = P.sb("p_U", [128, 128], stack=st2)
                        WT = P.sb("p_WT", [128, 128], stack=st2)
                        DL = P.sb("p_DL", [128, 256], stack=st2)
                        for c in range(DC):
                            T(lambda c=c: nc.tensor.matmul(pbank[0][0:8, 0:NTK], wab[:, c, 0:8], hmp[:, c, :], start=(c == 0), stop=(c == DC - 1)),
                              r=["wab", "hm"], w=["pb0"])
                        for c in range(DC):
                            T(lambda c=c: nc.tensor.matmul(pbank[1][0:8, 0:NTK], wab[:, c, 8:16], hmp[:, c, :], start=(c == 0), stop=(c == DC - 1)),
                              r=["wab", "hm"], w=["pb1"])
                        A(lambda: nc.scalar.activation(out=GT[:], in_=pbank[0][0:8, 0:NTK], func=AF.Exp, bias=abc[:, 0:1]), r=["pb0", "abc"], w=["GT"])
                        V(lambda: nc.vector.tensor_scalar(out=GT[:], in0=GT[:], scalar1=1.0, scalar2=None, op0=ALU.add), r=["GT"], w=["GT"])
                        A(lambda: nc.scalar.activation(out=GT[:], in_=GT[:], func=AF.Ln), r=["GT"], w=["GT"])
                        V(lambda: nc.vector.tensor_scalar(out=GT[:], in0=GT[:], scalar1=abc[:, 1:2], scalar2=None, op0=ALU.mult), r=["GT", "abc"], w=["GT"])
                        for blk in range(4):
                            V(lambda blk=blk: nc.vector.tensor_tensor_scan(
                                out=GT[:, blk * 128:(blk + 1) * 128], data0=onesf[0:8, 0:128], data1=GT[:, blk * 128:(blk + 1) * 128],
                                initial=0.0, op0=ALU.mult, op1=ALU.add), r=["GT", "cst"], w=["GT"])
                        A(lambda: nc.scalar.activation(out=BT[:], in_=pbank[1][0:8, 0:NTK], func=AF.Sigmoid), r=["pb1"], w=["BT"])
                        for blk in range(4):
                            T(lambda blk=blk: nc.tensor.transpose(pbank[2][:, blk * 8:blk * 8 + 8], GT[:, blk * 128:(blk + 1) * 128], ident[0:8, 0:8]),
                              r=["GT", "cst"], w=["pb2"])
                            T(lambda blk=blk: nc.tensor.transpose(pbank[2][:, 32 + blk * 8:32 + blk * 8 + 8], BT[:, blk * 128:(blk + 1) * 128], ident[0:8, 0:8]),
                              r=["BT", "cst"], w=["pb2"])
                        V(lambda: nc.vector.tensor_copy(out=Gc[:].rearrange("p b h -> p (b h)"), in_=pbank[2][:, 0:32]), r=["pb2"], w=["Gc"])
                        V(lambda: nc.vector.tensor_copy(out=Bc[:].rearrange("p b h -> p (b h)"), in_=pbank[2][:, 32:64]), r=["pb2"], w=["Bc"])
                        if "gA1" in dbg:
                            S.barrier()
                            return
                        UB, SB = (5, 7) if aug else (2, 6)

                        def head_views(h):
                            wt = wbuf[h % 2]
                            par = h % 2
                            kp = f"k{par}"
                            PKc = PKs[par]
                            X = PKc[:, 0:512].rearrange("p (b k) -> p b k", k=128)
                            BV = PKc[:, 512:1024].rearrange("p (b k) -> p b k", k=128)
                            BGK = PKc[:, 1024:1536].rearrange("p (b k) -> p b k", k=128)
                            KDC = PKc[:, 1536:2048].rearrange("p (b k) -> p b k", k=128)
                            sc4 = PKc[:, 2048:2080].rearrange("p (b k) -> p b k", k=8)
                            pkkeys = [f"{kp}{nm}{b_}" for nm in ("X", "BV", "BGK", "KDC", "sc4") for b_ in range(4)]

                            qi = par % NQ
                            kq = f"q{qi}"
                            return wt, par, kp, PKc, X, BV, BGK, KDC, sc4, pkkeys, CVQ[qi], QKTs[qi], OOs[qi], GSbs[qi], RNs[qi], kq

                        def head_front(h):
                            wt, par, kp, PKc, X, BV, BGK, KDC, sc4, pkkeys, CVq, QKT, OO, GSb, RN, kq = head_views(h)
                            EG, NB = OO, RN
                            cvt = lambda j_: CVq if j_ == 0 else CV[j_]
                            ck = lambda j_: f"CV{j_}" + (kq if j_ == 0 else "")
                            if piece == 0 and h == 0:
                                load_w(0, 0)
                            if not (piece == TP // NTK - 1 and h == 7):
                                load_w((h + 1) % 8, (h + 1) % 2)
                            if not aug:
                                S.dma("sp", lambda: nc.sync.dma_start(out=PKc[:], in_=gscr[piece * 8 + h]), reads=[f"gscr{piece * 8 + h}"], writes=pkkeys)
                            for j3 in ([1, 2] if aug else [0, 1]):
                                proj_fm(j3, NTK, wt, j3 * 128, hmp)
                                V(lambda j3=j3: nc.vector.tensor_copy(out=PC[j3][:, 0:3], in_=carry[:, j3, h, :]), r=["carry"], w=[f"PC{j3}"])
                                A(lambda j3=j3: nc.scalar.copy(out=PC[j3][:, 3:3 + NTK], in_=pbank[j3][:, 0:NTK]), r=[f"pb{j3}"], w=[f"PC{j3}"])
                                V(lambda j3=j3: nc.vector.tensor_copy(out=carry[:, j3, h, :], in_=PC[j3][:, NTK:NTK + 3]), r=[f"PC{j3}"], w=["carry"])
                                V(lambda j3=j3: nc.vector.tensor_scalar(out=cvt(j3)[:], in0=PC[j3][:, 0:NTK], scalar1=cw[:, j3, h, 0:1], scalar2=None,
                                                                        op0=ALU.mult), r=[f"PC{j3}", "cw"], w=[ck(j3)])
                                for tap in range(1, 4):
                                    V(lambda j3=j3, tap=tap: nc.vector.scalar_tensor_tensor(
                                        out=cvt(j3)[:], in0=PC[j3][:, tap:tap + NTK], scalar=cw[:, j3, h, tap:tap + 1], in1=cvt(j3)[:],
                                        op0=ALU.mult, op1=ALU.add), r=[f"PC{j3}", "cw", ck(j3)], w=[ck(j3)])
                                A(lambda j3=j3: nc.scalar.activation(out=cvt(j3)[:], in_=cvt(j3)[:], func=AF.Silu), r=[ck(j3)], w=[ck(j3)])
                                if j3 < 2:
                                    A(lambda j3=j3: nc.scalar.activation(out=SQb[:], in_=cvt(j3)[:], func=AF.Square), r=[ck(j3)], w=["SQb"])
                                    T(lambda: nc.tensor.matmul(pbank[3][:, 0:NTK], ones128[:], SQb[:], start=True, stop=True), r=["SQb", "ones128"], w=["pb3"])
                                    V(lambda: nc.vector.tensor_scalar(out=RN[:], in0=pbank[3][:, 0:NTK], scalar1=128.0, scalar2=1e-6,
                                                                      op0=ALU.mult, op1=ALU.add), r=["pb3"], w=[f"RN{kq}"])
                                    A(lambda: nc.scalar.activation(out=RN[:], in_=RN[:], func=AF.Ln), r=[f"RN{kq}"], w=[f"RN{kq}"])
                                    A(lambda: nc.scalar.activation(out=RN[:], in_=RN[:], func=AF.Exp, scale=-0.5), r=[f"RN{kq}"], w=[f"RN{kq}"])
                                    if j3 == 0:
                                        V(lambda: nc.vector.scalar_tensor_tensor(out=CVq[:], in0=CVq[:], scalar=128.0 ** -0.5, in1=RN[:],
                                                                                 op0=ALU.mult, op1=ALU.mult), r=[ck(0), f"RN{kq}"], w=[ck(0)])
                                    else:
                                        V(lambda: nc.vector.tensor_tensor(out=CV[1][:], in0=CV[1][:], in1=RN[:], op=ALU.mult), r=["CV1", f"RN{kq}"], w=["CV1"])
                            if not aug:
                                proj_fm(3, NTK, wt, 384, hmp)
                                A(lambda: nc.scalar.activation(out=GSb[:], in_=pbank[3][:, 0:NTK], func=AF.Silu), r=["pb3"], w=[f"GSb{kq}"])
                            if h == 7 and piece == TP // NTK - 1:
                                if aug:
                                    S.dma("sp", lambda: nc.sync.dma_start(out=o_gconv[:, 1:3, :, :], in_=carry[:, 1:3, :, :]), reads=["carry"])
                                else:
                                    S.dma("sp", lambda: nc.sync.dma_start(out=o_gconv[:, 0:1, :, :], in_=carry[:, 0:1, :, :]), reads=["carry"])
                            yield
                            T(lambda: nc.tensor.matmul(pbank[4][:, 0:NTK], ident[0:8, h:h + 1].to_broadcast([8, 128]), GT[:], start=True, stop=True),
                              r=["GT", "cst"], w=["pb4"])
                            V(lambda: nc.vector.tensor_copy(out=OO[:], in_=pbank[4][:, 0:NTK]), r=["pb4"], w=[f"OO{kq}"])
                            if aug:
                                T(lambda: nc.tensor.matmul(pbank[3][:, 0:NTK], ident[0:8, h:h + 1].to_broadcast([8, 128]), BT[:], start=True, stop=True),
                                  r=["BT", "cst"], w=["pb3"])
                                V(lambda: nc.vector.tensor_scalar(out=NB[:], in0=pbank[3][:, 0:NTK], scalar1=-1.0, scalar2=None, op0=ALU.mult), r=["pb3"], w=[f"RN{kq}"])
                            for blk in range(4):
                                bs = slice(blk * 128, (blk + 1) * 128)
                                if aug:
                                    V(lambda: nc.vector.tensor_scalar(out=sc4[:, blk, 3:4], in0=OO[:, blk * 128 + 127:blk * 128 + 128], scalar1=Gc[:, blk, h:h + 1],
                                                                      scalar2=None, op0=ALU.subtract), r=[f"OO{kq}", "Gc"], w=[f"{kp}sc4{blk}"])
                                    A(lambda: nc.scalar.activation(out=sc4[:, blk, 1:2], in_=sc4[:, blk, 3:4], func=AF.Exp), r=[f"{kp}sc4{blk}"], w=[f"{kp}sc4{blk}"])
                                    A(lambda: nc.scalar.activation(out=sc4[:, blk, 2:3], in_=OO[:, blk * 128 + 127:blk * 128 + 128], func=AF.Exp), r=[f"OO{kq}"], w=[f"{kp}sc4{blk}"])
                                    A(lambda: nc.scalar.activation(out=sc4[:, blk, 0:1], in_=Gc[:, blk, h:h + 1], func=AF.Exp), r=["Gc"], w=[f"{kp}sc4{blk}"])
                                    V(lambda: nc.vector.tensor_tensor(out=sc4[:, blk, 0:1], in0=sc4[:, blk, 0:1], in1=Bc[:, blk, h:h + 1], op=ALU.mult),
                                      r=["Bc", f"{kp}sc4{blk}"], w=[f"{kp}sc4{blk}"])
                                V(lambda: nc.vector.tensor_scalar(out=DmT[:, blk, :], in0=OO[:, bs], scalar1=Gc[:, blk, h:h + 1], scalar2=0.0,
                                                                  op0=ALU.subtract, op1=ALU.min), r=[f"OO{kq}", "Gc"], w=[f"DmT{blk}"])
                                A(lambda: nc.scalar.activation(out=DmT[:, blk, :], in_=DmT[:, blk, :], func=AF.Exp), r=[f"DmT{blk}"], w=[f"DmT{blk}"])
                            if not aug:
                                A(lambda: nc.scalar.activation(out=OO[:], in_=OO[:], func=AF.Exp), r=[f"OO{kq}"], w=[f"OO{kq}"])
                            if aug:
                                for blk in range(4):
                                    bs = slice(blk * 128, (blk + 1) * 128)
                                    T(lambda: nc.tensor.transpose(pbank[6][:, blk * 128:(blk + 1) * 128], CV[1][:, bs], ident), r=["CV1", "cst"], w=["pb6"])
                                    T(lambda: nc.tensor.transpose(pbank[4][:, blk * 128:(blk + 1) * 128], CV[2][:, bs], ident), r=["CV2", "cst"], w=["pb4"])
                                for blk in range(4):
                                    A(lambda blk=blk: nc.scalar.activation(out=BGK[:, blk, :], in_=pbank[6][:, blk * 128:(blk + 1) * 128], func=AF.Copy, scale=sc4[:, blk, 0:1]),
                                      r=["pb6", f"{kp}sc4{blk}"], w=[f"{kp}BGK{blk}"])
                                    A(lambda blk=blk: nc.scalar.activation(out=KDC[:, blk, :], in_=pbank[6][:, blk * 128:(blk + 1) * 128], func=AF.Copy, scale=sc4[:, blk, 1:2]),
                                      r=["pb6", f"{kp}sc4{blk}"], w=[f"{kp}KDC{blk}"])
                                    V(lambda blk=blk: nc.vector.tensor_scalar(out=BV[:, blk, :], in0=pbank[4][:, blk * 128:(blk + 1) * 128], scalar1=Bc[:, blk, h:h + 1],
                                                                              scalar2=None, op0=ALU.mult), r=["pb4", "Bc"], w=[f"{kp}BV{blk}"])
                                for blk in range(4):
                                    bs = slice(blk * 128, (blk + 1) * 128)
                                    T(lambda: nc.tensor.matmul(pbank[6][:, bs], CV[1][:, bs], CV[1][:, bs], start=True, stop=True), r=["CV1"], w=["pb6"])
                            else:
                                for blk in range(4):
                                    bs = slice(blk * 128, (blk + 1) * 128)
                                    T(lambda: nc.tensor.matmul(pbank[7][:, bs], CV[1][:, bs], CVq[:, bs], start=True, stop=True), r=["CV1", ck(0)], w=["pb7"])
                            if not aug:
                                V(lambda: nc.vector.tensor_tensor(out=CVq[:], in0=CVq[:], in1=EG[:], op=ALU.mult), r=[ck(0), f"OO{kq}"], w=[ck(0)])
                            QG = CVq
                            if aug:
                                R0, N0 = RN0[0], RN0[1]
                                for blk in range(4):
                                    bs = slice(blk * 128, (blk + 1) * 128)
                                    V(lambda: nc.vector.tensor_tensor(out=DmT[:, blk, :], in0=DmT[:, blk, :], in1=Ustrict, op=ALU.mult), r=[f"DmT{blk}", "tri"], w=[f"DmT{blk}"])
                                    V(lambda: nc.vector.tensor_tensor(out=DmT[:, blk, :], in0=DmT[:, blk, :], in1=NB[:, bs], op=ALU.mult), r=[f"DmT{blk}", f"RN{kq}"], w=[f"DmT{blk}"])
                                    V(lambda: nc.vector.tensor_tensor(out=R0[:, blk, :], in0=DmT[:, blk, :], in1=pbank[6][:, bs], op=ALU.mult), r=[f"DmT{blk}", "pb6"], w=[f"R0_{blk}"])
                                for blk in range(4):
                                    T(lambda blk=blk: nc.tensor.transpose(pbank[3][:, blk * 128:(blk + 1) * 128], R0[:, blk, :], ident), r=[f"R0_{blk}", "cst"], w=["pb3"])
                                V(lambda: nc.vector.tensor_copy(out=N0[:].rearrange("p b k -> p (b k)"), in_=pbank[3][:, 0:512]), r=["pb3"], w=[f"N0_{b_}" for b_ in range(4)])
                                allk = lambda nm: [f"{nm}{b_}" for b_ in range(4)]
                                grk = lambda nm, g_: [f"{nm}{b_}" for b_ in (2 * g_, 2 * g_ + 1)]
                                gfl = lambda t, g_: t[:, 2 * g_:2 * g_ + 2, :].rearrange("p b k -> p (b k)")
                                GB = ((0, 1, 2), (3, 4, 6))
                                m16u = tri[:, 256:384].unsqueeze(1).to_broadcast([128, 4, 128])
                                m16l = tri[:, 384:512].unsqueeze(1).to_broadcast([128, 4, 128])
                                Rd, Nd, Rd2, Nd2 = WK[0], WK[1], WK[2], WK[3]
                                V(lambda: nc.vector.tensor_tensor(out=Rd[:], in0=R0[:], in1=m16u, op=ALU.mult), r=allk("R0_") + ["tri"], w=allk("Rd"))
                                S.op("pool", lambda: nc.gpsimd.tensor_tensor(out=Nd[:], in0=N0[:], in1=m16l, op=ALU.mult), reads=allk("N0_") + ["tri"], writes=allk("Nd"))
                                V(lambda: nc.vector.tensor_tensor(out=X[:], in0=Rd[:], in1=ident.unsqueeze(1).to_broadcast([128, 4, 128]), op=ALU.add),
                                  r=allk("Rd") + ["cst"], w=allk(kp + "X"))
                                for lev in range(3):
                                    for g_ in range(2):
                                        b0_, b1_, b2_ = GB[g_]
                                        for q_ in range(2):
                                            blk = 2 * g_ + q_
                                            qs = slice(q_ * 128, (q_ + 1) * 128)
                                            T(lambda: nc.tensor.matmul(pbank[b0_][:, qs], Nd[:, blk, :], Rd[:, blk, :], start=True, stop=True), r=[f"Nd{blk}", f"Rd{blk}"], w=[f"pb{b0_}"])
                                            T(lambda: nc.tensor.matmul(pbank[b1_][:, qs], Rd[:, blk, :], Nd[:, blk, :], start=True, stop=True), r=[f"Nd{blk}", f"Rd{blk}"], w=[f"pb{b1_}"])
                                    for g_ in range(2):
                                        b0_, b1_, b2_ = GB[g_]
                                        A(lambda: nc.scalar.copy(out=gfl(Rd2, g_), in_=pbank[b0_][:, 0:256]), r=[f"pb{b0_}"], w=grk("Rd2", g_))
                                        V(lambda: nc.vector.tensor_copy(out=gfl(Nd2, g_), in_=pbank[b1_][:, 0:256]), r=[f"pb{b1_}"], w=grk("Nd2", g_))
                                    for g_ in range(2):
                                        b0_, b1_, b2_ = GB[g_]
                                        for q_ in range(2):
                                            blk = 2 * g_ + q_
                                            qs = slice(q_ * 128, (q_ + 1) * 128)
                                            T(lambda: nc.tensor.matmul(pbank[b2_][:, qs], Nd2[:, blk, :], X[:, blk, :], start=True, stop=True), r=[f"Nd2{blk}", f"{kp}X{blk}"], w=[f"pb{b2_}"])
                                    for g_ in range(2):
                                        b0_, b1_, b2_ = GB[g_]
                                        V(lambda: nc.vector.tensor_tensor(out=gfl(X, g_), in0=gfl(X, g_), in1=pbank[b2_][:, 0:256], op=ALU.add), r=[f"pb{b2_}"] + grk(kp + "X", g_), w=grk(kp + "X", g_))
                                    Rd, Nd, Rd2, Nd2 = Rd2, Nd2, Rd, Nd
                                    for b_ in range(4):
                                        for a_, c_ in (("Rd", "Rd2"), ("Nd", "Nd2")):
                                            sa, sc = S.res.get(f"{a_}{b_}"), S.res.get(f"{c_}{b_}")
                                            if sc is not None:
                                                S.res[f"{a_}{b_}"] = sc
                                            elif f"{a_}{b_}" in S.res:
                                                del S.res[f"{a_}{b_}"]
                                            if sa is not None:
                                                S.res[f"{c_}{b_}"] = sa
                                            elif f"{c_}{b_}" in S.res:
                                                del S.res[f"{c_}{b_}"]
                                    yield
                                for g_ in range(2):
                                    b0_ = GB[g_][0]
                                    for q_ in range(2):
                                        blk = 2 * g_ + q_
                                        T(lambda: nc.tensor.transpose(pbank[b0_][:, q_ * 128:(q_ + 1) * 128], X[:, blk, :], ident), r=[f"{kp}X{blk}", "cst"], w=[f"pb{b0_}"])
                                for g_ in range(2):
                                    b0_ = GB[g_][0]
                                    A(lambda: nc.scalar.copy(out=gfl(Xt, g_), in_=pbank[b0_][:, 0:256]), r=[f"pb{b0_}"], w=grk("Xt", g_))
                                Rl, Nl, Y, Yt = WK[0], WK[1], WK[2], WK[3]
                                for li in range(3):
                                    mu = tri[:, 512 + li * 256:640 + li * 256].unsqueeze(1).to_broadcast([128, 4, 128])
                                    ml = tri[:, 640 + li * 256:768 + li * 256].unsqueeze(1).to_broadcast([128, 4, 128])
                                    V(lambda: nc.vector.tensor_tensor(out=Rl[:], in0=R0[:], in1=mu, op=ALU.mult), r=allk("R0_") + ["tri"], w=allk("Rl") + allk("Rd") + allk("Rd2"))
                                    S.op("pool", lambda: nc.gpsimd.tensor_tensor(out=Nl[:], in0=N0[:], in1=ml, op=ALU.mult), reads=allk("N0_") + ["tri"], writes=allk("Nl") + allk("Nd") + allk("Nd2"))
                                    for g_ in range(2):
                                        b0_, b1_, b2_ = GB[g_]
                                        for q_ in range(2):
                                            blk = 2 * g_ + q_
                                            qs = slice(q_ * 128, (q_ + 1) * 128)
                                            T(lambda: nc.tensor.matmul(pbank[b0_][:, qs], Nl[:, blk, :], X[:, blk, :], start=True, stop=True), r=[f"Nl{blk}", f"{kp}X{blk}"], w=[f"pb{b0_}"])
                                            if li < 2:
                                                T(lambda: nc.tensor.matmul(pbank[b1_][:, qs], Rl[:, blk, :], Xt[:, blk, :], start=True, stop=True), r=[f"Rl{blk}", f"Xt{blk}"], w=[f"pb{b1_}"])
                                    for g_ in range(2):
                                        b0_, b1_, b2_ = GB[g_]
                                        A(lambda: nc.scalar.copy(out=gfl(Y, g_), in_=pbank[b0_][:, 0:256]), r=[f"pb{b0_}"], w=grk("Y", g_) + grk("Rd", g_) + grk("Rd2", g_) + grk("Nd", g_) + grk("Nd2", g_))
                                        if li < 2:
                                            V(lambda: nc.vector.tensor_copy(out=gfl(Yt, g_), in_=pbank[b1_][:, 0:256]), r=[f"pb{b1_}"], w=grk("Yt", g_) + grk("Rd", g_) + grk("Rd2", g_) + grk("Nd", g_) + grk("Nd2", g_))
                                    for g_ in range(2):
                                        b0_, b1_, b2_ = GB[g_]
                                        for q_ in range(2):
                                            blk = 2 * g_ + q_
                                            qs = slice(q_ * 128, (q_ + 1) * 128)
                                            T(lambda: nc.tensor.matmul(pbank[b2_][:, qs], Xt[:, blk, :], Y[:, blk, :], start=True, stop=True), r=[f"Xt{blk}", f"Y{blk}"], w=[f"pb{b2_}"])
                                            if li < 2:
                                                T(lambda: nc.tensor.matmul(pbank[b0_][:, qs], X[:, blk, :], Yt[:, blk, :], start=True, stop=True), r=[f"{kp}X{blk}", f"Yt{blk}"], w=[f"pb{b0_}"])
                                    for g_ in range(2):
                                        b0_, b1_, b2_ = GB[g_]
                                        V(lambda: nc.vector.tensor_tensor(out=gfl(X, g_), in0=gfl(X, g_), in1=pbank[b2_][:, 0:256], op=ALU.add), r=[f"pb{b2_}"] + grk(kp + "X", g_), w=grk(kp + "X", g_))
                                        if li < 2:
                                            V(lambda: nc.vector.tensor_tensor(out=gfl(Xt, g_), in0=gfl(Xt, g_), in1=pbank[b0_][:, 0:256], op=ALU.add), r=[f"pb{b0_}"] + grk("Xt", g_), w=grk("Xt", g_))
                                    yield
                                S.dma("sp", lambda: nc.sync.dma_start(out=gscr[piece * 8 + h], in_=PKc[:]), reads=pkkeys, writes=[f"gscr{piece * 8 + h}"])
                            else:
                                for blk in range(4):
                                    bs = slice(blk * 128, (blk + 1) * 128)
                                    V(lambda: nc.vector.tensor_tensor(out=QKT[:, blk, :], in0=DmT[:, blk, :], in1=Uincl, op=ALU.mult), r=[f"DmT{blk}", "tri"], w=[f"QKT{kq}{blk}"])
                                    V(lambda: nc.vector.tensor_tensor(out=QKT[:, blk, :], in0=QKT[:, blk, :], in1=pbank[7][:, bs], op=ALU.mult), r=[f"QKT{kq}{blk}", "pb7"], w=[f"QKT{kq}{blk}"])

                            yield

                        def head_back(h):
                            wt, par, kp, PKc, X, BV, BGK, KDC, sc4, pkkeys, CVq, QKT, OO, GSb, RN, kq = head_views(h)
                            EG, NB = OO, RN
                            cvt = lambda j_: CVq if j_ == 0 else CV[j_]
                            ck = lambda j_: f"CV{j_}" + (kq if j_ == 0 else "")
                            QG = CVq
                            for blk in range(4):
                                bs = slice(blk * 128, (blk + 1) * 128)
                                T(lambda: nc.tensor.matmul(pbank[UB][:, 0:128], X[:, blk, :], BV[:, blk, :], start=True, stop=True), r=[f"{kp}X{blk}", f"{kp}BV{blk}"], w=[f"pb{UB}"])
                                T(lambda: nc.tensor.matmul(pbank[UB][:, 128:256], BGK[:, blk, :], X[:, blk, :], start=True, stop=True), r=[f"{kp}X{blk}", f"{kp}BGK{blk}"], w=[f"pb{UB}"])
                                yield
                                V(lambda: nc.vector.tensor_copy(out=U[:], in_=pbank[UB][:, 0:128]), r=[f"pb{UB}"], w=["U"])
                                V(lambda: nc.vector.tensor_copy(out=WT[:], in_=pbank[UB][:, 128:256]), r=[f"pb{UB}"], w=["WT"])
                                T(lambda: nc.tensor.matmul(pbank[SB][:, 0:VW], WT[:], Sst[:, h, 0:VW], start=True, stop=True), r=["WT", f"S{h}"], w=[f"pb{SB}"])
                                yield
                                V(lambda: nc.vector.tensor_tensor(out=DL[:, 0:128], in0=U[:], in1=pbank[SB][:, 0:128], op=ALU.subtract), r=["U", f"pb{SB}"], w=["DL"])
                                if aug:
                                    V(lambda: nc.vector.tensor_scalar(out=DL[:, 128:256], in0=pbank[SB][:, 128:256], scalar1=-1.0, scalar2=None, op0=ALU.mult), r=[f"pb{SB}"], w=["DL"])
                                else:
                                    T(lambda: nc.tensor.matmul(pbank[5][:, 0:128], Sst[:, h, 0:128], QG[:, bs], start=True, stop=False), r=[f"S{h}", ck(0)], w=["pb5"])
                                    T(lambda: nc.tensor.matmul(pbank[5][:, 0:128], DL[:, 0:128], QKT[:, blk, :], start=False, stop=True), r=["DL", f"QKT{kq}{blk}"], w=["pb5"])
                                    A(lambda: nc.scalar.copy(out=OO[:, bs], in_=pbank[5][:, 0:128]), r=["pb5"], w=[f"OO{kq}"])
                                T(lambda: nc.tensor.matmul(pbank[SB][:, 256:256 + VW], KDC[:, blk, :], DL[:, 0:VW], start=True, stop=True), r=[f"{kp}KDC{blk}", "DL"], w=[f"pb{SB}"])
                                yield
                                V(lambda: nc.vector.scalar_tensor_tensor(out=Sst[:, h, 0:VW], in0=Sst[:, h, 0:VW], scalar=sc4[:, blk, 2:3], in1=pbank[SB][:, 256:256 + VW],
                                                                         op0=ALU.mult, op1=ALU.add), r=[f"S{h}", f"{kp}sc4{blk}", f"pb{SB}"], w=[f"S{h}"])
                                yield
                            if not aug:
                                A(lambda: nc.scalar.activation(out=SQb[:], in_=OO[:], func=AF.Square), r=[f"OO{kq}"], w=["SQb"])
                                T(lambda: nc.tensor.matmul(pbank[UB][:, 0:NTK], ones128[:], SQb[:], start=True, stop=True), r=["SQb", "ones128"], w=[f"pb{UB}"])
                                V(lambda: nc.vector.tensor_scalar(out=RN[:], in0=pbank[UB][:, 0:NTK], scalar1=1e-6, scalar2=None, op0=ALU.add), r=[f"pb{UB}"], w=[f"RN{kq}"])
                                A(lambda: nc.scalar.activation(out=RN[:], in_=RN[:], func=AF.Ln), r=[f"RN{kq}"], w=[f"RN{kq}"])
                                A(lambda: nc.scalar.activation(out=RN[:], in_=RN[:], func=AF.Exp, scale=-0.5), r=[f"RN{kq}"], w=[f"RN{kq}"])
                                V(lambda: nc.vector.tensor_tensor(out=OO[:], in0=OO[:], in1=RN[:], op=ALU.mult), r=[f"OO{kq}", f"RN{kq}"], w=[f"OO{kq}"])
                                V(lambda: nc.vector.scalar_tensor_tensor(out=og[:, h, :], in0=OO[:], scalar=gg_t[:, 0:1], in1=GSb[:], op0=ALU.mult, op1=ALU.mult),
                                  r=[f"OO{kq}", f"GSb{kq}", "gg"], w=["og"])


                            yield

                        if True:
                            for _ in head_front(0):
                                pass
                            for h in range(8):
                                gb = head_back(h)
                                gf = head_front(h + 1) if h + 1 < 8 else iter(())
                                done_b = done_f = False
                                while not (done_b and done_f):
                                    if not done_f:
                                        try:
                                            next(gf)
                                        except StopIteration:
                                            done_f = True
                                    if not done_b:
                                        try:
                                            next(gb)
                                        except StopIteration:
                                            done_b = True
                    S.barrier()
                    if not aug:
                        yield piece

            for h in range(8):
                V(lambda h=h: nc.vector.memset(Sst[:, h, 0:128], 0.0), w=[f"S{h}"])
                V(lambda h=h: nc.vector.tensor_copy(out=Sst[:, h, 128:256], in_=ident), r=["cst"], w=[f"S{h}"])
            if "stop0" in dbg:
                return
            with scope("gdnA"):
                for _ in gdn_pass("A"):
                    pass
            if "stopA" in dbg:
                return
            with contextlib.ExitStack() as st2:
                xs2 = P.sb("e_xs2", [128, 8, 256], stack=st2)
                for h in range(8):
                    V(lambda h=h: nc.vector.tensor_copy(out=xs2[:, h, 0:128], in_=Sst[:, h, 0:128]), r=[f"S{h}"], w=["xs2"])
                    T(lambda h=h: nc.tensor.transpose(pbank[h % 2][:, 0:128], Sst[:, h, 128:256], ident), r=[f"S{h}", "cst"], w=[f"pb{h % 2}"])
                    V(lambda h=h: nc.vector.tensor_copy(out=xs2[:, h, 128:256], in_=pbank[h % 2][:, 0:128]), r=[f"pb{h % 2}"], w=["xs2"])
                S.dma("pool", lambda: nc.gpsimd.dma_start(out=xsrc2[:, :], in_=xs2[:].rearrange("p h k -> p (h k)")), reads=["xs2"], writes=["xsrc2"])
                S.collective(lambda: nc.gpsimd.collective_compute(
                    "AllGather", ALU.bypass, replica_groups=([[0, 1, 2, 3]] if "half" in dbg else [[0, 1, 2, 3], [4, 5, 6, 7]]),
                    ins=[xsrc2.ap().opt()], outs=[xdst2.ap().opt()]), reads=["xsrc2"], writes=["xdst2"])
                for h in range(8):
                    V(lambda h=h: nc.vector.memset(Sst[:, h, 0:128], 0.0), r=["xs2"], w=[f"S{h}"])
                gr = P.sb("e_gr", [128, 8, 256], stack=st2)
                dS = P.sb("e_dS", [128, 128], stack=st2)
                for r_ in range(3):
                    S.dma("sp", lambda r_=r_: nc.sync.dma_start(out=gr[:].rearrange("p h k -> p (h k)"), in_=xdst2[r_ * 128:(r_ + 1) * 128, :]),
                          reads=["xdst2"], writes=["gr"])
                    for h in range(8):
                        T(lambda h=h: nc.tensor.matmul(pbank[h % 2][:, 0:128], gr[:, h, 128:256], Sst[:, h, 0:128], start=True, stop=True),
                          r=["gr", f"S{h}"], w=[f"pb{h % 2}"])
                        V(lambda h=h: nc.vector.tensor_tensor(out=dS[:], in0=gr[:, h, 0:128], in1=pbank[h % 2][:, 0:128], op=ALU.add), r=["gr", f"pb{h % 2}"], w=["dS"])
                        V(lambda h=h: nc.vector.tensor_tensor(out=dS[:], in0=dS[:], in1=Sst[:, h, 0:128], op=ALU.subtract), r=["dS", f"S{h}"], w=["dS"])
                        V(lambda h=h: nc.vector.scalar_tensor_tensor(out=Sst[:, h, 0:128], in0=dS[:], scalar=pc_t[:, 4 + r_:5 + r_], in1=Sst[:, h, 0:128],
                                                                     op0=ALU.mult, op1=ALU.add), r=["dS", "pc", f"S{h}"], w=[f"S{h}"])
                S.barrier()

            if "stopX" in dbg:
                return
            o_gs = P.sb("n_ogs", [128, 8, NS], BF16, stack=st)
            with scope("gdnS"):
                sample_gdn(o_gs, wbuf, load_w, pc_t, cw, abc, gg_t)
            if "stopS" in dbg:
                return

            for piece in gdn_pass("B"):
                out_proj_ln(l, jg, wout1, [("p", piece * NTK, NTK)], lambda kc, t0, n: og[:, kc, 0:n], "og")
            for h in range(8):
                S.dma("sp", lambda h=h: nc.sync.dma_start(out=o_gdn[:, h, :], in_=Sst[:, h, 0:128]), reads=[f"S{h}"])
            out_proj_ln(l, jg, wout1, [("s", 0, NS)], lambda kc, t0, n: o_gs[:, kc, :], "o_gs")
            S.barrier()

    def out_proj_ln(l, jg, wout_dram, tl, srcfn, srckey):
        with contextlib.ExitStack() as st2:
            wo = P.sb("o_wo", [128, 8, D], BF16, stack=st2)
            S.dma("pool", lambda: nc.gpsimd.dma_start(out=wo[:], in_=wout_dram[:, :, :]), writes=["wo"])
            xb = P.sb("o_xb", [128, DC, TT], BF16, stack=st2)
            sq = P.sb("o_sq", [128, DC, TT], BF16, stack=st2)
            r = [P.sb(f"o_r{i}", [128, TT], stack=st2) for i in range(2)]
            lt = [P.sb(f"o_lt{i}", [128, TT], stack=st2) for i in range(4)]
            for (kind, t0, n) in tl:
                xv = xview(kind, t0, n)
                xk = f"x{kind}{t0}"
                for co in range(DC):
                    pd = pbank[co % 2]
                    for kc in range(8):
                        T(lambda kc=kc: nc.tensor.matmul(pd[:, :n], wo[:, kc, co * 128:(co + 1) * 128], srcfn(kc, t0, n),
                                                         start=(kc == 0), stop=(kc == 7)), r=["wo", srckey], w=[f"pb{co % 2}"])
                    rr = r[co % 2]
                    if kind == "p":
                        A(lambda: nc.scalar.activation(out=rr[:, :n], in_=pd[:, :n], func=AF.Copy, scale=mods[:, l, jg * 8 + co, 0:1]),
                          r=[f"pb{co % 2}", "mods"], w=[f"r{co % 2}"])
                    else:
                        V(lambda: nc.vector.tensor_tensor(out=rr[:, :n], in0=pd[:, :n], in1=mods[:, l, jg * 8 + co, 1:1 + NS], op=ALU.mult),
                          r=[f"pb{co % 2}", "mods"], w=[f"r{co % 2}"])
                    V(lambda: nc.vector.scalar_tensor_tensor(out=xv[:, co, :], in0=xv[:, co, :], scalar=DN_ALPHA, in1=rr[:, :n],
                                                             op0=ALU.mult, op1=ALU.add), r=[f"r{co % 2}", xk, f"{xk}c{co}"], w=[f"{xk}c{co}"])
                    A(lambda: nc.scalar.copy(out=xb[:, co, :n], in_=xv[:, co, :]), r=[f"{xk}c{co}"], w=[f"xb{co}"])
                    A(lambda: nc.scalar.activation(out=sq[:, co, :n], in_=xv[:, co, :], func=AF.Square), r=[f"{xk}c{co}"], w=[f"sq{co}"])
                layer_norm_tile(xk, xv, n, xb, sq, lt, (l * 3 + 1) * 8)
            S.barrier()

    def sample_gdn(o_gs, wbuf, load_w, pc_t, cw, abc, gg_t):
        l = 1
        jsh, jsc = 3, 4
        with contextlib.ExitStack() as st2:
            hms = P.sb("z_hms", [128, DC, NS], BF16, stack=st2)
            t16 = P.sb("z_t16", [128, NS], stack=st2)
            for c in range(DC):
                V(lambda c=c: nc.vector.tensor_tensor(out=t16[:], in0=xs[:, c, :], in1=mods[:, l, jsc * 8 + c, 1:1 + NS], op=ALU.mult),
                  r=["xs0", "mods"], w=["t16"])
                V(lambda c=c: nc.vector.tensor_tensor(out=hms[:, c, :], in0=t16[:], in1=mods[:, l, jsh * 8 + c, 1:1 + NS], op=ALU.add),
                  r=["t16", "mods"], w=["hms"])
            nr = [P.sb(f"z_nr{i}", [NS, 384], stack=st2) for i in range(2)]
            hsl = [P.sb(f"z_hsl{i}", [3 * NS, 128], stack=st2) for i in range(2)]
            hcount = [0]
            S.dma("sp", lambda: nc.sync.dma_start(out=o_sc[:, 0:2, :], in_=sc_in[:, 1:3, :]))
            QKV = P.sb("z_QKV", [128, 3, 8, NS], stack=st2)
            GS = P.sb("z_GS", [128, 8, NS], stack=st2)
            hT = P.sb("z_hT", [128, 3 * NS], stack=st2)
            pre = P.sb("z_pre", [128, NS], stack=st2)
            AB = P.sb("z_AB", [128, 8, NS], stack=st2)
            BB = P.sb("z_BB", [128, 8, NS], stack=st2)
            ab = P.sb("z_ab", [8, 2 * NS], stack=st2)
            wab = P.sb("z_wab", [128, DC, 16], BF16, stack=st2)
            S.dma("pool", lambda: nc.gpsimd.dma_start(out=wab[:], in_=win1[8, :, :, 0:16]), writes=["zwab"])
            for c in range(DC):
                T(lambda c=c: nc.tensor.matmul(pbank[0][0:8, 0:NS], wab[:, c, 0:8], hms[:, c, :], start=(c == 0), stop=(c == DC - 1)), r=["zwab", "hms"], w=["pb0"])
            for c in range(DC):
                T(lambda c=c: nc.tensor.matmul(pbank[1][0:8, 0:NS], wab[:, c, 8:16], hms[:, c, :], start=(c == 0), stop=(c == DC - 1)), r=["zwab", "hms"], w=["pb1"])
            A(lambda: nc.scalar.activation(out=ab[:, 0:NS], in_=pbank[0][0:8, 0:NS], func=AF.Exp, bias=abc[:, 0:1]), r=["pb0", "abc"], w=["zab"])
            V(lambda: nc.vector.tensor_scalar(out=ab[:, 0:NS], in0=ab[:, 0:NS], scalar1=1.0, scalar2=None, op0=ALU.add), r=["zab"], w=["zab"])
            A(lambda: nc.scalar.activation(out=ab[:, 0:NS], in_=ab[:, 0:NS], func=AF.Ln), r=["zab"], w=["zab"])
            V(lambda: nc.vector.tensor_scalar(out=ab[:, 0:NS], in0=ab[:, 0:NS], scalar1=abc[:, 1:2], scalar2=None, op0=ALU.mult), r=["zab", "abc"], w=["zab"])
            A(lambda: nc.scalar.activation(out=ab[:, 0:NS], in_=ab[:, 0:NS], func=AF.Exp), r=["zab"], w=["zab"])
            A(lambda: nc.scalar.activation(out=ab[:, NS:2 * NS], in_=pbank[1][0:8, 0:NS], func=AF.Sigmoid), r=["pb1"], w=["zab"])
            for h in range(8):
                T(lambda h=h: nc.tensor.matmul(pbank[2][:, h * 2 * NS:(h + 1) * 2 * NS], ident[0:8, h:h + 1].to_broadcast([8, 128]), ab[:], start=True, stop=True),
                  r=["zab", "cst"], w=["pb2"])
            V(lambda: nc.vector.tensor_copy(out=AB[:], in_=pbank[2][:, 0:16 * NS].rearrange("p (h t s) -> p h t s", t=2, s=NS)[:, :, 0, :]), r=["pb2"], w=["AB"])
            V(lambda: nc.vector.tensor_copy(out=BB[:], in_=pbank[2][:, 0:16 * NS].rearrange("p (h t s) -> p h t s", t=2, s=NS)[:, :, 1, :]), r=["pb2"], w=["BB"])
            load_w(0, 0)
            for h in range(8):
                b = h % 2
                if h + 1 < 8:
                    load_w(h + 1, (h + 1) % 2)
                for c in range(DC):
                    T(lambda c=c: nc.tensor.matmul(pbank[3][0:NS, 0:384], hms[:, c, :], wbuf[b][:, c, 0:384], start=(c == 0), stop=(c == DC - 1)),
                      r=["nw0", "nw1", "hms"], w=["pb3"])
                V(lambda h=h: nc.vector.tensor_copy(out=nr[h % 2][:], in_=pbank[3][0:NS, 0:384]), r=["pb3"], w=[f"nr{h % 2}"])
                for j3 in range(3):
                    S.dma("sp", lambda j3=j3, h=h: nc.sync.dma_start(out=o_sc[:, 2, j3 * 1024 + h * 128:j3 * 1024 + (h + 1) * 128],
                                                                     in_=nr[h % 2][:, j3 * 128:(j3 + 1) * 128]), reads=[f"nr{h % 2}"])
                for j3 in range(3):
                    ch0 = j3 * 1024 + h * 128
                    for c in range(DC):
                        T(lambda c=c: nc.tensor.matmul(pbank[4][:, 0:NS], wbuf[b][:, c, j3 * 128:(j3 + 1) * 128], hms[:, c, :], start=(c == 0), stop=(c == DC - 1)),
                          r=["nw0", "nw1", "hms"], w=["pb4"])
                    hb = hcount[0] % 2
                    hcount[0] += 1
                    S.dma("sp", lambda: nc.sync.dma_start(out=hsl[hb][:], in_=sc_in.rearrange("s j c -> (s j) c")[:, ch0:ch0 + 128]), writes=[f"hsl{hb}"])
                    T(lambda: nc.tensor.transpose(pbank[5][:, 0:3 * NS], hsl[hb][:], ident[0:3 * NS, 0:3 * NS]), r=[f"hsl{hb}", "cst"], w=["pb5"])
                    V(lambda: nc.vector.tensor_copy(out=hT[:], in_=pbank[5][:, 0:3 * NS]), r=["pb5"], w=["hT"])
                    hv = hT[:].rearrange("p (s j) -> p s j", j=3)
                    V(lambda: nc.vector.tensor_scalar(out=pre[:], in0=pbank[4][:, 0:NS], scalar1=cw[:, j3, h, 3:4], scalar2=None, op0=ALU.mult), r=["pb4", "cw"], w=["pre"])
                    for tap in range(3):
                        V(lambda tap=tap: nc.vector.scalar_tensor_tensor(out=pre[:], in0=hv[:, :, tap], scalar=cw[:, j3, h, tap:tap + 1], in1=pre[:],
                                                                         op0=ALU.mult, op1=ALU.add), r=["hT", "cw", "pre"], w=["pre"])
                    A(lambda: nc.scalar.activation(out=QKV[:, j3, h, :], in_=pre[:], func=AF.Silu), r=["pre"], w=["QKV"])
                for c in range(DC):
                    T(lambda c=c: nc.tensor.matmul(pbank[4][:, 0:NS], wbuf[b][:, c, 384:512], hms[:, c, :], start=(c == 0), stop=(c == DC - 1)),
                      r=["nw0", "nw1", "hms"], w=["pb4"])
                A(lambda: nc.scalar.activation(out=GS[:, h, :], in_=pbank[4][:, 0:NS], func=AF.Silu), r=["pb4"], w=["zGS"])
            sqb = P.sb("z_sq", [128, 2 * 8 * NS], BF16, stack=st2)
            rn = P.sb("z_rn", [128, 2 * 8 * NS], stack=st2)
            A(lambda: nc.scalar.activation(out=sqb[:], in_=QKV[:, 0:2, :, :].rearrange("p a h s -> p (a h s)"), func=AF.Square), r=["QKV"], w=["zsq"])
            T(lambda: nc.tensor.matmul(pbank[6][:, 0:256], ones128[:], sqb[:], start=True, stop=True), r=["zsq", "ones128"], w=["pb6"])
            V(lambda: nc.vector.tensor_scalar(out=rn[:], in0=pbank[6][:, 0:256], scalar1=128.0, scalar2=1e-6, op0=ALU.mult, op1=ALU.add), r=["pb6"], w=["zrn"])
            A(lambda: nc.scalar.activation(out=rn[:], in_=rn[:], func=AF.Ln), r=["zrn"], w=["zrn"])
            A(lambda: nc.scalar.activation(out=rn[:], in_=rn[:], func=AF.Exp, scale=-0.5), r=["zrn"], w=["zrn"])
            V(lambda: nc.vector.tensor_tensor(out=QKV[:, 0:2, :, :].rearrange("p a h s -> p (a h s)"), in0=QKV[:, 0:2, :, :].rearrange("p a h s -> p (a h s)"),
                                              in1=rn[:], op=ALU.mult), r=["QKV", "zrn"], w=["QKV"])
            V(lambda: nc.vector.tensor_scalar(out=QKV[:, 0, :, :], in0=QKV[:, 0, :, :], scalar1=128.0 ** -0.5, scalar2=None, op0=ALU.mult), r=["QKV"], w=["QKV"])
            NAK = P.sb("z_NAK", [128, 8, NS], stack=st2)
            BK = P.sb("z_BK", [128, 8, NS], stack=st2)
            vtm = P.sb("z_vtm", [NS, 8, 128], stack=st2)
            V(lambda: nc.vector.scalar_tensor_tensor(out=NAK[:], in0=QKV[:, 1, :, :], scalar=-1.0, in1=AB[:], op0=ALU.mult, op1=ALU.mult), r=["QKV", "AB"], w=["NAK"])
            V(lambda: nc.vector.tensor_tensor(out=BK[:], in0=QKV[:, 1, :, :], in1=BB[:], op=ALU.mult), r=["QKV", "BB"], w=["BK"])
            for h in range(8):
                T(lambda h=h: nc.tensor.transpose(pbank[7][0:NS, (h % 4) * 128:(h % 4 + 1) * 128], QKV[:, 2, h, :], ident), r=["QKV", "cst"], w=["pb7"])
                if h % 4 == 3:
                    V(lambda h=h: nc.vector.tensor_copy(out=vtm[:, h - 3:h + 1, :].rearrange("p h v -> p (h v)"), in_=pbank[7][0:NS, 0:512]), r=["pb7"], w=["vtm"])
            Sin = P.sb("z_Sin", [128, NS, 128], stack=st2)
            Sout = P.sb("z_Sout", [128, NS, 128], stack=st2)
            tSa = P.sb("z_tSa", [128, NS, 128], stack=st2)
            ob = P.sb("z_ob", [128, 8 * NS], stack=st2)
            for h in range(8):
                S.dma("sp", lambda h=h: nc.sync.dma_start(out=Sin[:], in_=sg_in[:, h, :, :].rearrange("s k v -> k s v")), writes=["zSin"])
                banks_ = [2, 3, 6, 7]
                for s_ in range(NS):
                    pi = banks_[s_ // 4]
                    ps_ = pbank[pi][:, (s_ % 4) * 128:(s_ % 4 + 1) * 128]
                    T(lambda s_=s_, ps_=ps_: nc.tensor.matmul(ps_, ident[0:NS, s_:s_ + 1].to_broadcast([NS, 128]), vtm[:, h, :], start=True, stop=False),
                      r=["vtm", "cst"], w=[f"pb{pi}"])
                    T(lambda s_=s_, ps_=ps_: nc.tensor.matmul(ps_, NAK[:, h, s_:s_ + 1].to_broadcast([128, 128]), Sin[:, s_, :], start=False, stop=True),
                      r=["NAK", "zSin"], w=[f"pb{pi}"])
                for s_ in range(NS):
                    pi = banks_[s_ // 4]
                    V(lambda s_=s_, pi=pi: nc.vector.tensor_scalar(out=tSa[:, s_, :], in0=pbank[pi][:, (s_ % 4) * 128:(s_ % 4 + 1) * 128], scalar1=BK[:, h, s_:s_ + 1],
                                                                   scalar2=None, op0=ALU.mult), r=[f"pb{pi}", "BK"], w=[f"ztS{s_}"])
                for s_ in range(NS):
                    V(lambda s_=s_: nc.vector.scalar_tensor_tensor(out=Sout[:, s_, :], in0=Sin[:, s_, :], scalar=AB[:, h, s_:s_ + 1], in1=tSa[:, s_, :],
                                                                   op0=ALU.mult, op1=ALU.add), r=["zSin", "AB", f"ztS{s_}"], w=[f"zSout{s_}"])
                for s_ in range(NS):
                    T(lambda s_=s_: nc.tensor.matmul(pbank[5][:, h * NS + s_:h * NS + s_ + 1], Sout[:, s_, :], QKV[:, 0, h, s_:s_ + 1], start=True, stop=True),
                      r=[f"zSout{s_}", "QKV"], w=["pb5"])
                S.dma("sp", lambda h=h: nc.sync.dma_start(out=o_sg[:, h, :, :].rearrange("s k v -> k s v"), in_=Sout[:]), reads=[f"zSout{s_}" for s_ in range(NS)])
            V(lambda: nc.vector.tensor_copy(out=ob[:], in_=pbank[5][:, 0:8 * NS]), r=["pb5"], w=["zob"])
            A(lambda: nc.scalar.activation(out=sqb[:, 0:8 * NS], in_=ob[:], func=AF.Square), r=["zob"], w=["zsq"])
            T(lambda: nc.tensor.matmul(pbank[6][:, 0:8 * NS], ones128[:], sqb[:, 0:8 * NS], start=True, stop=True), r=["zsq", "ones128"], w=["pb6"])
            V(lambda: nc.vector.tensor_scalar(out=rn[:, 0:8 * NS], in0=pbank[6][:, 0:8 * NS], scalar1=1e-6, scalar2=None, op0=ALU.add), r=["pb6"], w=["zrn"])
            A(lambda: nc.scalar.activation(out=rn[:, 0:8 * NS], in_=rn[:, 0:8 * NS], func=AF.Ln), r=["zrn"], w=["zrn"])
            A(lambda: nc.scalar.activation(out=rn[:, 0:8 * NS], in_=rn[:, 0:8 * NS], func=AF.Exp, scale=-0.5), r=["zrn"], w=["zrn"])
            V(lambda: nc.vector.tensor_tensor(out=ob[:], in0=ob[:], in1=rn[:, 0:8 * NS], op=ALU.mult), r=["zob", "zrn"], w=["zob"])
            V(lambda: nc.vector.scalar_tensor_tensor(out=o_gs[:], in0=ob[:].rearrange("p (h s) -> p h s", s=NS), scalar=gg_t[:, 0:1], in1=GS[:],
                                                     op0=ALU.mult, op1=ALU.mult), r=["zob", "zGS", "gg"], w=["o_gs"])
            S.barrier()

    def dump(name):
        if name in dbg:
            o1 = P.dout(f"dbg_{name}_p", [128, DC, TP])
            o2 = P.dout(f"dbg_{name}_s", [128, DC, NS])
            S.dma("sp", lambda: nc.sync.dma_start(out=o1[:, :, :], in_=xp[:]))
            S.dma("sp", lambda: nc.sync.dma_start(out=o2[:, :, :], in_=xs[:]))
            S.barrier()

    def scope(name):
        return nc.named_scope(name) if "scopes" in dbg else contextlib.nullcontext()

    if "only_odd" not in dbg:
        with scope("ffn00"):
            ffn_sublayer(0, 0)
            S.barrier()
        dump("x0a")
        with scope("mix0"):
            mixer_even()
        dump("x0b")
        with scope("ffn01"):
            ffn_sublayer(0, 1)
            S.barrier()
        dump("x0c")
        with scope("ffn10"):
            ffn_sublayer(1, 0)
            S.barrier()
        dump("x1a")
    if "no_odd" not in dbg:
        with scope("mix1"):
            mixer_odd()
    dump("x1b")
    if "no_odd" not in dbg and "only_odd" not in dbg:
        with scope("ffn11"):
            ffn_sublayer(1, 1)
    S.barrier()
    S.dma("sp", lambda: nc.sync.dma_start(out=yTp[:, :, :], in_=xp[:]))
    S.dma("sp", lambda: nc.sync.dma_start(out=yTs[:, :, :], in_=xs[:]))


def _fm(a):
    T = a.shape[0]
    return np.ascontiguousarray(a.reshape(T, DC, 128).transpose(2, 1, 0))


def _unfm(a):
    return np.ascontiguousarray(a.transpose(2, 1, 0).reshape(a.shape[2], D))


def make_consts():
    c = np.zeros((128, 1800), np.float32)
    c[:, 0:128] = np.eye(128, dtype=np.float32)
    i = np.arange(128)
    c[:, 128:256] = ((i[:, None] // 16 == i[None, :] // 16) & (i[:, None] <= i[None, :])).astype(np.float32)
    c[:, 256:264] = (i[:, None] // 16 == np.arange(8)[None, :]).astype(np.float32)
    c[:, 264:264 + 512] = (np.arange(512) % 16 != 0).astype(np.float32)[None, :]
    c[:, 776:776 + 1024] = 1.0
    return c


def swa_bias_tables(first_invalid):
    q = np.arange(128)[:, None]
    k = np.arange(256)[None, :]
    dist = (128 + q) - k
    valid = (dist >= 0) & (dist < 128)
    out = np.zeros((128, 2, 256), np.float32)
    b = np.where(valid, dist, 1.0e7).astype(np.float32)
    out[:, 0] = b
    b1 = b.copy()
    if first_invalid:
        b1[:, 0:128] = 1.0e7
    out[:, 1] = b1
    return out


def prep_inputs(inp):
    f32 = np.float32
    shared = {}
    ada_w = inp["ada_w"]
    shared["ada_r"] = np.ascontiguousarray(
        ada_w.reshape(2, DC, 128, 18, 512).transpose(0, 3, 2, 1, 4))
    shared["ada_bT"] = np.ascontiguousarray(inp["ada_b"].reshape(2, 72, 128).transpose(2, 0, 1))
    shared["ln_gT"] = np.ascontiguousarray(inp["ln_g"].reshape(48, 128).T)
    shared["ln_bT"] = np.ascontiguousarray(inp["ln_b"].reshape(48, 128).T)
    wu = inp["ffn_w_up"].reshape(2, 2, DC, 128, 2, 11, 256)
    shared["wup_r"] = np.ascontiguousarray(wu.transpose(0, 1, 5, 3, 2, 4, 6)).reshape(2, 2, 11, 128, DC, 512)
    wd = inp["ffn_w_down"].reshape(2, 2, FC, 128, 4, 256)
    shared["wdn_r"] = np.ascontiguousarray(wd.transpose(0, 1, 4, 3, 2, 5))
    shared["consts"] = make_consts()
    w = inp["even_w_in"][0]
    tiles_ = []
    qcols = []
    for g in range(4):
        qcols += list(range((0 * 4 + g) * 64, (0 * 4 + g) * 64 + 64)) + list(range((1 * 4 + g) * 64, (1 * 4 + g) * 64 + 64))
    tiles_.append(w[:, qcols])
    kv = np.zeros((D, 512), f32)
    kv[:, 0:256] = w[:, 512:768]
    tiles_.append(kv)
    for hh in range(4):
        cols = []
        for base in (768, 1280, 2304, 1792):
            cols += list(range(base + hh * 128, base + (hh + 1) * 128))
        tiles_.append(w[:, cols])
    win0 = np.stack(tiles_, 0)
    shared["win0_r"] = np.ascontiguousarray(win0.reshape(6, DC, 128, 512).transpose(0, 2, 1, 3))
    shared["wout0_r"] = np.ascontiguousarray(inp["even_w_out"][0].reshape(8, 128, D).transpose(1, 0, 2))
    shared["sinks_b"] = np.ascontiguousarray(np.broadcast_to(inp["swa_sinks"][0][None, :], (128, 8))).astype(f32)
    shared["lb_logits"] = np.ascontiguousarray(inp["hgrn_lb_logits"].reshape(2, 4, 128).transpose(2, 0, 1))
    shared["hgrn_g"] = np.ascontiguousarray(inp["hgrn_norm_g"][0].reshape(128, 1))
    bias_tabs = [swa_bias_tables(False), swa_bias_tables(True)]
    w1 = inp["odd_w_in"][0]
    t1 = []
    for h in range(8):
        cols = []
        for base in (0, 1024, 2048, 3072):
            cols += list(range(base + h * 128, base + (h + 1) * 128))
        t1.append(w1[:, cols])
    ab_t = np.zeros((D, 512), f32)
    ab_t[:, 0:16] = w1[:, 4096:4112]
    t1.append(ab_t)
    win1 = np.stack(t1, 0)
    shared["win1_r"] = np.ascontiguousarray(win1.reshape(9, DC, 128, 512).transpose(0, 2, 1, 3))
    shared["wout1_r"] = np.ascontiguousarray(inp["odd_w_out"][0].reshape(8, 128, D).transpose(1, 0, 2))
    shared["convw_r"] = np.ascontiguousarray(inp["gdn_conv_w"][0].reshape(4, 3, 8, 128).transpose(3, 1, 2, 0))
    shared["abc"] = np.ascontiguousarray(np.stack([inp["gdn_dt_bias"][0], inp["gdn_a_log"][0]], 1)).astype(f32)
    shared["gdn_g"] = np.ascontiguousarray(inp["gdn_norm_g"][0].reshape(128, 1))
    ii = np.arange(128)
    S_, T_ = ii[:, None], ii[None, :]
    parts = [(S_ <= T_), (S_ < T_), (S_ // 16 == T_ // 16) & (S_ <= T_), (S_ // 16 == T_ // 16) & (S_ >= T_)]
    for m_ in (16, 32, 64):
        lev = (S_ // (2 * m_) == T_ // (2 * m_)) & (S_ // m_ != T_ // m_)
        parts += [lev & (S_ < T_), lev & (S_ > T_)]
    shared["tri"] = np.ascontiguousarray(np.concatenate([p_.astype(f32) for p_ in parts], 1))
    dt_ = np.zeros((128, 64), f32)
    pidx = np.arange(128)
    kidx = (pidx % 8)[:, None] * 16 + np.arange(16)[None, :]
    dt_[:, 0:16] = np.where(kidx >= 1, 128 - kidx, 1.0e7)
    dt_[:, 16:32] = (pidx[:, None] // 8 == np.arange(16)[None, :]).astype(f32)
    sperm = np.array([inp["swa_sinks"][0][(hd % 2) * 4 + hd // 2] for hd in range(8)], f32)
    dt_[:, 32:40] = sperm[None, :]
    dt_[0:8, 40] = sperm
    shared["dec_tab"] = dt_
    shared["dec_rep"] = np.ascontiguousarray(dt_[:, 16:32].T)
    maps = []
    for core in range(NCORES):
        b, seg = core // 4, core % 4
        m = dict(shared)
        m["xTp"] = _fm(inp["x_prompt"][b, seg * TP:(seg + 1) * TP])
        m["xTs"] = _fm(inp["x_sample"][core * NS:(core + 1) * NS, 0])
        cc = np.zeros((NCOND, D), f32)
        cc[0] = inp["c_prompt"][b]
        cc[1:1 + NS] = inp["c_sample"][core * NS:(core + 1) * NS]
        m["ccT"] = _fm(cc)
        m["swa_dist"] = bias_tabs[1 if seg == 0 else 0]
        pc = np.zeros((128, 16), f32)
        for r_ in range(4):
            pc[:, r_] = 1.0 if r_ == seg - 1 else 0.0
            pc[:, 4 + r_] = 1.0 if r_ < seg else 0.0
        m["percore"] = pc
        sl = slice(core * NS, (core + 1) * NS)
        m["ck_in"] = np.ascontiguousarray(inp["cache_swa_k"][0, sl].reshape(NS, 128, 128))
        m["cv_in"] = np.ascontiguousarray(inp["cache_swa_v"][0, sl].reshape(NS, 128, 128))
        m["sh_in"] = np.ascontiguousarray(inp["state_hgrn"][0, sl])
        m["sg_in"] = np.ascontiguousarray(inp["state_gdn"][0, sl])
        m["sc_in"] = np.ascontiguousarray(inp["state_gdn_conv"][0, sl])
        maps.append(m)
    return maps


_CACHE = {}


def run(inp, debug=(), trace=False, cores=None):
    key = tuple(sorted(debug))
    if key not in _CACHE:
        _CACHE[key] = build_program(debug)
    P = _CACHE[key]
    maps = prep_inputs(inp)
    maps = [{k: v for k, v in m.items() if k in P.inputs} for m in maps]
    if cores is not None:
        maps = [maps[c] for c in cores]
    res = run_bass_kernel_spmd(P.nc, maps, core_ids=list(range(len(maps))), **({"trace": True} if trace else {}))
    return res


def kernel(**inp):
    inp = {k: np.asarray(v) for k, v in inp.items()}
    res = run(inp)
    R = res.results
    f32 = np.float32
    y_prompt = np.zeros((2, 8192, D), f32)
    y_sample = np.zeros((128, 1, D), f32)
    p_k = np.zeros((1, 2, 128, 2, 64), f32)
    p_v = np.zeros((1, 2, 128, 2, 64), f32)
    p_h = np.zeros((1, 2, 4, 128, 128), f32)
    p_g = np.zeros((1, 2, 8, 128, 128), f32)
    p_c = np.zeros((1, 2, 3, 3072), f32)
    s_k = np.zeros((1, 128, 128, 2, 64), f32)
    s_v = np.zeros((1, 128, 128, 2, 64), f32)
    s_h = np.zeros((1, 128, 4, 128, 128), f32)
    s_g = np.zeros((1, 128, 8, 128, 128), f32)
    s_c = np.zeros((1, 128, 3, 3072), f32)
    for core in range(NCORES):
        b, seg = core // 4, core % 4
        r = R[core]
        y_prompt[b, seg * TP:(seg + 1) * TP] = _unfm(r["yTp"])
        sl = slice(core * NS, (core + 1) * NS)
        y_sample[sl, 0] = _unfm(r["yTs"])
        s_k[0, sl] = r["o_sk"].reshape(NS, 128, 2, 64)
        s_v[0, sl] = r["o_sv"].reshape(NS, 128, 2, 64)
        s_h[0, sl] = r["o_sh"]
        s_g[0, sl] = r["o_sg"]
        s_c[0, sl] = r["o_sc"]
        if seg == 3:
            p_k[0, b] = r["o_swak"].reshape(128, 2, 64)
            p_v[0, b] = r["o_swav"].reshape(128, 2, 64)
            p_h[0, b] = r["o_hgrn"].transpose(1, 0, 2)
            p_g[0, b] = r["o_gdn"].transpose(1, 0, 2)
            p_c[0, b] = r["o_gconv"].transpose(3, 1, 2, 0).reshape(3, 3072)
    return (y_prompt, y_sample, p_k, p_v, p_h, p_g, p_c, s_k, s_v, s_h, s_g, s_c)
```
